# Optimizing a Trainium2 kernel written in Bass

```python
import functools
import math
import jax
import jax.numpy as jnp
from jax import lax
import numpy as np

D_MODEL = 1024
BATCH = 8
SEQ = 2048
DEPTH = 1
DEC_BATCH = 128
DEC_SEQ = 8
PAST_LEN = 8192
PAGE_SIZE = 128

N_HEADS = 16
N_KV_HEADS = 4
HEAD_DIM = 64
Q_PER_KV = N_HEADS // N_KV_HEADS
ATTN_WIDTH = N_HEADS * HEAD_DIM
KV_WIDTH = N_KV_HEADS * HEAD_DIM
WINDOW = 128
BLOCK = 128
SSM_WIDTH = D_MODEL // 2
SSM_GROUP = 16
SSM_GROUPS = SSM_WIDTH // SSM_GROUP
SSM_STATE = 64
DT_MIN = 1e-3
DT_MAX = 1e-1
D_FF = 2816
IN_WIDTH = ATTN_WIDTH + 2 * KV_WIDTH + SSM_WIDTH + 2 * D_MODEL
SPLIT_POINTS = (
    ATTN_WIDTH,
    ATTN_WIDTH + KV_WIDTH,
    ATTN_WIDTH + 2 * KV_WIDTH,
    ATTN_WIDTH + 2 * KV_WIDTH + SSM_WIDTH,
    ATTN_WIDTH + 2 * KV_WIDTH + SSM_WIDTH + D_MODEL,
)
RMS_EPS = 1e-6
NEG_BIG = -1e30

kernel_name = 'hybrid_swa_sink_s5_macaron_step'


def _rmsnorm(x, g):
    x32 = x.astype(jnp.float32)
    y = x32 * lax.rsqrt(jnp.mean(x32 * x32, axis=-1, keepdims=True) + RMS_EPS)
    return (y * g.astype(jnp.float32)).astype(x.dtype)


def _swiglu(h, w_gate, w_up, w_down):
    return (jax.nn.silu(h @ w_gate) * (h @ w_up)) @ w_down


def _sink_attention(q, k, v, valid, sinks):
    s = jnp.einsum('...qgrd,...kgd->...grqk', q, k).astype(jnp.float32) * (HEAD_DIM ** -0.5)
    s = jnp.where(valid, s, NEG_BIG)
    sink = sinks.astype(jnp.float32)[:, :, None, None]
    m = jnp.maximum(jnp.max(s, axis=-1, keepdims=True), sink)
    p = jnp.exp(s - m)
    denom = jnp.sum(p, axis=-1, keepdims=True) + jnp.exp(sink - m)
    w = (p / denom).astype(v.dtype)
    return jnp.einsum('...grqk,...kgd->...qgrd', w, v)


def _attend_prompt(q, k, v, sinks):
    bt, seq = q.shape[0], q.shape[1]
    nb = seq // BLOCK
    qb = q.reshape(bt, nb, BLOCK, N_KV_HEADS, Q_PER_KV, HEAD_DIM)
    kb = k.reshape(bt, nb, BLOCK, N_KV_HEADS, HEAD_DIM)
    vb = v.reshape(bt, nb, BLOCK, N_KV_HEADS, HEAD_DIM)
    pad = ((0, 0), (1, 0), (0, 0), (0, 0), (0, 0))
    kk = jnp.concatenate([jnp.pad(kb, pad)[:, :-1], kb], axis=2)
    vv = jnp.concatenate([jnp.pad(vb, pad)[:, :-1], vb], axis=2)
    blk = jnp.arange(nb)[:, None, None]
    q_pos = blk * BLOCK + jnp.arange(BLOCK)[None, :, None]
    k_pos = (blk - 1) * BLOCK + jnp.arange(2 * BLOCK)[None, None, :]
    valid = (k_pos >= 0) & (k_pos <= q_pos) & (q_pos - k_pos <= WINDOW)
    out = _sink_attention(qb, kk, vv, valid[:, None, None], sinks)
    w_buf = min(WINDOW, PAST_LEN)
    return out.reshape(bt, seq, ATTN_WIDTH), k[:, -w_buf:], v[:, -w_buf:]


def _attend_sample(q, k, v, cache_k, cache_v, sinks):
    bt, s_len = q.shape[0], q.shape[1]
    w_buf = cache_k.shape[1]
    kk = jnp.concatenate([cache_k.astype(k.dtype), k], axis=1)
    vv = jnp.concatenate([cache_v.astype(v.dtype), v], axis=1)
    q_pos = w_buf + jnp.arange(s_len)[:, None]
    k_pos = jnp.arange(w_buf + s_len)[None, :]
    valid = (k_pos <= q_pos) & (q_pos - k_pos <= WINDOW)
    out = _sink_attention(q, kk, vv, valid, sinks)
    return out.reshape(bt, s_len, ATTN_WIDTH), kk[:, -w_buf:], vv[:, -w_buf:]


def _complex_affine_combine(earlier, later):
    a1r, a1i, b1r, b1i = earlier
    a2r, a2i, b2r, b2i = later
    return (a1r * a2r - a1i * a2i,
            a1r * a2i + a1i * a2r,
            a2r * b1r - a2i * b1i + b2r,
            a2r * b1i + a2i * b1r + b2i)


def _ssm_branch(u, h0_re, h0_im, lam_re, lam_im, log_dt, b_re, b_im, c_re, c_im, d_skip, glu_a, glu_b):
    f32 = jnp.float32
    bt, seq = u.shape[0], u.shape[1]
    lr = lam_re.astype(f32)
    li = lam_im.astype(f32)
    dt = jnp.exp(log_dt.astype(f32))[:, None]
    mag = jnp.exp(lr * dt)
    ang = li * dt
    abar_re = mag * jnp.cos(ang)
    abar_im = mag * jnp.sin(ang)
    den = lr * lr + li * li
    nr = abar_re - 1.0
    fr = (nr * lr + abar_im * li) / den
    fi = (abar_im * lr - nr * li) / den
    br = b_re.astype(f32)
    bi = b_im.astype(f32)
    bb_re = fr[..., None] * br - fi[..., None] * bi
    bb_im = fr[..., None] * bi + fi[..., None] * br
    u32 = u.astype(f32)
    ug = u32.reshape(bt, seq, SSM_GROUPS, SSM_GROUP)
    x_re = jnp.einsum('blgc,gnc->blgn', ug, bb_re)
    x_im = jnp.einsum('blgc,gnc->blgn', ug, bb_im)
    if h0_re is not None:
        h0r = h0_re.astype(f32)
        h0i = h0_im.astype(f32)
        x_re = x_re.at[:, 0].add(abar_re * h0r - abar_im * h0i)
        x_im = x_im.at[:, 0].add(abar_re * h0i + abar_im * h0r)
    a_re = jnp.broadcast_to(abar_re, x_re.shape)
    a_im = jnp.broadcast_to(abar_im, x_im.shape)
    _, _, h_re, h_im = lax.associative_scan(_complex_affine_combine, (a_re, a_im, x_re, x_im), axis=1)
    y = (jnp.einsum('blgn,gcn->blgc', h_re, c_re.astype(f32))
         - jnp.einsum('blgn,gcn->blgc', h_im, c_im.astype(f32))).reshape(bt, seq, SSM_WIDTH)
    y = jax.nn.gelu(y + d_skip.astype(f32) * u32).astype(u.dtype)
    out = (y @ glu_a) * jax.nn.sigmoid(y @ glu_b)
    return out, h_re[:, -1], h_im[:, -1]


def _layer(x, attend, h0_re, h0_im, ffn_a_norm, ffn_a_gate, ffn_a_up, ffn_a_down, mix_norm, w_in,
           lam_re, lam_im, log_dt, b_re, b_im, c_re, c_im, d_skip, glu_a, glu_b, w_out,
           ffn_b_norm, ffn_b_gate, ffn_b_up, ffn_b_down):
    bt, seq = x.shape[0], x.shape[1]
    x = x + 0.5 * _swiglu(_rmsnorm(x, ffn_a_norm), ffn_a_gate, ffn_a_up, ffn_a_down)
    h = _rmsnorm(x, mix_norm)
    q, k, v, u, g_attn, g_ssm = jnp.split(h @ w_in, SPLIT_POINTS, axis=-1)
    q = q.reshape(bt, seq, N_KV_HEADS, Q_PER_KV, HEAD_DIM)
    k = k.reshape(bt, seq, N_KV_HEADS, HEAD_DIM)
    v = v.reshape(bt, seq, N_KV_HEADS, HEAD_DIM)
    attn, k_buf, v_buf = attend(q, k, v)
    ssm, s_re, s_im = _ssm_branch(u, h0_re, h0_im, lam_re, lam_im, log_dt, b_re, b_im,
                                  c_re, c_im, d_skip, glu_a, glu_b)
    merged = jax.nn.sigmoid(g_attn) * attn + jax.nn.sigmoid(g_ssm) * ssm
    x = x + merged @ w_out
    x = x + 0.5 * _swiglu(_rmsnorm(x, ffn_b_norm), ffn_b_gate, ffn_b_up, ffn_b_down)
    return x, k_buf, v_buf, s_re, s_im


def setup_inputs(seed: int = 0) -> dict:
    key = jax.random.key(seed)
    ks = jax.random.split(key, 32)
    w_buf = min(WINDOW, PAST_LEN)

    def nrm(k, shape, scale):
        return jax.random.normal(k, shape, jnp.float32) * scale

    def gain(k, shape):
        return 1.0 + nrm(k, shape, 0.02)

    lam_im_base = jnp.pi * jnp.arange(SSM_STATE, dtype=jnp.float32)
    return {
        'x_prompt': nrm(ks[0], (BATCH, SEQ, D_MODEL), 1.0),
        'x_sample': nrm(ks[1], (DEC_BATCH, DEC_SEQ, D_MODEL), 1.0),
        'cache_k_win': nrm(ks[2], (DEPTH, DEC_BATCH, w_buf, N_KV_HEADS, HEAD_DIM), 1.0),
        'cache_v_win': nrm(ks[3], (DEPTH, DEC_BATCH, w_buf, N_KV_HEADS, HEAD_DIM), 1.0),
        'state_ssm_re': nrm(ks[4], (DEPTH, DEC_BATCH, SSM_GROUPS, SSM_STATE), 0.5),
        'state_ssm_im': nrm(ks[5], (DEPTH, DEC_BATCH, SSM_GROUPS, SSM_STATE), 0.5),
        'ffn_a_norm': gain(ks[6], (DEPTH, D_MODEL)),
        'ffn_a_gate': nrm(ks[7], (DEPTH, D_MODEL, D_FF), D_MODEL ** -0.5),
        'ffn_a_up': nrm(ks[8], (DEPTH, D_MODEL, D_FF), D_MODEL ** -0.5),
        'ffn_a_down': nrm(ks[9], (DEPTH, D_FF, D_MODEL), D_FF ** -0.5),
        'mix_norm': gain(ks[10], (DEPTH, D_MODEL)),
        'w_in': nrm(ks[11], (DEPTH, D_MODEL, IN_WIDTH), D_MODEL ** -0.5),
        'attn_sinks': nrm(ks[12], (DEPTH, N_HEADS), 1.0),
        'ssm_lambda_re': -0.5 * jnp.exp(nrm(ks[13], (DEPTH, SSM_GROUPS, SSM_STATE), 0.05)),
        'ssm_lambda_im': lam_im_base + nrm(ks[14], (DEPTH, SSM_GROUPS, SSM_STATE), 0.01),
        'ssm_log_dt': jax.random.uniform(ks[15], (DEPTH, SSM_GROUPS), jnp.float32,
                                         minval=math.log(DT_MIN), maxval=math.log(DT_MAX)),
        'ssm_b_re': nrm(ks[16], (DEPTH, SSM_GROUPS, SSM_STATE, SSM_GROUP), (2 * SSM_GROUP) ** -0.5),
        'ssm_b_im': nrm(ks[17], (DEPTH, SSM_GROUPS, SSM_STATE, SSM_GROUP), (2 * SSM_GROUP) ** -0.5),
        'ssm_c_re': nrm(ks[18], (DEPTH, SSM_GROUPS, SSM_GROUP, SSM_STATE), (2 * SSM_STATE) ** -0.5),
        'ssm_c_im': nrm(ks[19], (DEPTH, SSM_GROUPS, SSM_GROUP, SSM_STATE), (2 * SSM_STATE) ** -0.5),
        'ssm_d': nrm(ks[20], (DEPTH, SSM_WIDTH), 1.0),
        'glu_a': nrm(ks[21], (DEPTH, SSM_WIDTH, D_MODEL), SSM_WIDTH ** -0.5),
        'glu_b': nrm(ks[22], (DEPTH, SSM_WIDTH, D_MODEL), SSM_WIDTH ** -0.5),
        'w_out': nrm(ks[23], (DEPTH, D_MODEL, D_MODEL), D_MODEL ** -0.5),
        'ffn_b_norm': gain(ks[24], (DEPTH, D_MODEL)),
        'ffn_b_gate': nrm(ks[25], (DEPTH, D_MODEL, D_FF), D_MODEL ** -0.5),
        'ffn_b_up': nrm(ks[26], (DEPTH, D_MODEL, D_FF), D_MODEL ** -0.5),
        'ffn_b_down': nrm(ks[27], (DEPTH, D_FF, D_MODEL), D_FF ** -0.5),
        'final_norm': gain(ks[28], (D_MODEL,)),
    }


def reference(x_prompt, x_sample, cache_k_win, cache_v_win, state_ssm_re, state_ssm_im,
              ffn_a_norm, ffn_a_gate, ffn_a_up, ffn_a_down, mix_norm, w_in, attn_sinks,
              ssm_lambda_re, ssm_lambda_im, ssm_log_dt, ssm_b_re, ssm_b_im, ssm_c_re, ssm_c_im,
              ssm_d, glu_a, glu_b, w_out, ffn_b_norm, ffn_b_gate, ffn_b_up, ffn_b_down, final_norm):
    xp = x_prompt
    xs = x_sample
    kp, vp, ksm, vsm, sp_re, sp_im, ss_re, ss_im = [], [], [], [], [], [], [], []
    for l in range(DEPTH):
        lw = (ffn_a_norm[l], ffn_a_gate[l], ffn_a_up[l], ffn_a_down[l], mix_norm[l], w_in[l],
              ssm_lambda_re[l], ssm_lambda_im[l], ssm_log_dt[l], ssm_b_re[l], ssm_b_im[l],
              ssm_c_re[l], ssm_c_im[l], ssm_d[l], glu_a[l], glu_b[l], w_out[l],
              ffn_b_norm[l], ffn_b_gate[l], ffn_b_up[l], ffn_b_down[l])
        sinks = attn_sinks[l].reshape(N_KV_HEADS, Q_PER_KV)
        xp, k_new, v_new, s_re, s_im = _layer(
            xp, functools.partial(_attend_prompt, sinks=sinks), None, None, *lw)
        kp.append(k_new)
        vp.append(v_new)
        sp_re.append(s_re)
        sp_im.append(s_im)
        xs, k_new, v_new, s_re, s_im = _layer(
            xs, functools.partial(_attend_sample, cache_k=cache_k_win[l], cache_v=cache_v_win[l], sinks=sinks),
            state_ssm_re[l], state_ssm_im[l], *lw)
        ksm.append(k_new)
        vsm.append(v_new)
        ss_re.append(s_re)
        ss_im.append(s_im)
    y_prompt = _rmsnorm(xp, final_norm)
    y_sample = _rmsnorm(xs, final_norm)
    return (y_prompt, y_sample,
            jnp.stack(kp), jnp.stack(vp), jnp.stack(ksm), jnp.stack(vsm),
            jnp.stack(sp_re), jnp.stack(sp_im), jnp.stack(ss_re), jnp.stack(ss_im))
```

```python
import numpy as np
import concourse.bass as bass
import concourse.mybir as mybir
from concourse.bass_utils import run_bass_kernel_spmd

F32 = mybir.dt.float32
BF16 = mybir.dt.bfloat16
AF = mybir.ActivationFunctionType
ALU = mybir.AluOpType

ENGS = ("pe", "act", "dve", "pool", "sp")
N_DMA_SEMS = 96

D = 1024
DFF = 2816
NFC = DFF // 128
NT = 1088
TILES = [(i * 128, 128) for i in range(8)] + [(1024, 64)]
GROUPS = [(0, 512), (512, 512), (1024, 64)]
EPS = 1e-6


class Op:
    __slots__ = ("id", "eng", "fn", "deps", "dma", "sig", "sem", "val")

    def __init__(self, id, eng, fn, dma):
        self.id = id
        self.eng = eng
        self.fn = fn
        self.deps = set()
        self.dma = dma
        self.sig = False
        self.sem = None
        self.val = None


class Prog:
    def __init__(self, nc):
        self.nc = nc
        self.ops = []
        self.q = {e: [] for e in ENGS}
        self.lastw = {}
        self.readers = {}
        self.dma_last = [None] * N_DMA_SEMS
        self.dma_cnt = [0] * N_DMA_SEMS
        self.dma_rr = 0
        self.dma_rr2 = {True: 0, False: 0}
        self.bar = []
        self.phase = 'init'
        self.pe_phase = []

    def _add(self, eng, fn, reads, writes, dma=False):
        op = Op(len(self.ops), eng, fn, dma)
        op.deps.update(self.bar)
        isps = lambda k: isinstance(k, tuple) and k[0] == "ps"
        writes = list(writes) + [k for k in reads if isps(k)]
        reads = [k for k in reads if not isps(k)]
        for k in reads:
            w = self.lastw.get(k)
            if w is not None:
                op.deps.add(w)
        for k in writes:
            w = self.lastw.get(k)
            if w is not None:
                op.deps.add(w)
            for r in self.readers.get(k, ()):
                op.deps.add(r)
        for k in reads:
            self.readers.setdefault(k, []).append(op.id)
        for k in writes:
            self.lastw[k] = op.id
            self.readers[k] = []
        if dma:
            half = N_DMA_SEMS // 2
            base = 0 if eng == "pool" else half
            r = self.dma_rr2[eng == "pool"]
            self.dma_rr2[eng == "pool"] = (r + 1) % half
            s = base + r
            if self.dma_last[s] is not None:
                op.deps.add(self.dma_last[s])
            self.dma_last[s] = op.id
            self.dma_cnt[s] += 1
            op.sem = ("dma", s)
            op.val = 16 * self.dma_cnt[s]
            op.sig = True
        self.ops.append(op)
        self.q[eng].append(op)
        return op

    def barrier(self):
        b = []
        for e in ENGS:
            if self.q[e]:
                b.append(self.q[e][-1].id)
        for d in self.dma_last:
            if d is not None:
                b.append(d)
        self.bar = b

    def pe(self, fn, reads=(), writes=()):
        self.pe_phase.append(self.phase)
        return self._add("pe", fn, reads, writes)

    def act(self, fn, reads=(), writes=()):
        return self._add("act", fn, reads, writes)

    def dve(self, fn, reads=(), writes=()):
        return self._add("dve", fn, reads, writes)

    def pool(self, fn, reads=(), writes=()):
        return self._add("pool", fn, reads, writes)

    def dma(self, eng, fn, reads=(), writes=()):
        return self._add(eng, fn, reads, writes, dma=True)

    def emit(self, block, sems, dma_sems):
        ops = self.ops

        import os
        nosame = int(os.environ.get("KNOSAME", "0"))

        def skip(p, op):
            if p.dma or op.dma or p.eng != op.eng:
                return False
            return p.eng == "pe" or nosame

        for op in ops:
            for d in op.deps:
                p = ops[d]
                if p.dma or skip(p, op):
                    continue
                p.sig = True
        cnt = {e: 0 for e in ENGS}
        for e in ENGS:
            for op in self.q[e]:
                if not op.dma and op.sig:
                    cnt[e] += 1
                    op.sem = ("eng", e)
                    op.val = cnt[e]

        def semh(s):
            return sems[s[1]] if s[0] == "eng" else dma_sems[s[1]]

        def run_queue(e, eng):
            waited = {}
            for op in self.q[e]:
                need = {}
                for d in op.deps:
                    p = ops[d]
                    if skip(p, op) or p.sem is None:
                        continue
                    if need.get(p.sem, 0) < p.val:
                        need[p.sem] = p.val
                for s, v in need.items():
                    if waited.get(s, 0) >= v:
                        continue
                    eng.wait_ge(semh(s), v)
                    waited[s] = v
                inst = op.fn(eng)
                if op.sig:
                    inst.then_inc(semh(op.sem), 16 if op.dma else 1)
            for op in self.q[e]:
                if op.dma and waited.get(op.sem, 0) < op.val:
                    eng.wait_ge(semh(op.sem), op.val)
                    waited[op.sem] = op.val

        if self.q["pe"]:
            @block.tensor
            def _(eng):
                run_queue("pe", eng)
        if self.q["act"]:
            @block.scalar
            def _(eng):
                run_queue("act", eng)
        if self.q["dve"]:
            @block.vector
            def _(eng):
                run_queue("dve", eng)
        if self.q["pool"]:
            @block.gpsimd
            def _(eng):
                run_queue("pool", eng)
        if self.q["sp"]:
            @block.sync
            def _(eng):
                run_queue("sp", eng)


NB = 136
CH_ROWS = [96, 96, 96, 96, 96, 32]
PI = float(np.pi)


class Arena:
    def __init__(self, t, n):
        self.t = t
        self.n = n
        self.off = 0

    def take(self, shape, dt):
        elems = 1
        for s in shape[1:]:
            elems *= s
        size = elems * (2 if dt == F32 else 1)
        off = (self.off + 31) // 32 * 32
        assert off + size <= self.n, ("arena overflow", off + size, self.n)
        self.off = off + size
        v = self.t[:, off:off + size]
        if dt == F32:
            v = v.bitcast(F32)
        nd = len(shape) - 1
        if nd > 1:
            names = ["a", "b", "c", "d"][:nd]
            kw = {names[i]: shape[1 + i] for i in range(nd)}
            v = v.rearrange("p (" + " ".join(names) + ") -> p " + " ".join(names), **kw)
        return v


def build_nc(debug=False):
    import os
    STOP = int(os.environ.get('KSTOP', '99'))
    NOTAB = int(os.environ.get('KNOTAB', '0'))
    KSUB = int(os.environ.get('KSUB', '99'))
    nc = bass.Bass("TRN2", target_bir_lowering=False)

    def din(name, shape):
        return nc.dram_tensor(name, list(shape), F32, kind="ExternalInput").ap()

    def dout(name, shape):
        return nc.dram_tensor(name, list(shape), F32, kind="ExternalOutput").ap()

    xp = din("xp", [2048, D])
    xs = din("xs", [128, D])
    ck = din("ck", [16, 128, 256])
    cv = din("cv", [16, 128, 256])
    h0r_d = din("h0r", [16, 2048])
    h0i_d = din("h0i", [16, 2048])
    ident_d = din("ident", [128, 128])
    mcur_d = din("mcur", [128, 256])
    mprev_d = din("mprev", [128, 256])
    mnew_d = din("mnew", [64, 128])
    mc_d = din("mc", [128, 8, 128])
    m01_d = din("m01", [128, 2, 256])
    mc2_d = din("mc2", [128, 64])
    gains = {k: din(k, [D]) for k in ("ffn_a_norm", "mix_norm", "ffn_b_norm", "final_norm")}
    Wg = {"a": din("ffn_a_gate", [D, DFF]), "b": din("ffn_b_gate", [D, DFF])}
    Wu = {"a": din("ffn_a_up", [D, DFF]), "b": din("ffn_b_up", [D, DFF])}
    Wd = {"a": din("ffn_a_down", [DFF, D]), "b": din("ffn_b_down", [DFF, D])}
    w_in = din("w_in", [D, 4096])
    sinks_d = din("attn_sinks", [16])
    lam_re_d = din("ssm_lambda_re", [32, 64])
    lam_im_d = din("ssm_lambda_im", [32, 64])
    log_dt_d = din("ssm_log_dt", [32])
    b_re_d = din("ssm_b_re", [32, 64, 16])
    b_im_d = din("ssm_b_im", [32, 64, 16])
    c_re_d = din("ssm_c_re", [32, 16, 64])
    c_im_d = din("ssm_c_im", [32, 16, 64])
    dsk_d = din("ssm_d", [512])
    glu_a_d = din("glu_a", [512, D])
    glu_b_d = din("glu_b", [512, D])
    w_out_d = din("w_out", [D, D])

    yp = dout("yp", [2048, D])
    ys = dout("ys", [128, D])
    kwp = dout("kwp", [128, 256])
    vwp = dout("vwp", [128, 256])
    kws = dout("kws", [16, 128, 256])
    vws = dout("vws", [16, 128, 256])
    srp = dout("srp", [16, 128])
    sip = dout("sip", [16, 128])
    srs = dout("srs", [16, 2048])
    sis = dout("sis", [16, 2048])

    def sb(name, shape, dt):
        return nc.alloc_sbuf_tensor(name, list(shape), dt)

    x_tm = sb("x_tm", [128, 9, D], F32)
    gfin = sb("gfin", [128, D], F32)
    gcol = sb("gcol", [128, 3, 8], F32)
    ident_f = sb("ident_f", [128, 128], F32)
    ident_b = sb("ident_b", [128, 128], BF16)
    mcur_b = sb("mcur_b", [128, 256], BF16)
    mprev_b = sb("mprev_b", [128, 256], BF16)
    mnew_b = sb("mnew_b", [128, 128], BF16)
    mc_b = sb("mc_b", [128, 8, 128], BF16)
    m01_b = sb("m01_b", [128, 2, 256], BF16)
    mc2_b = sb("mc2_b", [128, 64], BF16)
    esink = sb("esink", [128, 16], F32)
    stat = sb("stat", [128, 4, 4], F32)
    kk_carry = sb("kk_carry", [128, 4, 128], BF16)
    v_carry = sb("v_carry", [128, 4, 68], BF16)
    KBD = sb("KBD", [128, 6, 8, 128], BF16)
    RB = sb("RB", [128, 6, 8, 2, 128], BF16)
    CAr_b = sb("CAr_b", [128, 9, 16, 32], BF16)
    nCAi_b = sb("nCAi_b", [128, 9, 16, 32], BF16)
    A8 = sb("A8", [128, 2, 16], F32)
    A8p = sb("A8p", [128, 9, 2, 16], F32)
    A64 = sb("A64", [128, 4, 2, 16], F32)
    Dsk = sb("Dsk", [128, 6], F32)
    h0T = sb("h0T", [128, 2, 16, 16], F32)
    Hc = sb("Hc", [128, 2, 16], F32)
    rt = sb("rt", [128, 4, 16], F32)
    ps = [nc.alloc_psum_tensor("ps%d" % i, [128, 512], F32) for i in range(8)]
    rem = nc.sbuf_bytes_remaining
    an = (rem - 1024) // 2 // 32 * 32
    arena_t = sb("arena", [128, an], BF16)
    A = Arena(arena_t, an)

    sems = {e: nc.alloc_semaphore("s_" + e) for e in ENGS}
    dsems = [nc.alloc_semaphore("d%d" % i) for i in range(N_DMA_SEMS)]
    P = Prog(nc)
    TT = ALU.mult
    rot = {"n": 0}

    def tt_(e, out, in0, in1, op):
        return e.tensor_tensor(out=out, in0=in0, in1=in1, op=op)

    P.dma("sp", lambda e: e.dma_start(out=ident_f[:], in_=ident_d[:]), writes=["ident_f"])
    P.dma("pool", lambda e: e.dma_start(out=ident_b[:], in_=ident_d[:]), writes=["ident_b"])
    P.dma("pool", lambda e: e.dma_start(out=mcur_b[:], in_=mcur_d[:]), writes=["mcur"])
    P.dma("pool", lambda e: e.dma_start(out=mprev_b[:], in_=mprev_d[:]), writes=["mprev"])
    P.dve(lambda e: e.memset(mnew_b[:], 0.0), writes=["mnew"])
    P.dma("pool", lambda e: e.dma_start(out=mnew_b[:64, :], in_=mnew_d[:]), reads=["mnew"], writes=["mnew"])
    P.dma("pool", lambda e: e.dma_start(out=mc_b[:], in_=mc_d[:]), writes=["mc"])
    P.dma("pool", lambda e: e.dma_start(out=m01_b[:], in_=m01_d[:]), writes=["m01"])
    P.dma("pool", lambda e: e.dma_start(out=mc2_b[:], in_=mc2_d[:]), writes=["mc2"])
    P.dma("sp", lambda e: e.dma_start(out=gfin[:], in_=gains["final_norm"].partition_broadcast(128)), writes=["gfin"])
    for gi, k in enumerate(("ffn_a_norm", "mix_norm", "ffn_b_norm")):
        P.dma("sp", lambda e, gi=gi, k=k: e.dma_start(
            out=gcol[:, gi, :], in_=gains[k].rearrange("(c p) -> p c", p=128), allow_slow_non_contiguous=True),
            writes=["gcol"])
    P.dma("sp", lambda e: e.dma_start(out=esink[:], in_=sinks_d.partition_broadcast(128)), writes=["esink"])
    P.act(lambda e: e.activation(out=esink[:], in_=esink[:], func=AF.Exp), reads=["esink"], writes=["esink"])

    def build_tables():
        P.phase = 'tables'
        A.off = 0
        lam = A.take([128, 2, 16], F32)
        ldt = A.take([128, 16], F32)
        lt = A.take([128, 3, 128], F32)
        ld2 = A.take([128, 2], F32)
        bb = A.take([128, 2, 16, 16], F32)
        cc = A.take([128, 2, 16, 16], F32)
        w = A.take([128, 12, 16], F32)
        w3 = A.take([128, 4, 16, 16], F32)
        BBp = A.take([128, 2, 16, 32], F32)
        PW = A.take([128, 9, 2, 16], F32)
        h0s = A.take([128, 2, 2048], F32)
        ct = A.take([128, 2, 2, 128], F32)
        BBh = A.take([128, 2, 16, 32], BF16)
        BBl = A.take([128, 2, 16, 32], BF16)
        dtmp = A.take([128, 2, 16, 32], F32)
        X = A.take([128, 4, 9, 16, 16], F32)
        CAl = A.take([128, 2, 8, 16, 32], BF16)
        pwt = A.take([128, 4, 4, 16], F32)
        Y = h0s[:, :, :].rearrange("p a b -> p (a b)").rearrange("p (q m g c) -> p q m g c", q=4, m=2, g=16)
        P.dma("sp", lambda e: e.dma_start(out=lt[:16, 0, :], in_=lam_re_d.rearrange("(p t) n -> p (t n)", t=2)),
              writes=["lt0"])
        P.dma("sp", lambda e: e.dma_start(out=lt[:16, 1, :], in_=lam_im_d.rearrange("(p t) n -> p (t n)", t=2)),
              writes=["lt1"])
        P.dma("sp", lambda e: e.dma_start(out=ld2[:16, :], in_=log_dt_d.rearrange("(p t) -> p t", t=2)),
              writes=["ld2"])
        P.dve(lambda e: e.tensor_copy(out=lt[:16, 2, :].rearrange("p (t n) -> p t n", t=2),
                                      in_=ld2[:16, :].unsqueeze(2).to_broadcast([16, 2, 64])),
              reads=["ld2"], writes=["lt2"])
        for k in range(3):
            P.pe(lambda e, k=k: e.transpose(out=ps[4][:, k * 16:(k + 1) * 16], in_=lt[:16, k, :],
                                            identity=ident_f[:16, :16]),
                 reads=["lt%d" % k, "ident_f"], writes=[("ps", 4)])
        P.dve(lambda e: e.tensor_copy(out=lam[:, :, :], in_=ps[4][:, 0:32].rearrange("p (a b) -> p a b", a=2)),
              reads=[("ps", 4)], writes=["lam"])
        P.dve(lambda e: e.tensor_copy(out=ldt[:, :], in_=ps[4][:, 32:48]), reads=[("ps", 4)], writes=["ldt"])
        for g2 in range(2):
            psl = slice(g2 * 64, (g2 + 1) * 64)
            for ri, src in enumerate((b_re_d, b_im_d)):
                P.dma("sp", lambda e, psl=psl, ri=ri, src=src, g2=g2: e.dma_start(
                    out=bb[psl, ri, :, :], in_=src.rearrange("(p t) n c -> t n p c", t=2)[g2]), writes=["bb"])
        for ri, src in enumerate((c_re_d, c_im_d)):
            for t in range(2):
                for pl in range(8):
                    p = t * 8 + pl
                    P.dma("sp" if ri == 0 else "pool", lambda e, ri=ri, t=t, pl=pl, p=p, src=src: e.dma_start(
                        out=ct[pl * 16:(pl + 1) * 16, ri, t, :].rearrange("c (t2 n) -> c t2 n", t2=2),
                        in_=src[2 * p:2 * p + 2].rearrange("t2 c n -> c t2 n")), writes=[("ct", ri, t)])
                pb = 2 + (2 * ri + t) % 2
                P.pe(lambda e, ri=ri, t=t, pb=pb: e.transpose(out=ps[pb][:, 0:128], in_=ct[:, ri, t, :],
                                                            identity=ident_f[:, :]),
                     reads=[("ct", ri, t), "ident_f"], writes=[("ps", pb)])
                P.act(lambda e, ri=ri, t=t, pb=pb: e.copy(
                    out=cc[:, ri, t * 8:(t + 1) * 8, :], in_=ps[pb][:, 0:128].rearrange("p (a c) -> p a c", c=16)),
                    reads=[("ps", pb)], writes=[("cc", ri, t)])
        P.dve(lambda e: e.memset(Dsk[:], 0.0), writes=["Dsk"])
        P.dma("sp", lambda e: e.dma_start(out=Dsk[:96, 0:5], in_=dsk_d[0:480].rearrange("(o r) -> r o", r=96),
                                          allow_slow_non_contiguous=True), reads=["Dsk"], writes=["Dsk"])
        P.dma("sp", lambda e: e.dma_start(out=Dsk[:32, 5:6], in_=dsk_d[480:512].rearrange("(o r) -> r o", r=32),
                                          allow_slow_non_contiguous=True), reads=["Dsk"], writes=["Dsk"])
        P.dma("sp", lambda e: e.dma_start(out=h0s[:16, 0, :], in_=h0r_d[:]), writes=["h0s"])
        P.dma("sp", lambda e: e.dma_start(out=h0s[:16, 1, :], in_=h0i_d[:]), writes=["h0s"])
        for ri in range(2):
            pv = ps[ri][:, :].rearrange("p (a s) -> p a s", s=32)
            for p in range(16):
                P.pe(lambda e, ri=ri, p=p, pv=pv: e.transpose(out=pv[:, p, 0:16], in_=h0s[:16, ri, p * 128:(p + 1) * 128],
                                                           identity=ident_f[:16, :16]),
                     reads=["h0s", "ident_f"], writes=[("ps", ri)])
            P.dve(lambda e, ri=ri, pv=pv: e.tensor_copy(out=h0T[:, ri, :, :], in_=pv[:, :, 0:16]),
                  reads=[("ps", ri)], writes=["h0T"])

        def V(fn, reads, writes):
            P.dve(fn, reads=reads, writes=writes)

        W = lambda i: w[:, i, :]
        lr, li = lam[:, 0, :], lam[:, 1, :]
        P.act(lambda e: e.activation(out=W(0), in_=ldt[:], func=AF.Exp), reads=["ldt"], writes=["w"])
        V(lambda e: tt_(e, W(1), lr, W(0), TT), ["lam", "w"], ["w"])
        P.act(lambda e: e.activation(out=W(2), in_=W(1), func=AF.Exp), reads=["w"], writes=["w"])
        V(lambda e: tt_(e, W(3), li, W(0), TT), ["lam", "w"], ["w"])
        for _ in range(4):
            V(lambda e: e.tensor_single_scalar(out=W(4), in_=W(3), scalar=PI, op=ALU.is_gt), ["w"], ["w"])
            V(lambda e: e.scalar_tensor_tensor(out=W(3), in0=W(4), scalar=-2.0 * PI, in1=W(3), op0=ALU.mult,
                                               op1=ALU.add), ["w"], ["w"])
        V(lambda e: e.tensor_single_scalar(out=W(4), in_=W(3), scalar=-PI, op=ALU.is_lt), ["w"], ["w"])
        V(lambda e: e.scalar_tensor_tensor(out=W(3), in0=W(4), scalar=2.0 * PI, in1=W(3), op0=ALU.mult,
                                           op1=ALU.add), ["w"], ["w"])
        V(lambda e: e.tensor_scalar(out=W(5), in0=W(3), scalar1=PI / 2, scalar2=None, op0=ALU.add), ["w"], ["w"])
        V(lambda e: e.tensor_single_scalar(out=W(4), in_=W(5), scalar=PI, op=ALU.is_gt), ["w"], ["w"])
        V(lambda e: e.scalar_tensor_tensor(out=W(5), in0=W(4), scalar=-2.0 * PI, in1=W(5), op0=ALU.mult,
                                           op1=ALU.add), ["w"], ["w"])
        P.act(lambda e: e.activation(out=W(6), in_=W(3), func=AF.Sin), reads=["w"], writes=["w"])
        P.act(lambda e: e.activation(out=W(7), in_=W(5), func=AF.Sin), reads=["w"], writes=["w"])
        abr, abi = PW[:, 1, 0, :], PW[:, 1, 1, :]
        V(lambda e: tt_(e, abr, W(2), W(7), TT), ["w"], ["PW"])
        V(lambda e: tt_(e, abi, W(2), W(6), TT), ["w"], ["PW"])
        V(lambda e: e.memset(PW[:, 0, 0, :], 1.0), [], ["PW"])
        V(lambda e: e.memset(PW[:, 0, 1, :], 0.0), [], ["PW"])
        V(lambda e: tt_(e, W(0), lr, lr, TT), ["lam"], ["w"])
        V(lambda e: tt_(e, W(1), li, li, TT), ["lam"], ["w"])
        V(lambda e: tt_(e, W(0), W(0), W(1), ALU.add), ["w"], ["w"])
        V(lambda e: e.reciprocal(out=W(0), in_=W(0)), ["w"], ["w"])
        V(lambda e: e.tensor_scalar(out=W(1), in0=abr, scalar1=-1.0, scalar2=None, op0=ALU.add), ["PW"], ["w"])
        V(lambda e: tt_(e, W(2), W(1), lr, TT), ["w", "lam"], ["w"])
        V(lambda e: tt_(e, W(3), abi, li, TT), ["PW", "lam"], ["w"])
        V(lambda e: tt_(e, W(2), W(2), W(3), ALU.add), ["w"], ["w"])
        V(lambda e: tt_(e, W(8), W(2), W(0), TT), ["w"], ["w"])
        V(lambda e: tt_(e, W(2), abi, lr, TT), ["PW", "lam"], ["w"])
        V(lambda e: tt_(e, W(3), W(1), li, TT), ["w", "lam"], ["w"])
        V(lambda e: tt_(e, W(2), W(2), W(3), ALU.subtract), ["w"], ["w"])
        V(lambda e: tt_(e, W(9), W(2), W(0), TT), ["w"], ["w"])
        frb = W(8).unsqueeze(2).to_broadcast([128, 16, 16])
        fib = W(9).unsqueeze(2).to_broadcast([128, 16, 16])
        V(lambda e: tt_(e, w3[:, 0], bb[:, 0], frb, TT), ["bb", "w"], ["w3"])
        V(lambda e: tt_(e, w3[:, 1], bb[:, 1], fib, TT), ["bb", "w"], ["w3"])
        V(lambda e: tt_(e, w3[:, 2], bb[:, 1], frb, TT), ["bb", "w"], ["w3"])
        V(lambda e: tt_(e, w3[:, 3], bb[:, 0], fib, TT), ["bb", "w"], ["w3"])
        V(lambda e: tt_(e, bb[:, 0], w3[:, 0], w3[:, 1], ALU.subtract), ["w3"], ["bb"])
        V(lambda e: tt_(e, bb[:, 1], w3[:, 2], w3[:, 3], ALU.add), ["w3"], ["bb"])
        P.pool(lambda e: e.memset(BBp[:], 0.0), [], ["BBp"])
        for ri in range(2):
            for g2 in range(2):
                psl = slice(g2 * 64, (g2 + 1) * 64)
                V(lambda e, ri=ri, g2=g2, psl=psl: e.tensor_copy(out=BBp[psl, ri, :, g2 * 16:(g2 + 1) * 16],
                                                               in_=bb[psl, ri, :, :]), ["bb", "BBp"], ["BBp"])
        V(lambda e: e.tensor_copy(out=BBh[:], in_=BBp[:]), ["BBp"], ["BBh"])
        V(lambda e: tt_(e, dtmp[:], BBp[:], BBh[:], ALU.subtract), ["BBp", "BBh"], ["dtmp"])
        V(lambda e: e.tensor_copy(out=BBl[:], in_=dtmp[:]), ["dtmp"], ["BBl"])
        def cmul(o_r, o_i, a_r, a_i, b_r, b_i, shp, rk, wk):
            n_ = shp[1] if len(shp) == 3 else 1
            t = [pwt[:, i, 0:n_, :] if len(shp) == 3 else pwt[:, i, 0, :] for i in range(4)]
            V(lambda e: tt_(e, t[0], a_r, b_r, TT), rk, ["pwt"])
            V(lambda e: tt_(e, t[1], a_i, b_i, TT), rk, ["pwt"])
            V(lambda e: tt_(e, t[2], a_r, b_i, TT), rk, ["pwt"])
            V(lambda e: tt_(e, t[3], a_i, b_r, TT), rk, ["pwt"])
            V(lambda e: tt_(e, o_r, t[0], t[1], ALU.subtract), ["pwt"], wk)
            V(lambda e: tt_(e, o_i, t[2], t[3], ALU.add), ["pwt"], wk)

        def powers(T, key):
            cmul(T[:, 2, 0, :], T[:, 2, 1, :], T[:, 1, 0, :], T[:, 1, 1, :], T[:, 1, 0, :], T[:, 1, 1, :],
                 [128, 16], [key], [key])
            for (lo, n_, b) in ((1, 2, 2), (1, 4, 4)):
                bc_ = lambda ap: ap.unsqueeze(1).to_broadcast([128, n_, 16])
                cmul(T[:, b + 1:b + 1 + n_, 0, :], T[:, b + 1:b + 1 + n_, 1, :],
                     T[:, lo:lo + n_, 0, :], T[:, lo:lo + n_, 1, :], bc_(T[:, b, 0, :]), bc_(T[:, b, 1, :]),
                     [128, n_, 16], [key], [key])

        powers(PW, "PW")
        V(lambda e: e.tensor_copy(out=A8[:], in_=PW[:, 8, :, :]), ["PW"], ["A8"])
        V(lambda e: e.tensor_copy(out=A8p[:, 1, :, :], in_=PW[:, 8, :, :]), ["PW"], ["A8p"])
        powers(A8p, "A8p")
        V(lambda e: e.tensor_copy(out=A64[:, 0, :, :], in_=A8p[:, 8, :, :]), ["A8p"], ["A64"])
        for k in range(3):
            cmul(A64[:, k + 1, 0, :], A64[:, k + 1, 1, :], A64[:, k, 0, :], A64[:, k, 1, :], A64[:, k, 0, :],
                 A64[:, k, 1, :], [128, 16], ["A64"], ["A64"])
        def rb_quarter(qd, G, Y, yk, ykr):
            m0 = 2 * qd
            for mm in range(2):
                pr_ = PW[:, m0 + mm, 0, :].unsqueeze(2).to_broadcast([128, 16, 32])
                pi_ = PW[:, m0 + mm, 1, :].unsqueeze(2).to_broadcast([128, 16, 32])
                G(lambda e, mm=mm, pr_=pr_: tt_(e, Y[:, 0, mm], BBp[:, 0], pr_, TT), ["BBp", "PW"], yk)
                G(lambda e, mm=mm, pi_=pi_: tt_(e, Y[:, 1, mm], BBp[:, 1], pi_, TT), ["BBp", "PW"], yk)
                G(lambda e, mm=mm: tt_(e, Y[:, 0, mm], Y[:, 0, mm], Y[:, 1, mm], ALU.subtract), [], yk)
                G(lambda e, mm=mm, pr_=pr_: tt_(e, Y[:, 2, mm], BBp[:, 1], pr_, TT), ["BBp", "PW"], yk)
                G(lambda e, mm=mm, pi_=pi_: tt_(e, Y[:, 3, mm], BBp[:, 0], pi_, TT), ["BBp", "PW"], yk)
                G(lambda e, mm=mm: tt_(e, Y[:, 2, mm], Y[:, 2, mm], Y[:, 3, mm], ALU.add), [], yk)
            for mm in range(2):
                i = 7 - (m0 + mm)
                for ri in range(2):
                    sel = (2 * (2 * qd + mm) + ri) % 2
                    for o in range(6):
                        nv = CH_ROWS[o]
                        bank = 4 + 2 * sel + (o // 4)
                        out = ps[bank][:nv, (o % 4) * 128:(o % 4 + 1) * 128]
                        src_ = Y[:, 2 * ri, mm, 3 * o:3 * o + nv // 32, :].rearrange("p a c -> p (a c)")
                        P.pe(lambda e, out=out, src_=src_: e.transpose(out=out, in_=src_, identity=ident_f[:]),
                             reads=[ykr, "ident_f"], writes=[("ps", bank)])
                    b0 = 4 + 2 * sel
                    P.act(lambda e, i=i, ri=ri, b0=b0: e.copy(
                        out=RB[:96, 0:4, i, ri, :], in_=ps[b0][:96, :].rearrange("p (o c) -> p o c", c=128)),
                        [("ps", b0)], [("RB", i, ri)])
                    P.act(lambda e, i=i, ri=ri, b0=b0: e.copy(
                        out=RB[:96, 4, i, ri, :], in_=ps[b0 + 1][:96, 0:128]), [("ps", b0 + 1)], [("RB4", i, ri)])
                    P.act(lambda e, i=i, ri=ri, b0=b0: e.copy(
                        out=RB[:32, 5, i, ri, :], in_=ps[b0 + 1][:32, 128:256]), [("ps", b0 + 1)], [("RB5", i, ri)])
        Gp = lambda fn, r, w_: P.pool(fn, reads=r, writes=w_)
        Gd = lambda fn, r, w_: P.dve(fn, reads=r, writes=w_)
        for qd in (0, 1):
            rb_quarter(qd, Gp, Y, ["Y", "h0s"], "Y")
        P.pool(lambda e: e.memset(KBD[:], 0.0), [], ["KBD"])
        P.pool(lambda e: e.memset(CAr_b[:], 0.0), [], ["CAb"])
        P.pool(lambda e: e.memset(nCAi_b[:], 0.0), [], ["CAb"])
        P.pool(lambda e: e.memset(CAl[:], 0.0), [], ["CAl"])
        ccb = lambda ri: cc[:, ri].unsqueeze(1).to_broadcast([128, 9, 16, 16])
        pwb = lambda ri: PW[:, :, ri, :].unsqueeze(3).to_broadcast([128, 9, 16, 16])
        cck = [("cc", r_, t_) for r_ in range(2) for t_ in range(2)] + ["PW"]
        V(lambda e: tt_(e, X[:, 0], ccb(0), pwb(0), TT), cck, ["X0"])
        V(lambda e: tt_(e, X[:, 1], ccb(1), pwb(1), TT), cck, ["X1"])
        V(lambda e: tt_(e, X[:, 2], ccb(0), pwb(1), TT), cck, ["X2"])
        V(lambda e: tt_(e, X[:, 3], ccb(1), pwb(0), TT), cck, ["X3"])
        V(lambda e: tt_(e, X[:, 0], X[:, 0], X[:, 1], ALU.subtract), ["X0", "X1"], ["X0"])
        V(lambda e: tt_(e, X[:, 2], X[:, 2], X[:, 3], ALU.add), ["X2", "X3"], ["X2"])
        V(lambda e: e.tensor_scalar(out=X[:, 2], in0=X[:, 2], scalar1=-1.0, scalar2=None, op0=ALU.mult),
          ["X2"], ["X2"])
        for q, (sx, tab) in enumerate(((0, CAr_b), (2, nCAi_b))):
            for g2 in range(2):
                psl = slice(g2 * 64, (g2 + 1) * 64)
                gc = slice(g2 * 16, (g2 + 1) * 16)
                V(lambda e, sx=sx, tab=tab, psl=psl, gc=gc: e.tensor_copy(out=tab[psl, :, :, gc], in_=X[psl, sx]),
                  ["X%d" % sx, "CAb"], ["CAb"])
                V(lambda e, sx=sx, tab=tab, psl=psl, gc=gc: tt_(e, X[psl, sx + 1, 0:8], X[psl, sx, 0:8],
                                                                 tab[psl, 0:8, :, gc], ALU.subtract),
                  ["X%d" % sx, "CAb"], ["X%d" % (sx + 1)])
                V(lambda e, sx=sx, q=q, psl=psl, gc=gc: e.tensor_copy(out=CAl[psl, q, :, :, gc],
                                                                    in_=X[psl, sx + 1, 0:8]),
                  ["X%d" % (sx + 1), "CAl"], ["CAl"])
        for m in range(8 if not int(os.environ.get('KT_NOKBD', '0')) else 0):
            bk = m // 2
            for p in range(16):
                o, i3 = p // 3, p % 3
                c0_ = (m % 2) * 192 + o * 32
                out = ps[bk][i3 * 32:(i3 + 1) * 32, c0_:c0_ + 32]
                combos = []
                for q, tab in ((0, CAr_b), (1, nCAi_b)):
                    combos += [(BBh[:, q, p, :], tab[:, m, p, :]), (BBh[:, q, p, :], CAl[:, q, m, p, :]),
                               (BBl[:, q, p, :], tab[:, m, p, :])]
                for ci, (l_, r_) in enumerate(combos):
                    P.pe(lambda e, out=out, l_=l_, r_=r_, ci=ci: e.matmul(out, lhsT=l_, rhs=r_, start=(ci == 0),
                                                                      stop=(ci == 5)),
                         reads=["BBh", "BBl", "CAb", "CAl"], writes=[("ps", bk)])
        for bk in range(4 if not int(os.environ.get('KT_NOEV', '0')) else 0):
            for i3 in range(3):
                no = 6 if i3 == 0 else 5
                rsl = slice(i3 * 32, (i3 + 1) * 32)
                P.act(lambda e, bk=bk, i3=i3, no=no, rsl=rsl: e.copy(
                    out=KBD[rsl, 0:no, 2 * bk:2 * bk + 2, i3 * 32:(i3 + 1) * 32].rearrange("p o m c -> p m o c"),
                    in_=ps[bk][rsl, 0:384].rearrange("p (m o c) -> p m o c", m=2, o=6)[:, :, 0:no, :]),
                    [("ps", bk), "KBD"], ["KBD"])

        Y2 = X[:, 0:2].rearrange("p a b c d -> p (a b c d)")[:, 0:4096].rearrange("p (q m g c) -> p q m g c", q=4, m=2, g=16)
        for qd in (2, 3):
            rb_quarter(qd, Gd, Y2, ["X0", "X1", "Y2"], "Y2")

    def load_x(hf):
        for tt, (c0, rows) in enumerate(TILES):
            src = xp[hf * 1024 + c0: hf * 1024 + c0 + rows, :] if tt < 8 else xs[hf * 64:(hf + 1) * 64, :]
            P.dma("sp", lambda e, tt=tt, rows=rows, src=src: e.dma_start(out=x_tm[:rows, tt, :], in_=src),
                  writes=[("x", tt)])

    load_x(0)
    if not NOTAB:
        build_tables()
    P.dma("sp", lambda e: e.dma_start(out=kws[:, 0:120, :], in_=ck[:, 8:128, :]), writes=["kws_c"])
    P.dma("sp", lambda e: e.dma_start(out=vws[:, 0:120, :], in_=cv[:, 8:128, :]), writes=["vws_c"])
    DBG = {}
    if int(os.environ.get('KDEBUG', '0')):
        P.barrier()
        for nm, t, shp in (("KBD", KBD, [128, 6 * 8 * 128]), ("RB", RB, [128, 6 * 8 * 2 * 128]),
                           ("CAr", CAr_b, [128, 9 * 16 * 32]), ("nCAi", nCAi_b, [128, 9 * 16 * 32]),
                           ("A8", A8, [128, 32]), ("Dsk", Dsk, [128, 6]), ("h0T", h0T, [128, 512])):
            dd = dout("dbg_" + nm, shp)
            DBG[nm] = dd
            flat = t[:]
            nd = len(t.shape)
            if nd == 3:
                flat = t[:].rearrange("p a b -> p (a b)")
            elif nd == 4:
                flat = t[:].rearrange("p a b c -> p (a b c)")
            elif nd == 5:
                flat = t[:].rearrange("p a b c d -> p (a b c d)")
            P.dma("pool", lambda e, dd=dd, flat=flat: e.dma_start(out=dd[:, :], in_=flat),
                  reads=["KBD", "RB", "CAb", "A8", "Dsk", "h0T"], writes=["dbg" + nm])

    cnt = {"n": 0, "pb": 0}

    def rmsnorm_stats(tt, rows, junk, stages=None):
        sl = cnt["n"] % 4
        cnt["n"] += 1
        s1 = lambda: P.act(lambda e: e.activation(out=junk[:rows, :], in_=x_tm[:rows, tt, :], func=AF.Square,
                                                  accum_out=stat[:rows, sl, 0:1]),
                           reads=[("x", tt)], writes=["junk", ("stat", sl)])
        s2 = lambda: P.dve(lambda e: e.tensor_scalar(out=stat[:rows, sl, 1:2], in0=stat[:rows, sl, 0:1],
                                                     scalar1=1.0 / D, scalar2=EPS, op0=ALU.mult, op1=ALU.add),
                           reads=[("stat", sl)], writes=[("stat", sl)])
        s3 = lambda: P.act(lambda e: e.activation(out=stat[:rows, sl, 2:3], in_=stat[:rows, sl, 1:2], func=AF.Sqrt),
                           reads=[("stat", sl)], writes=[("stat", sl)])
        s4 = lambda: P.dve(lambda e: e.reciprocal(out=stat[:rows, sl, 3:4], in_=stat[:rows, sl, 2:3]),
                           reads=[("stat", sl)], writes=[("stat", sl)])
        if stages is None:
            s1(); s2(); s3(); s4()
        else:
            stages.extend([s1, s2, s3, s4])
        return sl

    def norm_stages(gi, tt, hT, hn, junk):
        c0, rows = TILES[tt]
        st_ = []
        sl = rmsnorm_stats(tt, rows, junk, stages=st_)
        hs = tt % 2
        pbk = 6 + tt % 2
        ptv = ps[pbk][:, 0:512].bitcast(BF16).rearrange("p (c t) -> p c t", c=8)

        def s5():
            if tt % 2:
                P.act(lambda e: e.activation(out=hn[:rows, hs, :], in_=x_tm[:rows, tt, :], func=AF.Copy,
                                             scale=stat[:rows, sl, 3:4]),
                      reads=[("x", tt), ("stat", sl)], writes=[("hn", hs)])
            else:
                P.dve(lambda e: e.tensor_scalar(out=hn[:rows, hs, :], in0=x_tm[:rows, tt, :],
                                                scalar1=stat[:rows, sl, 3:4], scalar2=None, op0=ALU.mult),
                      reads=[("x", tt), ("stat", sl)], writes=[("hn", hs)])

        def s6():
            for dc in range(8):
                P.pe(lambda e, dc=dc: e.transpose(out=ptv[:, dc, :rows], in_=hn[:rows, hs, dc * 128:(dc + 1) * 128],
                                                  identity=ident_b[:rows, :rows]),
                     reads=[("hn", hs), "ident_b"], writes=[("ps", pbk)])
            P.dve(lambda e: e.tensor_tensor(out=hT[:, :, c0:c0 + rows], in0=ptv[:, :, :rows],
                                            in1=gcol[:, gi, :].unsqueeze(2).to_broadcast([128, 8, rows]),
                                            op=ALU.mult),
                  reads=[("ps", pbk), "gcol"], writes=[("hT", tt)])

        return [st_[0], st_[1], st_[2], st_[3], s5, s6]

    def norm_tile(gi, tt, hT, hn, junk):
        for s in norm_stages(gi, tt, hT, hn, junk):
            s()

    def norm_to_hT(gi, hT, hn, junk):
        per_tile = [norm_stages(gi, tt, hT, hn, junk) for tt in range(len(TILES))]
        nst = 6
        for step in range(len(TILES) + nst - 1):
            for k in range(nst):
                t = step - k
                if 0 <= t < len(TILES):
                    per_tile[t][k]()

    def grp(c):
        return 0 if c < 512 else (512 if c < 1024 else 1024)

    def tiles_in(c0, n):
        return [t for t, (a, r) in enumerate(TILES) if c0 <= a < c0 + n]

    def ffn_takes():
        A.off = 0
        hT = A.take([128, 8, NT], BF16)
        hid = A.take([128, 11, NT], BF16)
        wd_sb = A.take([128, 11, D], BF16)
        wgu = A.take([128, 3, 2, 8, 128], BF16)
        hn = A.take([128, 2, D], BF16)
        junk = A.take([128, D], BF16)
        sg = A.take([128, 2, 512], BF16)
        return hT, hid, wd_sb, wgu, hn, junk, sg

    def ffn(which, gi, barrier=True, after_tile=None, prenormed=False):
        if barrier:
            P.barrier()
        P.phase = 'ffn_' + which + str(P.hf)
        hT, hid, wd_sb, wgu, hn, junk, sg = ffn_takes()
        if after_tile is not None:
            A.off = 43136
            ep_bufs = (A.take([128, 2, D], F32), A.take([128, D], BF16))
            assert A.off <= 49024, A.off
        if not prenormed:
            norm_to_hT(gi, hT, hn, junk)
        wg, wu, wdn = Wg[which], Wu[which], Wd[which]
        k = 0
        for fh in range(2):
            for j in range(11):
                fc = fh * 11 + j
                s = fc % 3
                for m, w_ in enumerate((wg, wu)):
                    P.dma("pool", lambda e, fc=fc, s=s, m=m, w_=w_: e.dma_start(
                        out=wgu[:, s, m, :, :],
                        in_=w_[:, fc * 128:(fc + 1) * 128].rearrange("(c p) f -> p c f", p=128)),
                        writes=[("wgu", s, m)])
                P.dma("pool", lambda e, fc=fc, j=j: e.dma_start(
                    out=wd_sb[:, j, :], in_=wdn[fc * 128:(fc + 1) * 128, :]), writes=[("wd", j)])
                for (c0, n) in GROUPS:
                    tts = tiles_in(c0, n)
                    pb = (k % 3) * 2
                    ss = k % 2
                    k += 1
                    for m in range(2):
                        for dc in range(8):
                            P.pe(lambda e, m=m, dc=dc, s=s, c0=c0, n=n, pb=pb: e.matmul(
                                ps[pb + m][:, :n], lhsT=wgu[:, s, m, dc, :], rhs=hT[:, dc, c0:c0 + n],
                                start=(dc == 0), stop=(dc == 7)),
                                reads=[("wgu", s, m)] + [("hT", t) for t in tts], writes=[("ps", pb + m)])
                    P.act(lambda e, pb=pb, n=n, ss=ss: e.activation(out=sg[:, ss, :n], in_=ps[pb][:, :n],
                                                                  func=AF.Silu),
                          reads=[("ps", pb)], writes=[("sg", ss)])
                    P.dve(lambda e, pb=pb, n=n, ss=ss, j=j, c0=c0: e.tensor_tensor(
                        out=hid[:, j, c0:c0 + n], in0=sg[:, ss, :n], in1=ps[pb + 1][:, :n], op=ALU.mult),
                        reads=[("sg", ss), ("ps", pb + 1)], writes=[("hid", j, c0)])
            for tt, (c0, rows) in enumerate(TILES):
                g0 = [g for (g, n) in GROUPS if g <= c0 < g + n][0]
                for hh in range(2):
                    pb = 6 + hh
                    for j in range(11):
                        P.pe(lambda e, j=j, c0=c0, rows=rows, hh=hh, pb=pb: e.matmul(
                            ps[pb][:rows, :], lhsT=hid[:, j, c0:c0 + rows],
                            rhs=wd_sb[:, j, hh * 512:(hh + 1) * 512], start=(j == 0), stop=(j == 10)),
                            reads=[("hid", j, g0), ("wd", j)], writes=[("ps", pb)])
                    P.dve(lambda e, tt=tt, rows=rows, hh=hh, pb=pb: e.scalar_tensor_tensor(
                        out=x_tm[:rows, tt, hh * 512:(hh + 1) * 512], in0=ps[pb][:rows, :], scalar=0.5,
                        in1=x_tm[:rows, tt, hh * 512:(hh + 1) * 512], op0=ALU.mult, op1=ALU.add),
                        reads=[("ps", pb), ("x", tt)], writes=[("x", tt)])
                if fh == 1 and after_tile is not None:
                    after_tile(tt, *ep_bufs)
        return junk

    def pbank(lo, hi):
        cnt["pb"] += 1
        return lo + cnt["pb"] % (hi - lo)

    def wslice(col0, ncols):
        return w_in[:, col0:col0 + ncols].rearrange("(c p) f -> p c f", p=128)

    def mix_takes():
        A.off = 0
        hT = A.take([128, 8, NT], BF16)
        mg = A.take([128, 9, D], BF16)
        mark = A.off
        qT = A.take([128, 8, NT], BF16)
        kkT = A.take([128, 4, NT], BF16)
        vaug = A.take([128, 9, 4, 68], BF16)
        kvout = A.take([128, 2, 512], F32)
        wf = A.take([128, 3, 8, 128], BF16)
        wt = A.take([128, 2, 8, 256], BF16)
        oT_sb = wt[:, :, :, :].rearrange("p a b c -> p (a b c)")[:, 0:2048].bitcast(F32)
        et = A.take([128, 4, 2, 256], BF16)
        etc_ = A.take([128, 2, 128], BF16)
        etn = A.take([128, 1, 128], BF16)
        den = A.take([128, 2, 2, 4], F32)
        atmp = A.take([128, 2, 4, 64], F32)
        ckd = A.take([128, 2, 4, 2, 64], BF16)
        kcT = A.take([128, 2, 4, 128], BF16)
        vcaug = A.take([128, 3, 4, 68], BF16)
        assert A.off >= 43136, A.off
        hn = A.take([128, 2, D], BF16)
        etr = A.take([128, 2, 2, 256], BF16)
        junk = etr[:, 0:2, :, :].rearrange("p a b c -> p (a b c)")
        return (hT, mg, mark, hn, etr, junk, qT, kkT, vaug, kvout, wf, wt, oT_sb, et, etc_, etn, den, atmp, ckd,
                kcT, vcaug)

    def mix(hf):
        P.barrier()
        P.phase = 'mix1_' + str(hf)
        (hT, mg, mark, hn, etr, junk, qT, kkT, vaug, kvout, wf, wt, oT_sb, et, etc_, etn, den, atmp, ckd, kcT,
         vcaug) = mix_takes()
        P.dve(lambda e: e.memset(vaug[:, :, :, 64:68], 1.0), writes=["vaug1"])
        P.dve(lambda e: e.memset(vcaug[:, :, :, 64:68], 1.0), writes=["vcaug1"])

        ftiles = [("q", i) for i in range(8)] + [("kk", g) for g in range(4)]
        for ti, (kind, idx) in enumerate(ftiles):
            s = ti % 3
            if kind == "q":
                P.dma("pool", lambda e, s=s, idx=idx: e.dma_start(out=wf[:, s, :, :], in_=wslice(idx * 128, 128)),
                      writes=[("wf", s)])
                dest = qT
            else:
                for h2 in range(2):
                    P.dma("pool", lambda e, s=s, idx=idx, h2=h2: e.dma_start(
                        out=wf[:, s, :, h2 * 64:(h2 + 1) * 64], in_=wslice(1024 + idx * 64, 64)),
                        writes=[("wf", s)])
                dest = kkT
            for (c0, n) in GROUPS:
                pb = pbank(0, 4)
                for dc in range(8):
                    P.pe(lambda e, s=s, dc=dc, c0=c0, n=n, pb=pb: e.matmul(
                        ps[pb][:, :n], lhsT=wf[:, s, dc, :], rhs=hT[:, dc, c0:c0 + n], start=(dc == 0),
                        stop=(dc == 7)),
                        reads=[("wf", s)] + [("hT", t) for t in tiles_in(c0, n)], writes=[("ps", pb)])
                if cnt["pb"] % 2:
                    P.act(lambda e, dest=dest, idx=idx, c0=c0, n=n, pb=pb: e.copy(out=dest[:, idx, c0:c0 + n],
                                                                                in_=ps[pb][:, :n]),
                          reads=[("ps", pb)], writes=[(kind, idx, c0)])
                else:
                    P.dve(lambda e, dest=dest, idx=idx, c0=c0, n=n, pb=pb: e.tensor_copy(out=dest[:, idx, c0:c0 + n],
                                                                                       in_=ps[pb][:, :n]),
                          reads=[("ps", pb)], writes=[(kind, idx, c0)])

        if KSUB <= 1:
            return hT, mg, mark
        blocks = [("k", 1024), ("v", 1280)] + [("ga", 2048 + 256 * j) for j in range(4)]
        KBLK = os.environ.get('KBLK', 'k,v,ga').split(',')
        blocks = [b for b in blocks if b[0] in KBLK]
        for bi, (kind, col0) in enumerate(blocks):
            s = bi % 2
            P.dma("pool", lambda e, s=s, col0=col0: e.dma_start(out=wt[:, s, :, :], in_=wslice(col0, 256)),
                  writes=[("wt", s)])
            for tt, (c0, rows) in enumerate(TILES):
                is_out = (tt == 8) or (tt == 7 and hf == 1)
                osl = 0 if tt == 8 else 1
                if kind == "k" and not is_out:
                    continue
                pb = pbank(4, 8)
                for dc in range(8):
                    P.pe(lambda e, s=s, dc=dc, c0=c0, rows=rows, pb=pb: e.matmul(
                        ps[pb][:rows, :256], lhsT=hT[:, dc, c0:c0 + rows], rhs=wt[:, s, dc, :], start=(dc == 0),
                        stop=(dc == 7)),
                        reads=[("wt", s), ("hT", tt)], writes=[("ps", pb)])
                if kind == "k":
                    P.act(lambda e, rows=rows, osl=osl, pb=pb: e.copy(out=kvout[:rows, osl, 0:256],
                                                                     in_=ps[pb][:rows, :256]),
                          reads=[("ps", pb)], writes=[("kvout", osl, 0)])
                elif kind == "v":
                    P.act(lambda e, rows=rows, tt=tt, pb=pb: e.copy(
                        out=vaug[:rows, tt, :, 0:64], in_=ps[pb][:rows, :256].rearrange("p (g d) -> p g d", g=4)),
                        reads=[("ps", pb)], writes=[("vaug", tt)])
                    if is_out:
                        P.dve(lambda e, rows=rows, osl=osl, pb=pb: e.tensor_copy(out=kvout[:rows, osl, 256:512],
                                                                                in_=ps[pb][:rows, :256]),
                              reads=[("ps", pb)], writes=[("kvout", osl, 1)])
                else:
                    j = (col0 - 2048) // 256
                    P.act(lambda e, rows=rows, tt=tt, j=j, pb=pb: e.activation(
                        out=mg[:rows, tt, j * 256:(j + 1) * 256], in_=ps[pb][:rows, :256], func=AF.Sigmoid),
                        reads=[("ps", pb)], writes=[("mg", tt, j)])
        if KSUB <= 2:
            return hT, mg, mark
        if hf == 1:
            P.dma("sp", lambda e: e.dma_start(out=kwp[:, :], in_=kvout[:, 1, 0:256]), reads=[("kvout", 1, 0)],
                  writes=["kwp"])
            P.dma("sp", lambda e: e.dma_start(out=vwp[:, :], in_=kvout[:, 1, 256:512]), reads=[("kvout", 1, 1)],
                  writes=["vwp"])
        for s_ in range(8):
            sl_ = hf * 8 + s_
            P.dma("sp", lambda e, s_=s_, sl_=sl_: e.dma_start(out=kws[sl_, 120:128, :],
                                                            in_=kvout[s_ * 8:(s_ + 1) * 8, 0, 0:256]),
                  reads=[("kvout", 0, 0)], writes=[("kws", sl_)])
            P.dma("sp", lambda e, s_=s_, sl_=sl_: e.dma_start(out=vws[sl_, 120:128, :],
                                                            in_=kvout[s_ * 8:(s_ + 1) * 8, 0, 256:512]),
                  reads=[("kvout", 0, 1)], writes=[("vws", sl_)])

        if KSUB <= 3:
            return hT, mg, mark
        P.phase = 'attn_' + str(hf)
        def normalize(g, tt, rows, aob, dsl):
            ao = ps[aob][:rows, :].rearrange("p (r c) -> p r c", c=128)
            P.dve(lambda e: e.tensor_tensor(out=den[:rows, dsl, 0, :], in0=ao[:, :, 64],
                                            in1=esink[:rows, 4 * g:4 * g + 4], op=ALU.add),
                  reads=[("ps", aob), "esink"], writes=[("den", dsl)])
            P.dve(lambda e: e.reciprocal(out=den[:rows, dsl, 1, :], in_=den[:rows, dsl, 0, :]),
                  reads=[("den", dsl)], writes=[("den", dsl)])
            P.dve(lambda e: e.tensor_tensor(
                out=atmp[:rows, dsl, :, :], in0=ao[:, :, 0:64],
                in1=den[:rows, dsl, 1, :].unsqueeze(2).to_broadcast([rows, 4, 64]), op=ALU.mult),
                reads=[("ps", aob), ("den", dsl)], writes=[("atmp", dsl)])
            mgv = mg[:rows, tt, g * 256:(g + 1) * 256]
            P.dve(lambda e: e.tensor_tensor(out=mgv, in0=atmp[:rows, dsl, :, :].rearrange("p r c -> p (r c)"),
                                            in1=mgv, op=ALU.mult),
                  reads=[("atmp", dsl), ("mg", tt, g)], writes=[("mg", tt, g)])

        DEPTH = 2
        its = [(tt, g, hh) for tt in range(8) for g in range(4) for hh in range(2)]
        info = {}

        def stageA2(n0):
            todo = []
            for n in (n0, n0 + 1):
                tt, g, hh = its[n]
                c0 = tt * 128
                hsl = slice(hh * 64, (hh + 1) * 64)
                stb = n % 4
                st = ps[stb][:, :].rearrange("p (k c) -> p k c", k=2)
                es = n % 4
                kbs = []
                if tt > 0:
                    kbs.append((0, kkT[hsl, g, c0 - 128:c0], vaug[:, tt - 1, g, 0:65], [("kk", g, grp(c0 - 128)), ("vaug", tt - 1)]))
                elif hf == 1:
                    kbs.append((0, kk_carry[hsl, g, :], v_carry[:, g, 0:65], ["kk_carry", "v_carry"]))
                kbs.append((1, kkT[hsl, g, c0:c0 + 128], vaug[:, tt, g, 0:65], [("kk", g, grp(c0)), ("vaug", tt)]))
                info[n] = (kbs, es)
                todo.append((n, tt, g, hh, c0, hsl, stb, st, es, kbs))
            for ki in range(len(todo[0][9])):
                for (n, tt, g, hh, c0, hsl, stb, st, es, kbs) in todo:
                    (kb, kap, vap, keys) = kbs[ki]
                    qg = grp(c0)
                    P.pe(lambda e, kb=kb, kap=kap, hsl=hsl, g=g, c0=c0, st=st: e.matmul(
                        st[:, kb, :], lhsT=kap, rhs=qT[hsl, 2 * g:2 * g + 2, c0:c0 + 128], start=True, stop=True),
                        reads=[keys[0], ("q", 2 * g, qg), ("q", 2 * g + 1, qg)], writes=[("ps", stb)])
            for (n, tt, g, hh, c0, hsl, stb, st, es, kbs) in todo:
                k0 = kbs[0][0]
                er = n % 2
                P.act(lambda e, k0=k0, st=st, er=er: e.activation(out=etr[:, er, k0:2, :], in_=st[:, k0:2, :],
                                                                  func=AF.Exp, scale=0.125),
                      reads=[("ps", stb)], writes=[("etr", er)])
                (P.pool if hh == 0 else P.dve)(
                    lambda e, k0=k0, es=es, er=er: e.tensor_tensor(out=et[:, es, k0:2, :], in0=etr[:, er, k0:2, :],
                                                                   in1=m01_b[:, k0:2, :], op=ALU.mult),
                    reads=[("etr", er), "m01"], writes=[("et", es)])

        def stageB(n):
            tt, g, hh = its[n]
            kbs, es = info[n]
            aob = 4 + (n // 2) % 2
            dsl = (n // 2) % 2
            ao = ps[aob][:, :].rearrange("p (r c) -> p r c", c=128)
            for j in range(2):
                r = 2 * j + hh
                for n_, (kb, kap, vap, keys) in enumerate(kbs):
                    P.pe(lambda e, r=r, kb=kb, j=j, vap=vap, es=es, ao=ao, n_=n_, L=len(kbs): e.matmul(
                        ao[:, r, 0:65], lhsT=et[:, es, kb, j * 128:(j + 1) * 128], rhs=vap,
                        start=(n_ == 0), stop=(n_ == L - 1)),
                        reads=[("et", es), keys[1], "vaug1"], writes=[("ps", aob)])
            if hh == 1:
                normalize(g, tt, 128, aob, dsl)

        for n in range(0, len(its) + DEPTH, 2):
            if n < len(its):
                stageA2(n)
            for m_ in (n - DEPTH, n - DEPTH + 1):
                if 0 <= m_ < len(its):
                    stageB(m_)
        if KSUB <= 4:
            return hT, mg, mark
        P.phase = 'attS_' + str(hf)
        qc = slice(1024, 1088)
        oTb = [ps[0], ps[1]]
        first_in_bank = [True, True]
        ecnt = 0
        for g in range(4):
            for hh in range(2):
                hsl = slice(hh * 64, (hh + 1) * 64)
                es = 0
                ecnt += 1
                P.pe(lambda e, hsl=hsl, g=g: e.matmul(
                    ps[6][:64, 0:128], lhsT=kkT[hsl, g, qc], rhs=qT[hsl, 2 * g:2 * g + 2, qc], start=True,
                    stop=False),
                    reads=[("kk", g, 1024), ("q", 2 * g, 1024), ("q", 2 * g + 1, 1024)], writes=[("ps", 6)])
                P.pe(lambda e: e.matmul(ps[6][:64, 0:128], lhsT=ident_b[:, :64], rhs=mnew_b[:, :],
                                        start=False, stop=True),
                     reads=["ident_b", "mnew"], writes=[("ps", 6)])
                P.act(lambda e, es=es: e.activation(out=etn[:64, es, :], in_=ps[6][:64, 0:128], func=AF.Exp,
                                                    scale=0.125),
                      reads=[("ps", 6)], writes=[("etn", es)])
                for j in range(2):
                    h = 4 * g + 2 * j + hh
                    bk = h // 8
                    P.pe(lambda e, h=h, bk=bk, j=j, es=es, g=g, st_=first_in_bank[bk]: e.matmul(
                        oTb[bk][:65, (h % 8) * 64:(h % 8 + 1) * 64], lhsT=vaug[:64, 8, g, 0:65],
                        rhs=etn[:64, es, j * 64:(j + 1) * 64], start=st_, stop=False, skip_group_check=True),
                        reads=[("etn", es), ("vaug", 8), "vaug1"], writes=[("ps", bk)])
                    first_in_bank[bk] = False
        ptk = ps[7][:, 0:256].bitcast(BF16).rearrange("p (g t) -> p g t", g=4)
        def sA(s_):
            sl_ = hf * 8 + s_
            cs = s_ % 2
            for d2 in range(2):
                P.dma("pool", lambda e, cs=cs, d2=d2, sl_=sl_: e.dma_start(
                    out=ckd[:, cs, :, d2, :], in_=ck[sl_].rearrange("p (g d) -> p g d", g=4)),
                    writes=[("ckd", cs)])
            vs = s_ % 3
            P.dma("pool", lambda e, vs=vs, sl_=sl_: e.dma_start(
                out=vcaug[:, vs, :, 0:64], in_=cv[sl_].rearrange("p (g d) -> p g d", g=4)),
                reads=["vcaug1"], writes=[("vcaug", vs)])
            for g in range(4):
                P.pe(lambda e, cs=cs, g=g: e.transpose(
                    out=ptk[:, g, :], in_=ckd[:, cs, g, :, :].rearrange("p a d -> p (a d)"), identity=ident_b[:, :]),
                    reads=[("ckd", cs), "ident_b"], writes=[("ps", 7)])
            P.act(lambda e, cs=cs: e.copy(out=kcT[:, cs, :, :], in_=ptk[:, :, :]), reads=[("ps", 7)],
                  writes=[("kcT", cs)])
        def sA2(s_):
            cs = s_ % 2
            qs = slice(1024 + 8 * s_, 1032 + 8 * s_)
            for hh in range(2):
                hsl = slice(hh * 64, (hh + 1) * 64)
                stb = 2 + hh
                P.pe(lambda e, stb=stb: e.matmul(ps[stb][:, 0:64], lhsT=ident_b[:, :], rhs=mc2_b[:, :], start=True,
                                                 stop=False),
                     reads=["ident_b", "mc2"], writes=[("ps", stb)])
                for g in range(4):
                    P.pe(lambda e, stb=stb, hsl=hsl, g=g, cs=cs, qs=qs: e.matmul(
                        ps[stb][:, g * 16:(g + 1) * 16], lhsT=kcT[hsl, cs, g, :], rhs=qT[hsl, 2 * g:2 * g + 2, qs],
                        start=False, stop=(g == 3)),
                        reads=[("kcT", cs), ("q", 2 * g, 1024), ("q", 2 * g + 1, 1024)], writes=[("ps", stb)])
                P.act(lambda e, stb=stb, cs=cs, hh=hh: e.activation(out=etc_[:, cs, hh * 64:(hh + 1) * 64],
                                                                   in_=ps[stb][:, 0:64], func=AF.Exp, scale=0.125),
                      reads=[("ps", stb)], writes=[("etc", cs, hh)])
        def sB(s_):
            cs = s_ % 2
            for g in range(4):
                for j in range(2):
                    for hh in range(2):
                        h = 4 * g + 2 * j + hh
                        bk = h // 8
                        c_ = hh * 64 + g * 16 + j * 8
                        P.pe(lambda e, h=h, bk=bk, g=g, cs=cs, c_=c_, s_=s_: e.matmul(
                            oTb[bk][:65, (h % 8) * 64 + s_ * 8:(h % 8) * 64 + s_ * 8 + 8],
                            lhsT=vcaug[:, s_ % 3, g, 0:65], rhs=etc_[:, cs, c_:c_ + 8], start=False, stop=(s_ == 7),
                            skip_group_check=True),
                            reads=[("etc", cs, hh), ("vcaug", s_ % 3), "vcaug1"], writes=[("ps", bk)])
        for s_ in range(10):
            if s_ < 8:
                sA(s_)
            if 1 <= s_ < 9:
                sA2(s_ - 1)
            if s_ >= 2:
                sB(s_ - 2)
        for bk in range(2):
            P.act(lambda e, bk=bk: e.copy(out=oT_sb[:65, bk * 512:(bk + 1) * 512], in_=oTb[bk][:65, :]),
                  reads=[("ps", bk)], writes=[("oT_sb", bk)])
        for g in range(4):
            aob = 4 + g % 2
            ao = ps[aob][:64, :].rearrange("p (r c) -> p r c", c=128)
            for r in range(4):
                h = 4 * g + r
                P.pe(lambda e, ao=ao, r=r, h=h: e.transpose(out=ao[:, r, 0:65], in_=oT_sb[:65, h * 64:(h + 1) * 64],
                                                          identity=ident_f[:65, :65]),
                     reads=[("oT_sb", h // 8), "ident_f"], writes=[("ps", aob)])
            normalize(g, 8, 64, aob, g % 2)
        if hf == 0:
            P.dve(lambda e: e.tensor_copy(out=kk_carry[:, :, :], in_=kkT[:, :, 896:1024]),
                  reads=[("kk", g, 512) for g in range(4)], writes=["kk_carry"])
            P.dve(lambda e: e.tensor_copy(out=v_carry[:, :, :], in_=vaug[:, 7, :, :]), reads=[("vaug", 7), "vaug1"],
                  writes=["v_carry"])
        return hT, mg, mark

    def mix2(hf, hT, mg, mark):
        P.barrier()
        P.phase = 'ssmA_' + str(hf)
        A.off = mark
        uT = A.take([128, 6, 8, NB], BF16)
        Hb = A.take([128, 2, 16, NB], BF16)
        m5 = A.off
        Z = A.take([128, 2, 16, NB], F32)
        wf2 = A.take([128, 3, 8, 128], BF16)
        svt = A.take([128, 2, 16, 8], F32)
        sso = A.take([128, 2, 512], F32)
        ssp = A.take([128, 2, 128], F32)
        tw = A.take([128, 4, 16, 16], F32)
        Cb = A.take([128, 2, 16, 16], F32)
        Ea = A.take([128, 2, 16, 16], F32)
        Eb = A.take([128, 2, 16, 16], F32)
        for o in range(6):
            nv = CH_ROWS[o]
            s = o % 3
            P.dma("pool", lambda e, s=s, o=o, nv=nv: e.dma_start(out=wf2[:, s, :, 0:nv],
                                                               in_=wslice(1536 + o * 96, nv)),
                  writes=[("wf2", s)])
            for (c0, n) in GROUPS:
                pb = pbank(0, 4)
                for dc in range(8):
                    P.pe(lambda e, s=s, dc=dc, c0=c0, n=n, pb=pb, nv=nv: e.matmul(
                        ps[pb][:nv, :n], lhsT=wf2[:, s, dc, 0:nv], rhs=hT[:, dc, c0:c0 + n], start=(dc == 0),
                        stop=(dc == 7)),
                        reads=[("wf2", s)] + [("hT", t) for t in tiles_in(c0, n)], writes=[("ps", pb)])
                k0, nk = c0 // 8, n // 8
                if cnt["pb"] % 2:
                    P.act(lambda e, o=o, n=n, pb=pb, nv=nv, k0=k0, nk=nk: e.copy(
                        out=uT[:nv, o, :, k0:k0 + nk].rearrange("p j k -> p k j"),
                        in_=ps[pb][:nv, :n].rearrange("p (k j) -> p k j", j=8)),
                        reads=[("ps", pb)], writes=[("uT", o)])
                else:
                    P.dve(lambda e, o=o, n=n, pb=pb, nv=nv, k0=k0, nk=nk: e.tensor_copy(
                        out=uT[:nv, o, :, k0:k0 + nk].rearrange("p j k -> p k j"),
                        in_=ps[pb][:nv, :n].rearrange("p (k j) -> p k j", j=8)),
                        reads=[("ps", pb)], writes=[("uT", o)])
        for p in range(16):
            o, i3 = p // 3, p % 3
            rsl = slice(i3 * 32, (i3 + 1) * 32)
            for ri in range(2):
                pb = pbank(4, 8)
                for i in range(8):
                    P.pe(lambda e, o=o, rsl=rsl, ri=ri, i=i, pb=pb: e.matmul(
                        ps[pb][:, 0:NB], lhsT=RB[rsl, o, i, ri, :], rhs=uT[rsl, o, i, :], start=(i == 0),
                        stop=(i == 7)),
                        reads=["RB", ("uT", o)], writes=[("ps", pb)])
                P.act(lambda e, p=p, ri=ri, pb=pb: e.copy(out=Z[:, ri, p, :], in_=ps[pb][:, 0:NB]),
                      reads=[("ps", pb)], writes=["Z"])
        if hf == 0:
            P.dve(lambda e: e.memset(Hc[:], 0.0), writes=["Hc"])
        Zv = [Z[:, ri, :, 0:128].rearrange("p a (s j) -> p a s j", j=8) for ri in range(2)]
        bc = lambda ap: ap.unsqueeze(2).to_broadcast([128, 16, 16])

        pA8p = ps[0][:, 0:288].rearrange("p (a b c) -> p a b c", a=9, b=2)
        pA8 = ps[0][:, 288:320].rearrange("p (b c) -> p b c", b=2)
        pA64 = ps[0][:, 320:448].rearrange("p (a b c) -> p a b c", a=4, b=2)
        P.dve(lambda e: e.tensor_copy(out=pA8p[:, 1:9], in_=A8p[:, 1:9, :, :]), reads=["A8p"], writes=[("ps", 0)])
        P.dve(lambda e: e.tensor_copy(out=pA8, in_=A8[:, :, :]), reads=["A8"], writes=[("ps", 0)])
        P.dve(lambda e: e.tensor_copy(out=pA64, in_=A64[:, :, :, :]), reads=["A64"], writes=[("ps", 0)])
        pT = ps[1][:, :].rearrange("p (t a b) -> p t a b", t=2, a=16)
        first = {"x": [("ps", 0), ("ps", 1)]}

        def cmul_acc(dr, di, ar_, ai_, xr, xi, keys_r, keys_w, T_):
            t0, t1, t2, t3 = T_(0), T_(1), T_(2), T_(3)
            fx = first["x"]
            first["x"] = []
            P.dve(lambda e: tt_(e, t0, ar_, xr, TT), keys_r, ["tw0"] + fx)
            P.dve(lambda e: tt_(e, t1, ai_, xi, TT), keys_r, ["tw1"])
            P.dve(lambda e: tt_(e, t2, ar_, xi, TT), keys_r, ["tw2"])
            P.dve(lambda e: tt_(e, t3, ai_, xr, TT), keys_r, ["tw3"])
            P.dve(lambda e: tt_(e, t0, t0, t1, ALU.subtract), ["tw0", "tw1"], ["tw0"])
            P.dve(lambda e: tt_(e, t2, t2, t3, ALU.add), ["tw2", "tw3"], ["tw2"])
            P.dve(lambda e: tt_(e, dr, dr, t0, ALU.add), ["tw0"] + keys_w, keys_w)
            P.dve(lambda e: tt_(e, di, di, t2, ALU.add), ["tw2"] + keys_w, keys_w)

        def Tw(n_):
            return lambda i: (pT[:, i // 2, :, 0:n_] if i % 2 == 0 else tw[:, i, :, 0:n_])

        T_ = Tw(16)
        for j in range(1, 8):
            cmul_acc(Zv[0][:, :, :, j], Zv[1][:, :, :, j], bc(pA8[:, 0, :]), bc(pA8[:, 1, :]),
                     Zv[0][:, :, :, j - 1], Zv[1][:, :, :, j - 1], ["Z"], ["Z"], T_)
        P.dve(lambda e: e.tensor_copy(out=Ea[:, :, :, :], in_=Z[:, :, :, 7:128:8]), reads=["Z"], writes=["Ea"])
        if hf == 1:
            T1 = lambda i: (pT[:, i // 2, :, 0] if i % 2 == 0 else tw[:, i, :, 0])
            cmul_acc(Ea[:, 0, :, 0], Ea[:, 1, :, 0], pA64[:, 0, 0, :], pA64[:, 0, 1, :], Hc[:, 0, :], Hc[:, 1, :],
                     ["Hc", "Ea"], ["Ea"], T1)
        bufs = [(Ea, "Ea"), (Eb, "Eb")]
        for k, d in enumerate((1, 2, 4, 8)):
            (s_, sk), (d_, dk) = bufs[k % 2], bufs[(k + 1) % 2]
            n_ = 16 - d
            P.dve(lambda e, s_=s_, d_=d_: e.tensor_copy(out=d_[:, :, :, :], in_=s_[:, :, :, :]), reads=[sk], writes=[dk])
            bcn = lambda ap, n_=n_: ap.unsqueeze(2).to_broadcast([128, 16, n_])
            cmul_acc(d_[:, 0, :, d:16], d_[:, 1, :, d:16], bcn(pA64[:, k, 0, :]), bcn(pA64[:, k, 1, :]),
                     s_[:, 0, :, 0:n_], s_[:, 1, :, 0:n_], [sk, dk], [dk], Tw(n_))
        P.dve(lambda e: e.tensor_copy(out=Cb[:, :, :, 0], in_=Hc[:, :, :]), reads=["Hc"], writes=["Cb"])
        P.dve(lambda e: e.tensor_copy(out=Cb[:, :, :, 1:16], in_=Ea[:, :, :, 0:15]), reads=["Ea", "Cb"], writes=["Cb"])
        T_ = Tw(16)
        for j in range(8):
            cmul_acc(Zv[0][:, :, :, j], Zv[1][:, :, :, j], bc(pA8p[:, j + 1, 0, :]), bc(pA8p[:, j + 1, 1, :]),
                     Cb[:, 0, :, 0:16], Cb[:, 1, :, 0:16], ["Cb", "Z"], ["Z"], T_)
        P.dve(lambda e: e.tensor_copy(out=Hb[:, :, :, 0], in_=Hc[:, :, :]), reads=["Hc"], writes=["Hb"])
        P.dve(lambda e: e.tensor_copy(out=Hb[:, :, :, 1:128], in_=Z[:, :, :, 0:127]), reads=["Z"], writes=["Hb"])
        P.dve(lambda e: e.tensor_copy(out=Hb[:, :, :, 128:136], in_=h0T[:, :, :, hf * 8:(hf + 1) * 8]),
              reads=["h0T"], writes=["Hb"])
        P.dve(lambda e: e.tensor_copy(out=Hc[:, :, :], in_=Z[:, :, :, 127]), reads=["Z", "Hb"], writes=["Hc"])
        arb = A8[:, 0, :].unsqueeze(2).to_broadcast([128, 16, 8])
        aib = A8[:, 1, :].unsqueeze(2).to_broadcast([128, 16, 8])
        h0r_, h0i_ = h0T[:, 0, :, hf * 8:(hf + 1) * 8], h0T[:, 1, :, hf * 8:(hf + 1) * 8]
        zs_r, zs_i = Z[:, 0, :, 128:136], Z[:, 1, :, 128:136]
        for (dst, x1, x2, op) in ((zs_r, h0r_, h0i_, ALU.subtract), (zs_i, h0i_, h0r_, ALU.add)):
            P.dve(lambda e, x1=x1: tt_(e, svt[:, 0], x1, arb, TT), ["h0T", "A8"], ["svt"])
            P.dve(lambda e, x2=x2: tt_(e, svt[:, 1], x2, aib, TT), ["h0T", "A8"], ["svt"])
            P.dve(lambda e, op=op: tt_(e, svt[:, 0], svt[:, 0], svt[:, 1], op), ["svt"], ["svt"])
            P.dve(lambda e, dst=dst: tt_(e, dst, dst, svt[:, 0], ALU.add), ["svt", "Z", "Hb"], ["Z"])
        for ri, dd in enumerate((srs, sis)):
            for q4 in range(4):
                pb = pbank(4, 7)
                so = q4 % 2
                for pp in range(4):
                    p = q4 * 4 + pp
                    P.pe(lambda e, ri=ri, p=p, pp=pp, pb=pb: e.transpose(
                        out=ps[pb][:8, pp * 128:(pp + 1) * 128], in_=Z[:, ri, p, 128:136], identity=ident_f[:, :]),
                        reads=["Z", "ident_f"], writes=[("ps", pb)])
                P.act(lambda e, pb=pb, so=so: e.copy(out=sso[:8, so, :], in_=ps[pb][:8, :]),
                      reads=[("ps", pb)], writes=[("sso", so)])
                P.dma("sp", lambda e, dd=dd, q4=q4, so=so: e.dma_start(
                    out=dd[hf * 8:(hf + 1) * 8, q4 * 512:(q4 + 1) * 512], in_=sso[:8, so, :]),
                    reads=[("sso", so)], writes=[("srs", ri, q4, hf)])
        if hf == 1:
            for ri, dd in enumerate((srp, sip)):
                pb = pbank(4, 7)
                P.pe(lambda e, ri=ri, pb=pb: e.transpose(out=ps[pb][:16, 0:128], in_=Hc[:, ri, :],
                                                         identity=ident_f[:, :]),
                     reads=["Hc", "ident_f"], writes=[("ps", pb)])
                P.act(lambda e, ri=ri, pb=pb: e.copy(out=ssp[:16, ri, :], in_=ps[pb][:16, 0:128]),
                      reads=[("ps", pb)], writes=[("ssp", ri)])
                P.dma("sp", lambda e, ri=ri, dd=dd: e.dma_start(out=dd[:, :], in_=ssp[:16, ri, :]),
                      reads=[("ssp", ri)], writes=[("srp", ri)])
        return uT, Hb, m5

    def mix3(hf, hT, mg, uT, Hb, m5):
        P.barrier()
        P.phase = 'ssmB_' + str(hf)
        A.off = m5
        yT = A.take([128, 6, NT], BF16)
        glw = A.take([128, 2, 6, 512], BF16)
        wgs = A.take([128, 8, 512], BF16)
        ytmp = A.take([128, 2, 3, NB], F32)
        sbt = A.take([128, 2, 2, 512], BF16)
        t1 = A.take([128, 2, 512], F32)
        ycnt = 0
        for o in range(6):
            nv = CH_ROWS[o]
            npairs = nv // 32
            yv = yT[:nv, o, :].rearrange("p (k j) -> p j k", j=8)
            for j in range(8):
                bank, off = 4 + j // 3, (j % 3) * NB
                reg = ps[bank][:, off:off + NB]
                for i3 in range(npairs):
                    p = 3 * o + i3
                    rsl = slice(i3 * 32, (i3 + 1) * 32)
                    for ri, tab in enumerate((CAr_b, nCAi_b)):
                        P.pe(lambda e, reg=reg, rsl=rsl, tab=tab, j=j, p=p, ri=ri: e.matmul(
                            reg[rsl, :], lhsT=tab[:, j + 1, p, :], rhs=Hb[:, ri, p, :], start=(ri == 0),
                            stop=False),
                            reads=["CAb", "Hb"], writes=[("ps", bank)])
                for i in range(j + 1):
                    P.pe(lambda e, reg=reg, nv=nv, o=o, i=i, j=j: e.matmul(
                        reg[:nv, :], lhsT=KBD[:nv, o, j - i, :nv], rhs=uT[:nv, o, i, :], start=False,
                        stop=(i == j)),
                        reads=["KBD", ("uT", o)], writes=[("ps", bank)])
            for bq, (j0, j1) in enumerate(((0, 3), (3, 6), (6, 8))):
                ys_ = ycnt % 2
                ycnt += 1
                L = j1 - j0
                pv = ps[4 + bq][:nv, 0:L * NB].rearrange("p (j k) -> p j k", k=NB)
                P.dve(lambda e, j0=j0, j1=j1, nv=nv, o=o, pv=pv, ys_=ys_, L=L: e.scalar_tensor_tensor(
                    out=ytmp[:nv, ys_, 0:L, :], in0=uT[:nv, o, j0:j1, :], scalar=Dsk[:nv, o:o + 1], in1=pv,
                    op0=ALU.mult, op1=ALU.add),
                    reads=[("uT", o), "Dsk", ("ps", 4 + bq)], writes=[("ytmp", ys_)])
                P.act(lambda e, yv=yv, j0=j0, j1=j1, nv=nv, ys_=ys_, L=L: e.activation(
                    out=yv[:, j0:j1, :], in_=ytmp[:nv, ys_, 0:L, :], func=AF.Gelu_apprx_tanh),
                    reads=[("ytmp", ys_)], writes=[("yT", o)])
        P.phase = 'glu_' + str(hf)
        for ch in range(2):
            csl = slice(ch * 512, (ch + 1) * 512)
            for ab, src in enumerate((glu_a_d, glu_b_d)):
                for o in range(6):
                    nv = CH_ROWS[o]
                    P.dma("pool", lambda e, ab=ab, src=src, csl=csl, o=o, nv=nv: e.dma_start(
                        out=glw[:nv, ab, o, :], in_=src[o * 96:o * 96 + nv, csl]), writes=[("glw", ab, o)])
            for dh in range(2):
                P.dma("pool", lambda e, ch=ch, dh=dh: e.dma_start(
                    out=wgs[:, dh * 4:(dh + 1) * 4, :],
                    in_=w_in[dh * 512:(dh + 1) * 512, 3072 + ch * 512:3072 + (ch + 1) * 512].rearrange(
                        "(c p) f -> p c f", p=128)), writes=[("wgs", dh)])
            for tt, (c0, rows) in enumerate(TILES):
                set_ = tt % 2
                pa, pb_, pg = 3 * set_, 3 * set_ + 1, 3 * set_ + 2
                for ab, bank in ((0, pa), (1, pb_)):
                    for o in range(6):
                        nv = CH_ROWS[o]
                        P.pe(lambda e, ab=ab, bank=bank, o=o, nv=nv, c0=c0, rows=rows: e.matmul(
                            ps[bank][:rows, :], lhsT=yT[:nv, o, c0:c0 + rows], rhs=glw[:nv, ab, o, :],
                            start=(o == 0), stop=(o == 5)),
                            reads=[("yT", o), ("glw", ab, o)], writes=[("ps", bank)])
                for dc in range(8):
                    P.pe(lambda e, dc=dc, c0=c0, rows=rows, pg=pg: e.matmul(
                        ps[pg][:rows, :], lhsT=hT[:, dc, c0:c0 + rows], rhs=wgs[:, dc, :], start=(dc == 0),
                        stop=(dc == 7)),
                        reads=[("hT", tt), ("wgs", dc // 4)], writes=[("ps", pg)])
                P.act(lambda e, rows=rows, set_=set_, pb_=pb_: e.activation(out=sbt[:rows, set_, 0, :],
                                                                         in_=ps[pb_][:rows, :], func=AF.Sigmoid),
                      reads=[("ps", pb_)], writes=[("sbt", set_, 0)])
                P.act(lambda e, rows=rows, set_=set_, pg=pg: e.activation(out=sbt[:rows, set_, 1, :],
                                                                       in_=ps[pg][:rows, :], func=AF.Sigmoid),
                      reads=[("ps", pg)], writes=[("sbt", set_, 1)])
                P.dve(lambda e, rows=rows, set_=set_, pa=pa: e.tensor_tensor(
                    out=t1[:rows, set_, :], in0=ps[pa][:rows, :], in1=sbt[:rows, set_, 0, :], op=ALU.mult),
                    reads=[("ps", pa), ("sbt", set_, 0)], writes=[("t1", set_)])
                P.dve(lambda e, rows=rows, set_=set_: e.tensor_tensor(
                    out=t1[:rows, set_, :], in0=t1[:rows, set_, :], in1=sbt[:rows, set_, 1, :], op=ALU.mult),
                    reads=[("t1", set_), ("sbt", set_, 1)], writes=[("t1", set_)])
                P.dve(lambda e, rows=rows, set_=set_, tt=tt, csl=csl: e.tensor_tensor(
                    out=mg[:rows, tt, csl], in0=mg[:rows, tt, csl], in1=t1[:rows, set_, :], op=ALU.add),
                    reads=[("t1", set_)] + [("mg", tt, j) for j in range(4)], writes=[("mg", tt, j) for j in range(4)])
        P.barrier()
        P.phase = 'wout_' + str(hf)
        A.off = m5
        wo = A.take([128, 2, 8, 512], BF16)
        ptv = ps[7][:, 0:512].bitcast(BF16).rearrange("p (c t) -> p c t", c=8)
        for hh in range(2):
            P.dma("pool", lambda e, hh=hh: e.dma_start(
                out=wo[:, hh, :, :], in_=w_out_d[:, hh * 512:(hh + 1) * 512].rearrange("(c p) f -> p c f", p=128)),
                writes=[("wo", hh)])
        def woA(tt):
            c0, rows = TILES[tt]
            pv_ = ps[6 + tt % 2][:, 0:512].bitcast(BF16).rearrange("p (c t) -> p c t", c=8)
            for dc in range(8):
                P.pe(lambda e, dc=dc, rows=rows, tt=tt, pv_=pv_: e.transpose(
                    out=pv_[:, dc, :rows], in_=mg[:rows, tt, dc * 128:(dc + 1) * 128], identity=ident_b[:rows, :rows]),
                    reads=[("mg", tt, j) for j in range(4)] + ["ident_b"], writes=[("ps", 6 + tt % 2)])
            P.act(lambda e, c0=c0, rows=rows, pv_=pv_: e.copy(out=hT[:, :, c0:c0 + rows], in_=pv_[:, :, :rows]),
                  reads=[("ps", 6 + tt % 2)], writes=[("hT", tt)])

        def woB(tt):
            c0, rows = TILES[tt]
            for hh in range(2):
                pb = (2 * tt + hh) % 4
                for dc in range(8):
                    P.pe(lambda e, dc=dc, c0=c0, rows=rows, hh=hh, pb=pb: e.matmul(
                        ps[pb][:rows, :], lhsT=hT[:, dc, c0:c0 + rows], rhs=wo[:, hh, dc, :], start=(dc == 0),
                        stop=(dc == 7)),
                        reads=[("hT", tt), ("wo", hh)], writes=[("ps", pb)])
                P.dve(lambda e, tt=tt, rows=rows, hh=hh, pb=pb: e.scalar_tensor_tensor(
                    out=x_tm[:rows, tt, hh * 512:(hh + 1) * 512], in0=ps[pb][:rows, :], scalar=1.0,
                    in1=x_tm[:rows, tt, hh * 512:(hh + 1) * 512], op0=ALU.mult, op1=ALU.add),
                    reads=[("ps", pb), ("x", tt)], writes=[("x", tt)])

        fT = ffn_takes()
        for tt in range(len(TILES) + 1):
            if tt < len(TILES):
                woA(tt)
            if tt >= 1:
                woB(tt - 1)
                norm_tile(2, tt - 1, fT[0], fT[4], fT[5])

    P.barrier()
    for hf in range(2):
        P.hf = hf
        if hf == 1:
            load_x(1)
        if STOP <= 0:
            break
        mt = mix_takes()
        ffn("a", 0, barrier=(hf == 0),
            after_tile=lambda tt, *_, mt=mt: norm_tile(1, tt, mt[0], mt[3], mt[5]))
        if STOP <= 1:
            break
        hT, mg, mark = mix(hf)
        if STOP <= 2:
            break
        uT, Hb, m5 = mix2(hf, hT, mg, mark)
        if STOP <= 3:
            break
        mix3(hf, hT, mg, uT, Hb, m5)
        if STOP <= 4:
            break
        def fin_tile(tt, yout, junk, hf=hf):
            c0, rows = TILES[tt]
            sl = rmsnorm_stats(tt, rows, junk)
            ys_ = tt % 2
            P.dve(lambda e: e.scalar_tensor_tensor(
                out=yout[:rows, ys_, :], in0=x_tm[:rows, tt, :], scalar=stat[:rows, sl, 3:4],
                in1=gfin[:rows, :], op0=ALU.mult, op1=ALU.mult),
                reads=[("x", tt), ("stat", sl), "gfin"], writes=[("yout", ys_)])
            dst = yp[hf * 1024 + c0: hf * 1024 + c0 + rows, :] if tt < 8 else ys[hf * 64:(hf + 1) * 64, :]
            P.dma("sp", lambda e: e.dma_start(out=dst, in_=yout[:rows, ys_, :]),
                  reads=[("yout", ys_)], writes=[("yd", hf, tt)])

        ffn("b", 2, after_tile=fin_tile, prenormed=True)

    with nc.Block() as block:
        P.emit(block, sems, dsems)
    nc._pe_phase = P.pe_phase
    return nc


def _masks():
    NEG = -30000.0
    k = np.arange(128)[:, None]
    q = np.arange(128)[None, :]
    cur = np.where(k <= q, 0.0, NEG).astype(np.float32)
    prev = np.where(k >= q, 0.0, NEG).astype(np.float32)
    mcur = np.concatenate([cur, cur], 1)
    mprev = np.concatenate([prev, prev], 1)
    k6 = np.arange(64)[:, None]
    q6 = np.arange(64)[None, :]
    new = np.where((k6 // 8 == q6 // 8) & (k6 % 8 <= q6 % 8), 0.0, NEG).astype(np.float32)
    mnew = np.concatenate([new, new], 1)
    mc = np.zeros((128, 8, 128), np.float32)
    for s in range(8):
        one = np.where((q6 // 8 == s) & (k >= q6 % 8), 0.0, NEG).astype(np.float32)
        mc[:, s, :] = np.concatenate([one, one], 1)
    return mcur, mprev, mnew, mc


_NC = None


def kernel(**inputs):
    global _NC
    if _NC is None:
        _NC = build_nc()
    nc = _NC
    f = lambda a: np.ascontiguousarray(np.asarray(a, dtype=np.float32))
    x_prompt = f(inputs["x_prompt"])
    x_sample = f(inputs["x_sample"])
    cache_k = f(inputs["cache_k_win"])[0]
    cache_v = f(inputs["cache_v_win"])[0]
    st_re = f(inputs["state_ssm_re"])[0]
    st_im = f(inputs["state_ssm_im"])[0]
    mcur, mprev, mnew, mc = _masks()
    m01 = np.stack([(mprev == 0).astype(np.float32), (mcur == 0).astype(np.float32)], 1)
    shared = {"ident": np.eye(128, dtype=np.float32), "mcur": mcur, "mprev": mprev, "mnew": mnew, "mc": mc,
              "m01": np.ascontiguousarray(m01),
              "mc2": np.where(np.arange(128)[:, None] >= (np.arange(64)[None, :] % 8), 0.0, -30000.0).astype(np.float32)}
    for k in ("ffn_a_norm", "mix_norm", "ffn_b_norm", "final_norm"):
        shared[k] = f(inputs[k]).reshape(D)
    for k in ("ffn_a_gate", "ffn_a_up", "ffn_b_gate", "ffn_b_up"):
        shared[k] = f(inputs[k]).reshape(D, DFF)
    for k in ("ffn_a_down", "ffn_b_down"):
        shared[k] = f(inputs[k]).reshape(DFF, D)
    shared["w_in"] = f(inputs["w_in"]).reshape(D, 4096)
    shared["attn_sinks"] = f(inputs["attn_sinks"]).reshape(16)
    shared["ssm_lambda_re"] = f(inputs["ssm_lambda_re"]).reshape(32, 64)
    shared["ssm_lambda_im"] = f(inputs["ssm_lambda_im"]).reshape(32, 64)
    shared["ssm_log_dt"] = f(inputs["ssm_log_dt"]).reshape(32)
    shared["ssm_b_re"] = f(inputs["ssm_b_re"]).reshape(32, 64, 16)
    shared["ssm_b_im"] = f(inputs["ssm_b_im"]).reshape(32, 64, 16)
    shared["ssm_c_re"] = f(inputs["ssm_c_re"]).reshape(32, 16, 64)
    shared["ssm_c_im"] = f(inputs["ssm_c_im"]).reshape(32, 16, 64)
    shared["ssm_d"] = f(inputs["ssm_d"]).reshape(512)
    shared["glu_a"] = f(inputs["glu_a"]).reshape(512, D)
    shared["glu_b"] = f(inputs["glu_b"]).reshape(512, D)
    shared["w_out"] = f(inputs["w_out"]).reshape(D, D)
    in_maps = []
    for c in range(8):
        m = dict(shared)
        sl = slice(16 * c, 16 * (c + 1))
        m["xp"] = x_prompt[c]
        m["xs"] = x_sample[sl].reshape(128, D)
        m["ck"] = cache_k[sl].reshape(16, 128, 256)
        m["cv"] = cache_v[sl].reshape(16, 128, 256)
        m["h0r"] = st_re[sl].reshape(16, 2048)
        m["h0i"] = st_im[sl].reshape(16, 2048)
        in_maps.append(m)
    res = run_bass_kernel_spmd(nc, in_maps, core_ids=list(range(8)))
    r = res.results
    global LAST
    LAST = r
    cat = lambda k, shp: np.concatenate([r[c][k].reshape(shp) for c in range(8)], 0)[None]
    y_prompt = np.stack([r[c]["yp"] for c in range(8)], 0)
    y_sample = np.concatenate([r[c]["ys"].reshape(16, 8, D) for c in range(8)], 0)
    return (y_prompt, y_sample,
            cat("kwp", (1, 128, 4, 64)), cat("vwp", (1, 128, 4, 64)),
            cat("kws", (16, 128, 4, 64)), cat("vws", (16, 128, 4, 64)),
            cat("srp", (1, 32, 64)), cat("sip", (1, 32, 64)),
            cat("srs", (16, 32, 64)), cat("sis", (16, 32, 64)))
```

```python
import numpy as np
import concourse.bass as bass
import concourse.mybir as mybir
from concourse.bass_utils import run_bass_kernel_spmd

F32 = mybir.dt.float32
BF16 = mybir.dt.bfloat16
AF = mybir.ActivationFunctionType
ALU = mybir.AluOpType

ENGS = ("pe", "act", "dve", "pool", "sp")
N_DMA_SEMS = 96

D = 1024
DFF = 2816
NFC = DFF // 128
NT = 1088
TILES = [(i * 128, 128) for i in range(8)] + [(1024, 64)]
GROUPS = [(0, 512), (512, 512), (1024, 64)]
EPS = 1e-6


class Op:
    __slots__ = ("id", "eng", "fn", "deps", "dma", "sig", "sem", "val")

    def __init__(self, id, eng, fn, dma):
        self.id = id
        self.eng = eng
        self.fn = fn
        self.deps = set()
        self.dma = dma
        self.sig = False
        self.sem = None
        self.val = None


class Prog:
    def __init__(self, nc):
        self.nc = nc
        self.ops = []
        self.q = {e: [] for e in ENGS}
        self.lastw = {}
        self.readers = {}
        self.dma_last = [None] * N_DMA_SEMS
        self.dma_cnt = [0] * N_DMA_SEMS
        self.dma_rr = 0
        self.dma_rr2 = {True: 0, False: 0}
        self.bar = []
        self.phase = 'init'
        self.pe_phase = []

    def _add(self, eng, fn, reads, writes, dma=False):
        op = Op(len(self.ops), eng, fn, dma)
        op.deps.update(self.bar)
        isps = lambda k: isinstance(k, tuple) and k[0] == "ps"
        writes = list(writes) + [k for k in reads if isps(k)]
        reads = [k for k in reads if not isps(k)]
        for k in reads:
            w = self.lastw.get(k)
            if w is not None:
                op.deps.add(w)
        for k in writes:
            w = self.lastw.get(k)
            if w is not None:
                op.deps.add(w)
            for r in self.readers.get(k, ()):
                op.deps.add(r)
        for k in reads:
            self.readers.setdefault(k, []).append(op.id)
        for k in writes:
            self.lastw[k] = op.id
            self.readers[k] = []
        if dma:
            half = N_DMA_SEMS // 2
            base = 0 if eng == "pool" else half
            r = self.dma_rr2[eng == "pool"]
            self.dma_rr2[eng == "pool"] = (r + 1) % half
            s = base + r
            if self.dma_last[s] is not None:
                op.deps.add(self.dma_last[s])
            self.dma_last[s] = op.id
            self.dma_cnt[s] += 1
            op.sem = ("dma", s)
            op.val = 16 * self.dma_cnt[s]
            op.sig = True
        self.ops.append(op)
        self.q[eng].append(op)
        return op

    def barrier(self):
        b = []
        for e in ENGS:
            if self.q[e]:
                b.append(self.q[e][-1].id)
        for d in self.dma_last:
            if d is not None:
                b.append(d)
        self.bar = b

    def pe(self, fn, reads=(), writes=()):
        self.pe_phase.append(self.phase)
        return self._add("pe", fn, reads, writes)

    def act(self, fn, reads=(), writes=()):
        return self._add("act", fn, reads, writes)

    def dve(self, fn, reads=(), writes=()):
        return self._add("dve", fn, reads, writes)

    def pool(self, fn, reads=(), writes=()):
        return self._add("pool", fn, reads, writes)

    def dma(self, eng, fn, reads=(), writes=()):
        return self._add(eng, fn, reads, writes, dma=True)

    def emit(self, block, sems, dma_sems):
        ops = self.ops

        import os
        nosame = int(os.environ.get("KNOSAME", "0"))

        def skip(p, op):
            if p.dma or op.dma or p.eng != op.eng:
                return False
            return p.eng == "pe" or nosame

        for op in ops:
            for d in op.deps:
                p = ops[d]
                if p.dma or skip(p, op):
                    continue
                p.sig = True
        cnt = {e: 0 for e in ENGS}
        for e in ENGS:
            for op in self.q[e]:
                if not op.dma and op.sig:
                    cnt[e] += 1
                    op.sem = ("eng", e)
                    op.val = cnt[e]

        def semh(s):
            return sems[s[1]] if s[0] == "eng" else dma_sems[s[1]]

        def run_queue(e, eng):
            waited = {}
            for op in self.q[e]:
                need = {}
                for d in op.deps:
                    p = ops[d]
                    if skip(p, op) or p.sem is None:
                        continue
                    if need.get(p.sem, 0) < p.val:
                        need[p.sem] = p.val
                for s, v in need.items():
                    if waited.get(s, 0) >= v:
                        continue
                    eng.wait_ge(semh(s), v)
                    waited[s] = v
                inst = op.fn(eng)
                if op.sig:
                    inst.then_inc(semh(op.sem), 16 if op.dma else 1)
            for op in self.q[e]:
                if op.dma and waited.get(op.sem, 0) < op.val:
                    eng.wait_ge(semh(op.sem), op.val)
                    waited[op.sem] = op.val

        if self.q["pe"]:
            @block.tensor
            def _(eng):
                run_queue("pe", eng)
        if self.q["act"]:
            @block.scalar
            def _(eng):
                run_queue("act", eng)
        if self.q["dve"]:
            @block.vector
            def _(eng):
                run_queue("dve", eng)
        if self.q["pool"]:
            @block.gpsimd
            def _(eng):
                run_queue("pool", eng)
        if self.q["sp"]:
            @block.sync
            def _(eng):
                run_queue("sp", eng)


NB = 136
CH_ROWS = [96, 96, 96, 96, 96, 32]
PI = float(np.pi)


class Arena:
    def __init__(self, t, n):
        self.t = t
        self.n = n
        self.off = 0

    def take(self, shape, dt):
        elems = 1
        for s in shape[1:]:
            elems *= s
        size = elems * (2 if dt == F32 else 1)
        off = (self.off + 31) // 32 * 32
        assert off + size <= self.n, ("arena overflow", off + size, self.n)
        self.off = off + size
        v = self.t[:, off:off + size]
        if dt == F32:
            v = v.bitcast(F32)
        nd = len(shape) - 1
        if nd > 1:
            names = ["a", "b", "c", "d"][:nd]
            kw = {names[i]: shape[1 + i] for i in range(nd)}
            v = v.rearrange("p (" + " ".join(names) + ") -> p " + " ".join(names), **kw)
        return v


def build_nc(debug=False):
    import os
    STOP = int(os.environ.get('KSTOP', '99'))
    NOTAB = int(os.environ.get('KNOTAB', '0'))
    KSUB = int(os.environ.get('KSUB', '99'))
    nc = bass.Bass("TRN2", target_bir_lowering=False)

    def din(name, shape):
        return nc.dram_tensor(name, list(shape), F32, kind="ExternalInput").ap()

    def dout(name, shape):
        return nc.dram_tensor(name, list(shape), F32, kind="ExternalOutput").ap()

    xp = din("xp", [2048, D])
    xs = din("xs", [128, D])
    ck = din("ck", [16, 128, 256])
    cv = din("cv", [16, 128, 256])
    h0r_d = din("h0r", [16, 2048])
    h0i_d = din("h0i", [16, 2048])
    ident_d = din("ident", [128, 128])
    mcur_d = din("mcur", [128, 256])
    mprev_d = din("mprev", [128, 256])
    mnew_d = din("mnew", [64, 128])
    mc_d = din("mc", [128, 8, 128])
    m01_d = din("m01", [128, 2, 256])
    mc2_d = din("mc2", [128, 64])
    gains = {k: din(k, [D]) for k in ("ffn_a_norm", "mix_norm", "ffn_b_norm", "final_norm")}
    Wg = {"a": din("ffn_a_gate", [D, DFF]), "b": din("ffn_b_gate", [D, DFF])}
    Wu = {"a": din("ffn_a_up", [D, DFF]), "b": din("ffn_b_up", [D, DFF])}
    Wd = {"a": din("ffn_a_down", [DFF, D]), "b": din("ffn_b_down", [DFF, D])}
    w_in = din("w_in", [D, 4096])
    sinks_d = din("attn_sinks", [16])
    lam_re_d = din("ssm_lambda_re", [32, 64])
    lam_im_d = din("ssm_lambda_im", [32, 64])
    log_dt_d = din("ssm_log_dt", [32])
    b_re_d = din("ssm_b_re", [32, 64, 16])
    b_im_d = din("ssm_b_im", [32, 64, 16])
    c_re_d = din("ssm_c_re", [32, 16, 64])
    c_im_d = din("ssm_c_im", [32, 16, 64])
    dsk_d = din("ssm_d", [512])
    glu_a_d = din("glu_a", [512, D])
    glu_b_d = din("glu_b", [512, D])
    w_out_d = din("w_out", [D, D])

    yp = dout("yp", [2048, D])
    ys = dout("ys", [128, D])
    kwp = dout("kwp", [128, 256])
    vwp = dout("vwp", [128, 256])
    kws = dout("kws", [16, 128, 256])
    vws = dout("vws", [16, 128, 256])
    srp = dout("srp", [16, 128])
    sip = dout("sip", [16, 128])
    srs = dout("srs", [16, 2048])
    sis = dout("sis", [16, 2048])

    def sb(name, shape, dt):
        return nc.alloc_sbuf_tensor(name, list(shape), dt)

    x_tm = sb("x_tm", [128, 9, D], F32)
    gfin = sb("gfin", [128, D], F32)
    gcol = sb("gcol", [128, 3, 8], F32)
    ident_f = sb("ident_f", [128, 128], F32)
    ident_b = sb("ident_b", [128, 128], BF16)
    mcur_b = sb("mcur_b", [128, 256], BF16)
    mprev_b = sb("mprev_b", [128, 256], BF16)
    mnew_b = sb("mnew_b", [128, 128], BF16)
    mc_b = sb("mc_b", [128, 8, 128], BF16)
    m01_b = sb("m01_b", [128, 2, 256], BF16)
    mc2_b = sb("mc2_b", [128, 64], BF16)
    esink = sb("esink", [128, 16], F32)
    stat = sb("stat", [128, 4, 4], F32)
    kk_carry = sb("kk_carry", [128, 4, 128], BF16)
    v_carry = sb("v_carry", [128, 4, 68], BF16)
    KBD = sb("KBD", [128, 6, 8, 128], BF16)
    RB = sb("RB", [128, 6, 8, 2, 128], BF16)
    CAr_b = sb("CAr_b", [128, 9, 16, 32], BF16)
    nCAi_b = sb("nCAi_b", [128, 9, 16, 32], BF16)
    A8 = sb("A8", [128, 2, 16], F32)
    A8p = sb("A8p", [128, 9, 2, 16], F32)
    A64 = sb("A64", [128, 4, 2, 16], F32)
    Dsk = sb("Dsk", [128, 6], F32)
    h0T = sb("h0T", [128, 2, 16, 16], F32)
    Hc = sb("Hc", [128, 2, 16], F32)
    rt = sb("rt", [128, 4, 16], F32)
    ps = [nc.alloc_psum_tensor("ps%d" % i, [128, 512], F32) for i in range(8)]
    rem = nc.sbuf_bytes_remaining
    an = (rem - 1024) // 2 // 32 * 32
    arena_t = sb("arena", [128, an], BF16)
    A = Arena(arena_t, an)

    sems = {e: nc.alloc_semaphore("s_" + e) for e in ENGS}
    dsems = [nc.alloc_semaphore("d%d" % i) for i in range(N_DMA_SEMS)]
    P = Prog(nc)
    TT = ALU.mult
    rot = {"n": 0}

    def tt_(e, out, in0, in1, op):
        return e.tensor_tensor(out=out, in0=in0, in1=in1, op=op)

    P.dma("sp", lambda e: e.dma_start(out=ident_f[:], in_=ident_d[:]), writes=["ident_f"])
    P.dma("pool", lambda e: e.dma_start(out=ident_b[:], in_=ident_d[:]), writes=["ident_b"])
    P.dma("pool", lambda e: e.dma_start(out=mcur_b[:], in_=mcur_d[:]), writes=["mcur"])
    P.dma("pool", lambda e: e.dma_start(out=mprev_b[:], in_=mprev_d[:]), writes=["mprev"])
    P.dve(lambda e: e.memset(mnew_b[:], 0.0), writes=["mnew"])
    P.dma("pool", lambda e: e.dma_start(out=mnew_b[:64, :], in_=mnew_d[:]), reads=["mnew"], writes=["mnew"])
    P.dma("pool", lambda e: e.dma_start(out=mc_b[:], in_=mc_d[:]), writes=["mc"])
    P.dma("pool", lambda e: e.dma_start(out=m01_b[:], in_=m01_d[:]), writes=["m01"])
    P.dma("pool", lambda e: e.dma_start(out=mc2_b[:], in_=mc2_d[:]), writes=["mc2"])
    P.dma("sp", lambda e: e.dma_start(out=gfin[:], in_=gains["final_norm"].partition_broadcast(128)), writes=["gfin"])
    for gi, k in enumerate(("ffn_a_norm", "mix_norm", "ffn_b_norm")):
        P.dma("sp", lambda e, gi=gi, k=k: e.dma_start(
            out=gcol[:, gi, :], in_=gains[k].rearrange("(c p) -> p c", p=128), allow_slow_non_contiguous=True),
            writes=["gcol"])
    P.dma("sp", lambda e: e.dma_start(out=esink[:], in_=sinks_d.partition_broadcast(128)), writes=["esink"])
    P.act(lambda e: e.activation(out=esink[:], in_=esink[:], func=AF.Exp), reads=["esink"], writes=["esink"])

    def build_tables():
        P.phase = 'tables'
        A.off = 0
        lam = A.take([128, 2, 16], F32)
        ldt = A.take([128, 16], F32)
        lt = A.take([128, 3, 128], F32)
        ld2 = A.take([128, 2], F32)
        bb = A.take([128, 2, 16, 16], F32)
        cc = A.take([128, 2, 16, 16], F32)
        w = A.take([128, 12, 16], F32)
        w3 = A.take([128, 4, 16, 16], F32)
        BBp = A.take([128, 2, 16, 32], F32)
        PW = A.take([128, 9, 2, 16], F32)
        h0s = A.take([128, 2, 2048], F32)
        ct = A.take([128, 2, 2, 128], F32)
        BBh = A.take([128, 2, 16, 32], BF16)
        BBl = A.take([128, 2, 16, 32], BF16)
        dtmp = A.take([128, 2, 16, 32], F32)
        X = A.take([128, 4, 9, 16, 16], F32)
        CAl = A.take([128, 2, 8, 16, 32], BF16)
        pwt = A.take([128, 4, 4, 16], F32)
        Y = h0s[:, :, :].rearrange("p a b -> p (a b)").rearrange("p (q m g c) -> p q m g c", q=4, m=2, g=16)
        P.dma("sp", lambda e: e.dma_start(out=lt[:16, 0, :], in_=lam_re_d.rearrange("(p t) n -> p (t n)", t=2)),
              writes=["lt0"])
        P.dma("sp", lambda e: e.dma_start(out=lt[:16, 1, :], in_=lam_im_d.rearrange("(p t) n -> p (t n)", t=2)),
              writes=["lt1"])
        P.dma("sp", lambda e: e.dma_start(out=ld2[:16, :], in_=log_dt_d.rearrange("(p t) -> p t", t=2)),
              writes=["ld2"])
        P.dve(lambda e: e.tensor_copy(out=lt[:16, 2, :].rearrange("p (t n) -> p t n", t=2),
                                      in_=ld2[:16, :].unsqueeze(2).to_broadcast([16, 2, 64])),
              reads=["ld2"], writes=["lt2"])
        for k in range(3):
            P.pe(lambda e, k=k: e.transpose(out=ps[4][:, k * 16:(k + 1) * 16], in_=lt[:16, k, :],
                                            identity=ident_f[:16, :16]),
                 reads=["lt%d" % k, "ident_f"], writes=[("ps", 4)])
        P.dve(lambda e: e.tensor_copy(out=lam[:, :, :], in_=ps[4][:, 0:32].rearrange("p (a b) -> p a b", a=2)),
              reads=[("ps", 4)], writes=["lam"])
        P.dve(lambda e: e.tensor_copy(out=ldt[:, :], in_=ps[4][:, 32:48]), reads=[("ps", 4)], writes=["ldt"])
        for g2 in range(2):
            psl = slice(g2 * 64, (g2 + 1) * 64)
            for ri, src in enumerate((b_re_d, b_im_d)):
                P.dma("sp", lambda e, psl=psl, ri=ri, src=src, g2=g2: e.dma_start(
                    out=bb[psl, ri, :, :], in_=src.rearrange("(p t) n c -> t n p c", t=2)[g2]), writes=["bb"])
        for ri, src in enumerate((c_re_d, c_im_d)):
            for t in range(2):
                for pl in range(8):
                    p = t * 8 + pl
                    P.dma("sp" if ri == 0 else "pool", lambda e, ri=ri, t=t, pl=pl, p=p, src=src: e.dma_start(
                        out=ct[pl * 16:(pl + 1) * 16, ri, t, :].rearrange("c (t2 n) -> c t2 n", t2=2),
                        in_=src[2 * p:2 * p + 2].rearrange("t2 c n -> c t2 n")), writes=[("ct", ri, t)])
                pb = 2 + (2 * ri + t) % 2
                P.pe(lambda e, ri=ri, t=t, pb=pb: e.transpose(out=ps[pb][:, 0:128], in_=ct[:, ri, t, :],
                                                            identity=ident_f[:, :]),
                     reads=[("ct", ri, t), "ident_f"], writes=[("ps", pb)])
                P.act(lambda e, ri=ri, t=t, pb=pb: e.copy(
                    out=cc[:, ri, t * 8:(t + 1) * 8, :], in_=ps[pb][:, 0:128].rearrange("p (a c) -> p a c", c=16)),
                    reads=[("ps", pb)], writes=[("cc", ri, t)])
        P.dve(lambda e: e.memset(Dsk[:], 0.0), writes=["Dsk"])
        P.dma("sp", lambda e: e.dma_start(out=Dsk[:96, 0:5], in_=dsk_d[0:480].rearrange("(o r) -> r o", r=96),
                                          allow_slow_non_contiguous=True), reads=["Dsk"], writes=["Dsk"])
        P.dma("sp", lambda e: e.dma_start(out=Dsk[:32, 5:6], in_=dsk_d[480:512].rearrange("(o r) -> r o", r=32),
                                          allow_slow_non_contiguous=True), reads=["Dsk"], writes=["Dsk"])
        P.dma("sp", lambda e: e.dma_start(out=h0s[:16, 0, :], in_=h0r_d[:]), writes=["h0s"])
        P.dma("sp", lambda e: e.dma_start(out=h0s[:16, 1, :], in_=h0i_d[:]), writes=["h0s"])
        for ri in range(2):
            pv = ps[ri][:, :].rearrange("p (a s) -> p a s", s=32)
            for p in range(16):
                P.pe(lambda e, ri=ri, p=p, pv=pv: e.transpose(out=pv[:, p, 0:16], in_=h0s[:16, ri, p * 128:(p + 1) * 128],
                                                           identity=ident_f[:16, :16]),
                     reads=["h0s", "ident_f"], writes=[("ps", ri)])
            P.dve(lambda e, ri=ri, pv=pv: e.tensor_copy(out=h0T[:, ri, :, :], in_=pv[:, :, 0:16]),
                  reads=[("ps", ri)], writes=["h0T"])

        def V(fn, reads, writes):
            P.dve(fn, reads=reads, writes=writes)

        W = lambda i: w[:, i, :]
        lr, li = lam[:, 0, :], lam[:, 1, :]
        P.act(lambda e: e.activation(out=W(0), in_=ldt[:], func=AF.Exp), reads=["ldt"], writes=["w"])
        V(lambda e: tt_(e, W(1), lr, W(0), TT), ["lam", "w"], ["w"])
        P.act(lambda e: e.activation(out=W(2), in_=W(1), func=AF.Exp), reads=["w"], writes=["w"])
        V(lambda e: tt_(e, W(3), li, W(0), TT), ["lam", "w"], ["w"])
        for _ in range(4):
            V(lambda e: e.tensor_single_scalar(out=W(4), in_=W(3), scalar=PI, op=ALU.is_gt), ["w"], ["w"])
            V(lambda e: e.scalar_tensor_tensor(out=W(3), in0=W(4), scalar=-2.0 * PI, in1=W(3), op0=ALU.mult,
                                               op1=ALU.add), ["w"], ["w"])
        V(lambda e: e.tensor_single_scalar(out=W(4), in_=W(3), scalar=-PI, op=ALU.is_lt), ["w"], ["w"])
        V(lambda e: e.scalar_tensor_tensor(out=W(3), in0=W(4), scalar=2.0 * PI, in1=W(3), op0=ALU.mult,
                                           op1=ALU.add), ["w"], ["w"])
        V(lambda e: e.tensor_scalar(out=W(5), in0=W(3), scalar1=PI / 2, scalar2=None, op0=ALU.add), ["w"], ["w"])
        V(lambda e: e.tensor_single_scalar(out=W(4), in_=W(5), scalar=PI, op=ALU.is_gt), ["w"], ["w"])
        V(lambda e: e.scalar_tensor_tensor(out=W(5), in0=W(4), scalar=-2.0 * PI, in1=W(5), op0=ALU.mult,
                                           op1=ALU.add), ["w"], ["w"])
        P.act(lambda e: e.activation(out=W(6), in_=W(3), func=AF.Sin), reads=["w"], writes=["w"])
        P.act(lambda e: e.activation(out=W(7), in_=W(5), func=AF.Sin), reads=["w"], writes=["w"])
        abr, abi = PW[:, 1, 0, :], PW[:, 1, 1, :]
        V(lambda e: tt_(e, abr, W(2), W(7), TT), ["w"], ["PW"])
        V(lambda e: tt_(e, abi, W(2), W(6), TT), ["w"], ["PW"])
        V(lambda e: e.memset(PW[:, 0, 0, :], 1.0), [], ["PW"])
        V(lambda e: e.memset(PW[:, 0, 1, :], 0.0), [], ["PW"])
        V(lambda e: tt_(e, W(0), lr, lr, TT), ["lam"], ["w"])
        V(lambda e: tt_(e, W(1), li, li, TT), ["lam"], ["w"])
        V(lambda e: tt_(e, W(0), W(0), W(1), ALU.add), ["w"], ["w"])
        V(lambda e: e.reciprocal(out=W(0), in_=W(0)), ["w"], ["w"])
        V(lambda e: e.tensor_scalar(out=W(1), in0=abr, scalar1=-1.0, scalar2=None, op0=ALU.add), ["PW"], ["w"])
        V(lambda e: tt_(e, W(2), W(1), lr, TT), ["w", "lam"], ["w"])
        V(lambda e: tt_(e, W(3), abi, li, TT), ["PW", "lam"], ["w"])
        V(lambda e: tt_(e, W(2), W(2), W(3), ALU.add), ["w"], ["w"])
        V(lambda e: tt_(e, W(8), W(2), W(0), TT), ["w"], ["w"])
        V(lambda e: tt_(e, W(2), abi, lr, TT), ["PW", "lam"], ["w"])
        V(lambda e: tt_(e, W(3), W(1), li, TT), ["w", "lam"], ["w"])
        V(lambda e: tt_(e, W(2), W(2), W(3), ALU.subtract), ["w"], ["w"])
        V(lambda e: tt_(e, W(9), W(2), W(0), TT), ["w"], ["w"])
        frb = W(8).unsqueeze(2).to_broadcast([128, 16, 16])
        fib = W(9).unsqueeze(2).to_broadcast([128, 16, 16])
        V(lambda e: tt_(e, w3[:, 0], bb[:, 0], frb, TT), ["bb", "w"], ["w3"])
        V(lambda e: tt_(e, w3[:, 1], bb[:, 1], fib, TT), ["bb", "w"], ["w3"])
        V(lambda e: tt_(e, w3[:, 2], bb[:, 1], frb, TT), ["bb", "w"], ["w3"])
        V(lambda e: tt_(e, w3[:, 3], bb[:, 0], fib, TT), ["bb", "w"], ["w3"])
        V(lambda e: tt_(e, bb[:, 0], w3[:, 0], w3[:, 1], ALU.subtract), ["w3"], ["bb"])
        V(lambda e: tt_(e, bb[:, 1], w3[:, 2], w3[:, 3], ALU.add), ["w3"], ["bb"])
        P.pool(lambda e: e.memset(BBp[:], 0.0), [], ["BBp"])
        for ri in range(2):
            for g2 in range(2):
                psl = slice(g2 * 64, (g2 + 1) * 64)
                V(lambda e, ri=ri, g2=g2, psl=psl: e.tensor_copy(out=BBp[psl, ri, :, g2 * 16:(g2 + 1) * 16],
                                                               in_=bb[psl, ri, :, :]), ["bb", "BBp"], ["BBp"])
        V(lambda e: e.tensor_copy(out=BBh[:], in_=BBp[:]), ["BBp"], ["BBh"])
        V(lambda e: tt_(e, dtmp[:], BBp[:], BBh[:], ALU.subtract), ["BBp", "BBh"], ["dtmp"])
        V(lambda e: e.tensor_copy(out=BBl[:], in_=dtmp[:]), ["dtmp"], ["BBl"])
        def cmul(o_r, o_i, a_r, a_i, b_r, b_i, shp, rk, wk):
            n_ = shp[1] if len(shp) == 3 else 1
            t = [pwt[:, i, 0:n_, :] if len(shp) == 3 else pwt[:, i, 0, :] for i in range(4)]
            V(lambda e: tt_(e, t[0], a_r, b_r, TT), rk, ["pwt"])
            V(lambda e: tt_(e, t[1], a_i, b_i, TT), rk, ["pwt"])
            V(lambda e: tt_(e, t[2], a_r, b_i, TT), rk, ["pwt"])
            V(lambda e: tt_(e, t[3], a_i, b_r, TT), rk, ["pwt"])
            V(lambda e: tt_(e, o_r, t[0], t[1], ALU.subtract), ["pwt"], wk)
            V(lambda e: tt_(e, o_i, t[2], t[3], ALU.add), ["pwt"], wk)

        def powers(T, key):
            cmul(T[:, 2, 0, :], T[:, 2, 1, :], T[:, 1, 0, :], T[:, 1, 1, :], T[:, 1, 0, :], T[:, 1, 1, :],
                 [128, 16], [key], [key])
            for (lo, n_, b) in ((1, 2, 2), (1, 4, 4)):
                bc_ = lambda ap: ap.unsqueeze(1).to_broadcast([128, n_, 16])
                cmul(T[:, b + 1:b + 1 + n_, 0, :], T[:, b + 1:b + 1 + n_, 1, :],
                     T[:, lo:lo + n_, 0, :], T[:, lo:lo + n_, 1, :], bc_(T[:, b, 0, :]), bc_(T[:, b, 1, :]),
                     [128, n_, 16], [key], [key])

        powers(PW, "PW")
        V(lambda e: e.tensor_copy(out=A8[:], in_=PW[:, 8, :, :]), ["PW"], ["A8"])
        V(lambda e: e.tensor_copy(out=A8p[:, 1, :, :], in_=PW[:, 8, :, :]), ["PW"], ["A8p"])
        powers(A8p, "A8p")
        V(lambda e: e.tensor_copy(out=A64[:, 0, :, :], in_=A8p[:, 8, :, :]), ["A8p"], ["A64"])
        for k in range(3):
            cmul(A64[:, k + 1, 0, :], A64[:, k + 1, 1, :], A64[:, k, 0, :], A64[:, k, 1, :], A64[:, k, 0, :],
                 A64[:, k, 1, :], [128, 16], ["A64"], ["A64"])
        def rb_quarter(qd, G, Y, yk, ykr):
            m0 = 2 * qd
            for mm in range(2):
                pr_ = PW[:, m0 + mm, 0, :].unsqueeze(2).to_broadcast([128, 16, 32])
                pi_ = PW[:, m0 + mm, 1, :].unsqueeze(2).to_broadcast([128, 16, 32])
                G(lambda e, mm=mm, pr_=pr_: tt_(e, Y[:, 0, mm], BBp[:, 0], pr_, TT), ["BBp", "PW"], yk)
                G(lambda e, mm=mm, pi_=pi_: tt_(e, Y[:, 1, mm], BBp[:, 1], pi_, TT), ["BBp", "PW"], yk)
                G(lambda e, mm=mm: tt_(e, Y[:, 0, mm], Y[:, 0, mm], Y[:, 1, mm], ALU.subtract), [], yk)
                G(lambda e, mm=mm, pr_=pr_: tt_(e, Y[:, 2, mm], BBp[:, 1], pr_, TT), ["BBp", "PW"], yk)
                G(lambda e, mm=mm, pi_=pi_: tt_(e, Y[:, 3, mm], BBp[:, 0], pi_, TT), ["BBp", "PW"], yk)
                G(lambda e, mm=mm: tt_(e, Y[:, 2, mm], Y[:, 2, mm], Y[:, 3, mm], ALU.add), [], yk)
            for mm in range(2):
                i = 7 - (m0 + mm)
                for ri in range(2):
                    sel = (2 * (2 * qd + mm) + ri) % 2
                    for o in range(6):
                        nv = CH_ROWS[o]
                        bank = 4 + 2 * sel + (o // 4)
                        out = ps[bank][:nv, (o % 4) * 128:(o % 4 + 1) * 128]
                        src_ = Y[:, 2 * ri, mm, 3 * o:3 * o + nv // 32, :].rearrange("p a c -> p (a c)")
                        P.pe(lambda e, out=out, src_=src_: e.transpose(out=out, in_=src_, identity=ident_f[:]),
                             reads=[ykr, "ident_f"], writes=[("ps", bank)])
                    b0 = 4 + 2 * sel
                    P.act(lambda e, i=i, ri=ri, b0=b0: e.copy(
                        out=RB[:96, 0:4, i, ri, :], in_=ps[b0][:96, :].rearrange("p (o c) -> p o c", c=128)),
                        [("ps", b0)], [("RB", i, ri)])
                    P.act(lambda e, i=i, ri=ri, b0=b0: e.copy(
                        out=RB[:96, 4, i, ri, :], in_=ps[b0 + 1][:96, 0:128]), [("ps", b0 + 1)], [("RB4", i, ri)])
                    P.act(lambda e, i=i, ri=ri, b0=b0: e.copy(
                        out=RB[:32, 5, i, ri, :], in_=ps[b0 + 1][:32, 128:256]), [("ps", b0 + 1)], [("RB5", i, ri)])
        Gp = lambda fn, r, w_: P.pool(fn, reads=r, writes=w_)
        Gd = lambda fn, r, w_: P.dve(fn, reads=r, writes=w_)
        for qd in (0, 1):
            rb_quarter(qd, Gp, Y, ["Y", "h0s"], "Y")
        P.pool(lambda e: e.memset(KBD[:], 0.0), [], ["KBD"])
        P.pool(lambda e: e.memset(CAr_b[:], 0.0), [], ["CAb"])
        P.pool(lambda e: e.memset(nCAi_b[:], 0.0), [], ["CAb"])
        P.pool(lambda e: e.memset(CAl[:], 0.0), [], ["CAl"])
        ccb = lambda ri: cc[:, ri].unsqueeze(1).to_broadcast([128, 9, 16, 16])
        pwb = lambda ri: PW[:, :, ri, :].unsqueeze(3).to_broadcast([128, 9, 16, 16])
        cck = [("cc", r_, t_) for r_ in range(2) for t_ in range(2)] + ["PW"]
        V(lambda e: tt_(e, X[:, 0], ccb(0), pwb(0), TT), cck, ["X0"])
        V(lambda e: tt_(e, X[:, 1], ccb(1), pwb(1), TT), cck, ["X1"])
        V(lambda e: tt_(e, X[:, 2], ccb(0), pwb(1), TT), cck, ["X2"])
        V(lambda e: tt_(e, X[:, 3], ccb(1), pwb(0), TT), cck, ["X3"])
        V(lambda e: tt_(e, X[:, 0], X[:, 0], X[:, 1], ALU.subtract), ["X0", "X1"], ["X0"])
        V(lambda e: tt_(e, X[:, 2], X[:, 2], X[:, 3], ALU.add), ["X2", "X3"], ["X2"])
        V(lambda e: e.tensor_scalar(out=X[:, 2], in0=X[:, 2], scalar1=-1.0, scalar2=None, op0=ALU.mult),
          ["X2"], ["X2"])
        for q, (sx, tab) in enumerate(((0, CAr_b), (2, nCAi_b))):
            for g2 in range(2):
                psl = slice(g2 * 64, (g2 + 1) * 64)
                gc = slice(g2 * 16, (g2 + 1) * 16)
                V(lambda e, sx=sx, tab=tab, psl=psl, gc=gc: e.tensor_copy(out=tab[psl, :, :, gc], in_=X[psl, sx]),
                  ["X%d" % sx, "CAb"], ["CAb"])
                V(lambda e, sx=sx, tab=tab, psl=psl, gc=gc: tt_(e, X[psl, sx + 1, 0:8], X[psl, sx, 0:8],
                                                                 tab[psl, 0:8, :, gc], ALU.subtract),
                  ["X%d" % sx, "CAb"], ["X%d" % (sx + 1)])
                V(lambda e, sx=sx, q=q, psl=psl, gc=gc: e.tensor_copy(out=CAl[psl, q, :, :, gc],
                                                                    in_=X[psl, sx + 1, 0:8]),
                  ["X%d" % (sx + 1), "CAl"], ["CAl"])
        for m in range(8 if not int(os.environ.get('KT_NOKBD', '0')) else 0):
            bk = m // 2
            for p in range(16):
                o, i3 = p // 3, p % 3
                c0_ = (m % 2) * 192 + o * 32
                out = ps[bk][i3 * 32:(i3 + 1) * 32, c0_:c0_ + 32]
                combos = []
                for q, tab in ((0, CAr_b), (1, nCAi_b)):
                    combos += [(BBh[:, q, p, :], tab[:, m, p, :]), (BBh[:, q, p, :], CAl[:, q, m, p, :]),
                               (BBl[:, q, p, :], tab[:, m, p, :])]
                for ci, (l_, r_) in enumerate(combos):
                    P.pe(lambda e, out=out, l_=l_, r_=r_, ci=ci: e.matmul(out, lhsT=l_, rhs=r_, start=(ci == 0),
                                                                      stop=(ci == 5)),
                         reads=["BBh", "BBl", "CAb", "CAl"], writes=[("ps", bk)])
        for bk in range(4 if not int(os.environ.get('KT_NOEV', '0')) else 0):
            for i3 in range(3):
                no = 6 if i3 == 0 else 5
                rsl = slice(i3 * 32, (i3 + 1) * 32)
                P.act(lambda e, bk=bk, i3=i3, no=no, rsl=rsl: e.copy(
                    out=KBD[rsl, 0:no, 2 * bk:2 * bk + 2, i3 * 32:(i3 + 1) * 32].rearrange("p o m c -> p m o c"),
                    in_=ps[bk][rsl, 0:384].rearrange("p (m o c) -> p m o c", m=2, o=6)[:, :, 0:no, :]),
                    [("ps", bk), "KBD"], ["KBD"])

        Y2 = X[:, 0:2].rearrange("p a b c d -> p (a b c d)")[:, 0:4096].rearrange("p (q m g c) -> p q m g c", q=4, m=2, g=16)
        for qd in (2, 3):
            rb_quarter(qd, Gd, Y2, ["X0", "X1", "Y2"], "Y2")

    def load_x(hf):
        for tt, (c0, rows) in enumerate(TILES):
            src = xp[hf * 1024 + c0: hf * 1024 + c0 + rows, :] if tt < 8 else xs[hf * 64:(hf + 1) * 64, :]
            P.dma("sp", lambda e, tt=tt, rows=rows, src=src: e.dma_start(out=x_tm[:rows, tt, :], in_=src),
                  writes=[("x", tt)])

    load_x(0)
    if not NOTAB:
        build_tables()
    P.dma("sp", lambda e: e.dma_start(out=kws[:, 0:120, :], in_=ck[:, 8:128, :]), writes=["kws_c"])
    P.dma("sp", lambda e: e.dma_start(out=vws[:, 0:120, :], in_=cv[:, 8:128, :]), writes=["vws_c"])
    DBG = {}
    if int(os.environ.get('KDEBUG', '0')):
        P.barrier()
        for nm, t, shp in (("KBD", KBD, [128, 6 * 8 * 128]), ("RB", RB, [128, 6 * 8 * 2 * 128]),
                           ("CAr", CAr_b, [128, 9 * 16 * 32]), ("nCAi", nCAi_b, [128, 9 * 16 * 32]),
                           ("A8", A8, [128, 32]), ("Dsk", Dsk, [128, 6]), ("h0T", h0T, [128, 512])):
            dd = dout("dbg_" + nm, shp)
            DBG[nm] = dd
            flat = t[:]
            nd = len(t.shape)
            if nd == 3:
                flat = t[:].rearrange("p a b -> p (a b)")
            elif nd == 4:
                flat = t[:].rearrange("p a b c -> p (a b c)")
            elif nd == 5:
                flat = t[:].rearrange("p a b c d -> p (a b c d)")
            P.dma("pool", lambda e, dd=dd, flat=flat: e.dma_start(out=dd[:, :], in_=flat),
                  reads=["KBD", "RB", "CAb", "A8", "Dsk", "h0T"], writes=["dbg" + nm])

    cnt = {"n": 0, "pb": 0}

    def rmsnorm_stats(tt, rows, junk, stages=None):
        sl = cnt["n"] % 4
        cnt["n"] += 1
        s1 = lambda: P.act(lambda e: e.activation(out=junk[:rows, :], in_=x_tm[:rows, tt, :], func=AF.Square,
                                                  accum_out=stat[:rows, sl, 0:1]),
                           reads=[("x", tt)], writes=["junk", ("stat", sl)])
        s2 = lambda: P.dve(lambda e: e.tensor_scalar(out=stat[:rows, sl, 1:2], in0=stat[:rows, sl, 0:1],
                                                     scalar1=1.0 / D, scalar2=EPS, op0=ALU.mult, op1=ALU.add),
                           reads=[("stat", sl)], writes=[("stat", sl)])
        s3 = lambda: P.act(lambda e: e.activation(out=stat[:rows, sl, 2:3], in_=stat[:rows, sl, 1:2], func=AF.Sqrt),
                           reads=[("stat", sl)], writes=[("stat", sl)])
        s4 = lambda: P.dve(lambda e: e.reciprocal(out=stat[:rows, sl, 3:4], in_=stat[:rows, sl, 2:3]),
                           reads=[("stat", sl)], writes=[("stat", sl)])
        if stages is None:
            s1(); s2(); s3(); s4()
        else:
            stages.extend([s1, s2, s3, s4])
        return sl

    def norm_stages(gi, tt, hT, hn, junk):
        c0, rows = TILES[tt]
        st_ = []
        sl = rmsnorm_stats(tt, rows, junk, stages=st_)
        hs = tt % 2
        pbk = 6 + tt % 2
        ptv = ps[pbk][:, 0:512].bitcast(BF16).rearrange("p (c t) -> p c t", c=8)

        def s5():
            if tt % 2:
                P.act(lambda e: e.activation(out=hn[:rows, hs, :], in_=x_tm[:rows, tt, :], func=AF.Copy,
                                             scale=stat[:rows, sl, 3:4]),
                      reads=[("x", tt), ("stat", sl)], writes=[("hn", hs)])
            else:
                P.dve(lambda e: e.tensor_scalar(out=hn[:rows, hs, :], in0=x_tm[:rows, tt, :],
                                                scalar1=stat[:rows, sl, 3:4], scalar2=None, op0=ALU.mult),
                      reads=[("x", tt), ("stat", sl)], writes=[("hn", hs)])

        def s6():
            for dc in range(8):
                P.pe(lambda e, dc=dc: e.transpose(out=ptv[:, dc, :rows], in_=hn[:rows, hs, dc * 128:(dc + 1) * 128],
                                                  identity=ident_b[:rows, :rows]),
                     reads=[("hn", hs), "ident_b"], writes=[("ps", pbk)])
            P.dve(lambda e: e.tensor_tensor(out=hT[:, :, c0:c0 + rows], in0=ptv[:, :, :rows],
                                            in1=gcol[:, gi, :].unsqueeze(2).to_broadcast([128, 8, rows]),
                                            op=ALU.mult),
                  reads=[("ps", pbk), "gcol"], writes=[("hT", tt)])

        return [st_[0], st_[1], st_[2], st_[3], s5, s6]

    class NormEpilogue:
        LAG = 1

        def __init__(self, gi, hT, hn, junk):
            self.a = (gi, hT, hn, junk)
            self.pending = []

        def __call__(self, tt, *unused):
            st = norm_stages(self.a[0], tt, *self.a[1:])
            for s in st[:5]:
                s()
            self.pending.append(st[5])
            if len(self.pending) > self.LAG:
                self.pending.pop(0)()

        def flush(self):
            while self.pending:
                self.pending.pop(0)()

    def norm_to_hT(gi, hT, hn, junk):
        per_tile = [norm_stages(gi, tt, hT, hn, junk) for tt in range(len(TILES))]
        nst = 6
        for step in range(len(TILES) + nst - 1):
            for k in range(nst):
                t = step - k
                if 0 <= t < len(TILES):
                    per_tile[t][k]()

    def grp(c):
        return 0 if c < 512 else (512 if c < 1024 else 1024)

    def tiles_in(c0, n):
        return [t for t, (a, r) in enumerate(TILES) if c0 <= a < c0 + n]

    def ffn_takes():
        A.off = 0
        hT = A.take([128, 8, NT], BF16)
        hid = A.take([128, 11, NT], BF16)
        wd_sb = A.take([128, 11, D], BF16)
        wgu = A.take([128, 3, 2, 8, 128], BF16)
        hn = A.take([128, 2, D], BF16)
        junk = A.take([128, D], BF16)
        sg = A.take([128, 2, 512], BF16)
        return hT, hid, wd_sb, wgu, hn, junk, sg

    def ffn(which, gi, barrier=True, after_tile=None, prenormed=False):
        if barrier:
            P.barrier()
        P.phase = 'ffn_' + which + str(P.hf)
        hT, hid, wd_sb, wgu, hn, junk, sg = ffn_takes()
        if after_tile is not None:
            A.off = 43136
            ep_bufs = (A.take([128, 2, D], F32), A.take([128, D], BF16))
            assert A.off <= 49024, A.off
        if not prenormed:
            norm_to_hT(gi, hT, hn, junk)
        wg, wu, wdn = Wg[which], Wu[which], Wd[which]
        k = 0
        for fh in range(2):
            for j in range(11):
                fc = fh * 11 + j
                s = fc % 3
                for m, w_ in enumerate((wg, wu)):
                    P.dma("pool", lambda e, fc=fc, s=s, m=m, w_=w_: e.dma_start(
                        out=wgu[:, s, m, :, :],
                        in_=w_[:, fc * 128:(fc + 1) * 128].rearrange("(c p) f -> p c f", p=128)),
                        writes=[("wgu", s, m)])
                P.dma("pool", lambda e, fc=fc, j=j: e.dma_start(
                    out=wd_sb[:, j, :], in_=wdn[fc * 128:(fc + 1) * 128, :]), writes=[("wd", j)])
                for (c0, n) in GROUPS:
                    tts = tiles_in(c0, n)
                    pb = (k % 3) * 2
                    ss = k % 2
                    k += 1
                    for m in range(2):
                        for dc in range(8):
                            P.pe(lambda e, m=m, dc=dc, s=s, c0=c0, n=n, pb=pb: e.matmul(
                                ps[pb + m][:, :n], lhsT=wgu[:, s, m, dc, :], rhs=hT[:, dc, c0:c0 + n],
                                start=(dc == 0), stop=(dc == 7)),
                                reads=[("wgu", s, m)] + [("hT", t) for t in tts], writes=[("ps", pb + m)])
                    P.act(lambda e, pb=pb, n=n, ss=ss: e.activation(out=sg[:, ss, :n], in_=ps[pb][:, :n],
                                                                  func=AF.Silu),
                          reads=[("ps", pb)], writes=[("sg", ss)])
                    P.dve(lambda e, pb=pb, n=n, ss=ss, j=j, c0=c0: e.tensor_tensor(
                        out=hid[:, j, c0:c0 + n], in0=sg[:, ss, :n], in1=ps[pb + 1][:, :n], op=ALU.mult),
                        reads=[("sg", ss), ("ps", pb + 1)], writes=[("hid", j, c0)])
            for tt, (c0, rows) in enumerate(TILES):
                g0 = [g for (g, n) in GROUPS if g <= c0 < g + n][0]
                for hh in range(2):
                    pb = 6 + hh
                    for j in range(11):
                        P.pe(lambda e, j=j, c0=c0, rows=rows, hh=hh, pb=pb: e.matmul(
                            ps[pb][:rows, :], lhsT=hid[:, j, c0:c0 + rows],
                            rhs=wd_sb[:, j, hh * 512:(hh + 1) * 512], start=(j == 0), stop=(j == 10)),
                            reads=[("hid", j, g0), ("wd", j)], writes=[("ps", pb)])
                    P.dve(lambda e, tt=tt, rows=rows, hh=hh, pb=pb: e.scalar_tensor_tensor(
                        out=x_tm[:rows, tt, hh * 512:(hh + 1) * 512], in0=ps[pb][:rows, :], scalar=0.5,
                        in1=x_tm[:rows, tt, hh * 512:(hh + 1) * 512], op0=ALU.mult, op1=ALU.add),
                        reads=[("ps", pb), ("x", tt)], writes=[("x", tt)])
                if fh == 1 and after_tile is not None:
                    after_tile(tt, *ep_bufs)
        if after_tile is not None and hasattr(after_tile, "flush"):
            after_tile.flush()
        return junk

    def pbank(lo, hi):
        cnt["pb"] += 1
        return lo + cnt["pb"] % (hi - lo)

    def wslice(col0, ncols):
        return w_in[:, col0:col0 + ncols].rearrange("(c p) f -> p c f", p=128)

    def mix_takes():
        A.off = 0
        hT = A.take([128, 8, NT], BF16)
        mg = A.take([128, 9, D], BF16)
        mark = A.off
        qT = A.take([128, 8, NT], BF16)
        kkT = A.take([128, 4, NT], BF16)
        vaug = A.take([128, 9, 4, 68], BF16)
        kvout = A.take([128, 2, 512], F32)
        wf = A.take([128, 3, 8, 128], BF16)
        wt = A.take([128, 2, 8, 256], BF16)
        oT_sb = wt[:, :, :, :].rearrange("p a b c -> p (a b c)")[:, 0:2048].bitcast(F32)
        et = A.take([128, 4, 2, 256], BF16)
        etc_ = A.take([128, 2, 128], BF16)
        etn = A.take([128, 1, 128], BF16)
        den = A.take([128, 2, 2, 4], F32)
        atmp = A.take([128, 2, 4, 64], F32)
        ckd = A.take([128, 2, 4, 2, 64], BF16)
        kcT = A.take([128, 2, 4, 128], BF16)
        vcaug = A.take([128, 3, 4, 68], BF16)
        assert A.off >= 43136, A.off
        hn = A.take([128, 2, D], BF16)
        etr = A.take([128, 2, 2, 256], BF16)
        junk = etr[:, 0:2, :, :].rearrange("p a b c -> p (a b c)")
        return (hT, mg, mark, hn, etr, junk, qT, kkT, vaug, kvout, wf, wt, oT_sb, et, etc_, etn, den, atmp, ckd,
                kcT, vcaug)

    def mix(hf):
        P.barrier()
        P.phase = 'mix1_' + str(hf)
        (hT, mg, mark, hn, etr, junk, qT, kkT, vaug, kvout, wf, wt, oT_sb, et, etc_, etn, den, atmp, ckd, kcT,
         vcaug) = mix_takes()
        P.dve(lambda e: e.memset(vaug[:, :, :, 64:68], 1.0), writes=["vaug1"])
        P.dve(lambda e: e.memset(vcaug[:, :, :, 64:68], 1.0), writes=["vcaug1"])

        ftiles = [("q", i) for i in range(8)] + [("kk", g) for g in range(4)]
        for ti, (kind, idx) in enumerate(ftiles):
            s = ti % 3
            if kind == "q":
                P.dma("pool", lambda e, s=s, idx=idx: e.dma_start(out=wf[:, s, :, :], in_=wslice(idx * 128, 128)),
                      writes=[("wf", s)])
                dest = qT
            else:
                for h2 in range(2):
                    P.dma("pool", lambda e, s=s, idx=idx, h2=h2: e.dma_start(
                        out=wf[:, s, :, h2 * 64:(h2 + 1) * 64], in_=wslice(1024 + idx * 64, 64)),
                        writes=[("wf", s)])
                dest = kkT
            for (c0, n) in GROUPS:
                pb = pbank(0, 4)
                for dc in range(8):
                    P.pe(lambda e, s=s, dc=dc, c0=c0, n=n, pb=pb: e.matmul(
                        ps[pb][:, :n], lhsT=wf[:, s, dc, :], rhs=hT[:, dc, c0:c0 + n], start=(dc == 0),
                        stop=(dc == 7)),
                        reads=[("wf", s)] + [("hT", t) for t in tiles_in(c0, n)], writes=[("ps", pb)])
                if cnt["pb"] % 2:
                    P.act(lambda e, dest=dest, idx=idx, c0=c0, n=n, pb=pb: e.copy(out=dest[:, idx, c0:c0 + n],
                                                                                in_=ps[pb][:, :n]),
                          reads=[("ps", pb)], writes=[(kind, idx, c0)])
                else:
                    P.dve(lambda e, dest=dest, idx=idx, c0=c0, n=n, pb=pb: e.tensor_copy(out=dest[:, idx, c0:c0 + n],
                                                                                       in_=ps[pb][:, :n]),
                          reads=[("ps", pb)], writes=[(kind, idx, c0)])

        if KSUB <= 1:
            return hT, mg, mark
        blocks = [("k", 1024), ("v", 1280)] + [("ga", 2048 + 256 * j) for j in range(4)]
        KBLK = os.environ.get('KBLK', 'k,v,ga').split(',')
        blocks = [b for b in blocks if b[0] in KBLK]
        for bi, (kind, col0) in enumerate(blocks):
            s = bi % 2
            P.dma("pool", lambda e, s=s, col0=col0: e.dma_start(out=wt[:, s, :, :], in_=wslice(col0, 256)),
                  writes=[("wt", s)])
            for tt, (c0, rows) in enumerate(TILES):
                is_out = (tt == 8) or (tt == 7 and hf == 1)
                osl = 0 if tt == 8 else 1
                if kind == "k" and not is_out:
                    continue
                pb = pbank(4, 8)
                for dc in range(8):
                    P.pe(lambda e, s=s, dc=dc, c0=c0, rows=rows, pb=pb: e.matmul(
                        ps[pb][:rows, :256], lhsT=hT[:, dc, c0:c0 + rows], rhs=wt[:, s, dc, :], start=(dc == 0),
                        stop=(dc == 7)),
                        reads=[("wt", s), ("hT", tt)], writes=[("ps", pb)])
                if kind == "k":
                    P.act(lambda e, rows=rows, osl=osl, pb=pb: e.copy(out=kvout[:rows, osl, 0:256],
                                                                     in_=ps[pb][:rows, :256]),
                          reads=[("ps", pb)], writes=[("kvout", osl, 0)])
                elif kind == "v":
                    P.act(lambda e, rows=rows, tt=tt, pb=pb: e.copy(
                        out=vaug[:rows, tt, :, 0:64], in_=ps[pb][:rows, :256].rearrange("p (g d) -> p g d", g=4)),
                        reads=[("ps", pb)], writes=[("vaug", tt)])
                    if is_out:
                        P.dve(lambda e, rows=rows, osl=osl, pb=pb: e.tensor_copy(out=kvout[:rows, osl, 256:512],
                                                                                in_=ps[pb][:rows, :256]),
                              reads=[("ps", pb)], writes=[("kvout", osl, 1)])
                else:
                    j = (col0 - 2048) // 256
                    P.act(lambda e, rows=rows, tt=tt, j=j, pb=pb: e.activation(
                        out=mg[:rows, tt, j * 256:(j + 1) * 256], in_=ps[pb][:rows, :256], func=AF.Sigmoid),
                        reads=[("ps", pb)], writes=[("mg", tt, j)])
        if KSUB <= 2:
            return hT, mg, mark
        if hf == 1:
            P.dma("sp", lambda e: e.dma_start(out=kwp[:, :], in_=kvout[:, 1, 0:256]), reads=[("kvout", 1, 0)],
                  writes=["kwp"])
            P.dma("sp", lambda e: e.dma_start(out=vwp[:, :], in_=kvout[:, 1, 256:512]), reads=[("kvout", 1, 1)],
                  writes=["vwp"])
        for s_ in range(8):
            sl_ = hf * 8 + s_
            P.dma("sp", lambda e, s_=s_, sl_=sl_: e.dma_start(out=kws[sl_, 120:128, :],
                                                            in_=kvout[s_ * 8:(s_ + 1) * 8, 0, 0:256]),
                  reads=[("kvout", 0, 0)], writes=[("kws", sl_)])
            P.dma("sp", lambda e, s_=s_, sl_=sl_: e.dma_start(out=vws[sl_, 120:128, :],
                                                            in_=kvout[s_ * 8:(s_ + 1) * 8, 0, 256:512]),
                  reads=[("kvout", 0, 1)], writes=[("vws", sl_)])

        if KSUB <= 3:
            return hT, mg, mark
        P.phase = 'attn_' + str(hf)
        def normalize(g, tt, rows, aob, dsl):
            ao = ps[aob][:rows, :].rearrange("p (r c) -> p r c", c=128)
            P.dve(lambda e: e.tensor_tensor(out=den[:rows, dsl, 0, :], in0=ao[:, :, 64],
                                            in1=esink[:rows, 4 * g:4 * g + 4], op=ALU.add),
                  reads=[("ps", aob), "esink"], writes=[("den", dsl)])
            P.dve(lambda e: e.reciprocal(out=den[:rows, dsl, 1, :], in_=den[:rows, dsl, 0, :]),
                  reads=[("den", dsl)], writes=[("den", dsl)])
            P.dve(lambda e: e.tensor_tensor(
                out=atmp[:rows, dsl, :, :], in0=ao[:, :, 0:64],
                in1=den[:rows, dsl, 1, :].unsqueeze(2).to_broadcast([rows, 4, 64]), op=ALU.mult),
                reads=[("ps", aob), ("den", dsl)], writes=[("atmp", dsl)])
            mgv = mg[:rows, tt, g * 256:(g + 1) * 256]
            P.dve(lambda e: e.tensor_tensor(out=mgv, in0=atmp[:rows, dsl, :, :].rearrange("p r c -> p (r c)"),
                                            in1=mgv, op=ALU.mult),
                  reads=[("atmp", dsl), ("mg", tt, g)], writes=[("mg", tt, g)])

        DEPTH = 2
        its = [(tt, g, hh) for tt in range(8) for g in range(4) for hh in range(2)]
        info = {}

        def stageA2(n0):
            todo = []
            for n in (n0, n0 + 1):
                tt, g, hh = its[n]
                c0 = tt * 128
                hsl = slice(hh * 64, (hh + 1) * 64)
                stb = n % 4
                st = ps[stb][:, :].rearrange("p (k c) -> p k c", k=2)
                es = n % 4
                kbs = []
                if tt > 0:
                    kbs.append((0, kkT[hsl, g, c0 - 128:c0], vaug[:, tt - 1, g, 0:65], [("kk", g, grp(c0 - 128)), ("vaug", tt - 1)]))
                elif hf == 1:
                    kbs.append((0, kk_carry[hsl, g, :], v_carry[:, g, 0:65], ["kk_carry", "v_carry"]))
                kbs.append((1, kkT[hsl, g, c0:c0 + 128], vaug[:, tt, g, 0:65], [("kk", g, grp(c0)), ("vaug", tt)]))
                info[n] = (kbs, es)
                todo.append((n, tt, g, hh, c0, hsl, stb, st, es, kbs))
            for ki in range(len(todo[0][9])):
                for (n, tt, g, hh, c0, hsl, stb, st, es, kbs) in todo:
                    (kb, kap, vap, keys) = kbs[ki]
                    qg = grp(c0)
                    P.pe(lambda e, kb=kb, kap=kap, hsl=hsl, g=g, c0=c0, st=st: e.matmul(
                        st[:, kb, :], lhsT=kap, rhs=qT[hsl, 2 * g:2 * g + 2, c0:c0 + 128], start=True, stop=True),
                        reads=[keys[0], ("q", 2 * g, qg), ("q", 2 * g + 1, qg)], writes=[("ps", stb)])
            for (n, tt, g, hh, c0, hsl, stb, st, es, kbs) in todo:
                k0 = kbs[0][0]
                er = n % 2
                P.act(lambda e, k0=k0, st=st, er=er: e.activation(out=etr[:, er, k0:2, :], in_=st[:, k0:2, :],
                                                                  func=AF.Exp, scale=0.125),
                      reads=[("ps", stb)], writes=[("etr", er)])
                (P.pool if hh == 0 else P.dve)(
                    lambda e, k0=k0, es=es, er=er: e.tensor_tensor(out=et[:, es, k0:2, :], in0=etr[:, er, k0:2, :],
                                                                   in1=m01_b[:, k0:2, :], op=ALU.mult),
                    reads=[("etr", er), "m01"], writes=[("et", es)])

        def stageB(n):
            tt, g, hh = its[n]
            kbs, es = info[n]
            aob = 4 + (n // 2) % 2
            dsl = (n // 2) % 2
            ao = ps[aob][:, :].rearrange("p (r c) -> p r c", c=128)
            for j in range(2):
                r = 2 * j + hh
                for n_, (kb, kap, vap, keys) in enumerate(kbs):
                    P.pe(lambda e, r=r, kb=kb, j=j, vap=vap, es=es, ao=ao, n_=n_, L=len(kbs): e.matmul(
                        ao[:, r, 0:65], lhsT=et[:, es, kb, j * 128:(j + 1) * 128], rhs=vap,
                        start=(n_ == 0), stop=(n_ == L - 1)),
                        reads=[("et", es), keys[1], "vaug1"], writes=[("ps", aob)])
            if hh == 1:
                normalize(g, tt, 128, aob, dsl)

        for n in range(0, len(its) + DEPTH, 2):
            if n < len(its):
                stageA2(n)
            for m_ in (n - DEPTH, n - DEPTH + 1):
                if 0 <= m_ < len(its):
                    stageB(m_)
        if KSUB <= 4:
            return hT, mg, mark
        P.phase = 'attS_' + str(hf)
        qc = slice(1024, 1088)
        oTb = [ps[0], ps[1]]
        first_in_bank = [True, True]
        ecnt = 0
        for g in range(4):
            for hh in range(2):
                hsl = slice(hh * 64, (hh + 1) * 64)
                es = 0
                ecnt += 1
                P.pe(lambda e, hsl=hsl, g=g: e.matmul(
                    ps[6][:64, 0:128], lhsT=kkT[hsl, g, qc], rhs=qT[hsl, 2 * g:2 * g + 2, qc], start=True,
                    stop=False),
                    reads=[("kk", g, 1024), ("q", 2 * g, 1024), ("q", 2 * g + 1, 1024)], writes=[("ps", 6)])
                P.pe(lambda e: e.matmul(ps[6][:64, 0:128], lhsT=ident_b[:, :64], rhs=mnew_b[:, :],
                                        start=False, stop=True),
                     reads=["ident_b", "mnew"], writes=[("ps", 6)])
                P.act(lambda e, es=es: e.activation(out=etn[:64, es, :], in_=ps[6][:64, 0:128], func=AF.Exp,
                                                    scale=0.125),
                      reads=[("ps", 6)], writes=[("etn", es)])
                for j in range(2):
                    h = 4 * g + 2 * j + hh
                    bk = h // 8
                    P.pe(lambda e, h=h, bk=bk, j=j, es=es, g=g, st_=first_in_bank[bk]: e.matmul(
                        oTb[bk][:65, (h % 8) * 64:(h % 8 + 1) * 64], lhsT=vaug[:64, 8, g, 0:65],
                        rhs=etn[:64, es, j * 64:(j + 1) * 64], start=st_, stop=False, skip_group_check=True),
                        reads=[("etn", es), ("vaug", 8), "vaug1"], writes=[("ps", bk)])
                    first_in_bank[bk] = False
        ptk = ps[7][:, 0:256].bitcast(BF16).rearrange("p (g t) -> p g t", g=4)
        def sA(s_):
            sl_ = hf * 8 + s_
            cs = s_ % 2
            for d2 in range(2):
                P.dma("pool", lambda e, cs=cs, d2=d2, sl_=sl_: e.dma_start(
                    out=ckd[:, cs, :, d2, :], in_=ck[sl_].rearrange("p (g d) -> p g d", g=4)),
                    writes=[("ckd", cs)])
            vs = s_ % 3
            P.dma("pool", lambda e, vs=vs, sl_=sl_: e.dma_start(
                out=vcaug[:, vs, :, 0:64], in_=cv[sl_].rearrange("p (g d) -> p g d", g=4)),
                reads=["vcaug1"], writes=[("vcaug", vs)])
            for g in range(4):
                P.pe(lambda e, cs=cs, g=g: e.transpose(
                    out=ptk[:, g, :], in_=ckd[:, cs, g, :, :].rearrange("p a d -> p (a d)"), identity=ident_b[:, :]),
                    reads=[("ckd", cs), "ident_b"], writes=[("ps", 7)])
            P.act(lambda e, cs=cs: e.copy(out=kcT[:, cs, :, :], in_=ptk[:, :, :]), reads=[("ps", 7)],
                  writes=[("kcT", cs)])
        def sA2(s_):
            cs = s_ % 2
            qs = slice(1024 + 8 * s_, 1032 + 8 * s_)
            for hh in range(2):
                hsl = slice(hh * 64, (hh + 1) * 64)
                stb = 2 + hh
                P.pe(lambda e, stb=stb: e.matmul(ps[stb][:, 0:64], lhsT=ident_b[:, :], rhs=mc2_b[:, :], start=True,
                                                 stop=False),
                     reads=["ident_b", "mc2"], writes=[("ps", stb)])
                for g in range(4):
                    P.pe(lambda e, stb=stb, hsl=hsl, g=g, cs=cs, qs=qs: e.matmul(
                        ps[stb][:, g * 16:(g + 1) * 16], lhsT=kcT[hsl, cs, g, :], rhs=qT[hsl, 2 * g:2 * g + 2, qs],
                        start=False, stop=(g == 3)),
                        reads=[("kcT", cs), ("q", 2 * g, 1024), ("q", 2 * g + 1, 1024)], writes=[("ps", stb)])
                P.act(lambda e, stb=stb, cs=cs, hh=hh: e.activation(out=etc_[:, cs, hh * 64:(hh + 1) * 64],
                                                                   in_=ps[stb][:, 0:64], func=AF.Exp, scale=0.125),
                      reads=[("ps", stb)], writes=[("etc", cs, hh)])
        def sB(s_):
            cs = s_ % 2
            for g in range(4):
                for j in range(2):
                    for hh in range(2):
                        h = 4 * g + 2 * j + hh
                        bk = h // 8
                        c_ = hh * 64 + g * 16 + j * 8
                        P.pe(lambda e, h=h, bk=bk, g=g, cs=cs, c_=c_, s_=s_: e.matmul(
                            oTb[bk][:65, (h % 8) * 64 + s_ * 8:(h % 8) * 64 + s_ * 8 + 8],
                            lhsT=vcaug[:, s_ % 3, g, 0:65], rhs=etc_[:, cs, c_:c_ + 8], start=False, stop=(s_ == 7),
                            skip_group_check=True),
                            reads=[("etc", cs, hh), ("vcaug", s_ % 3), "vcaug1"], writes=[("ps", bk)])
        for s_ in range(10):
            if s_ < 8:
                sA(s_)
            if 1 <= s_ < 9:
                sA2(s_ - 1)
            if s_ >= 2:
                sB(s_ - 2)
        for bk in range(2):
            P.act(lambda e, bk=bk: e.copy(out=oT_sb[:65, bk * 512:(bk + 1) * 512], in_=oTb[bk][:65, :]),
                  reads=[("ps", bk)], writes=[("oT_sb", bk)])
        for g in range(4):
            aob = 4 + g % 2
            ao = ps[aob][:64, :].rearrange("p (r c) -> p r c", c=128)
            for r in range(4):
                h = 4 * g + r
                P.pe(lambda e, ao=ao, r=r, h=h: e.transpose(out=ao[:, r, 0:65], in_=oT_sb[:65, h * 64:(h + 1) * 64],
                                                          identity=ident_f[:65, :65]),
                     reads=[("oT_sb", h // 8), "ident_f"], writes=[("ps", aob)])
            normalize(g, 8, 64, aob, g % 2)
        if hf == 0:
            P.dve(lambda e: e.tensor_copy(out=kk_carry[:, :, :], in_=kkT[:, :, 896:1024]),
                  reads=[("kk", g, 512) for g in range(4)], writes=["kk_carry"])
            P.dve(lambda e: e.tensor_copy(out=v_carry[:, :, :], in_=vaug[:, 7, :, :]), reads=[("vaug", 7), "vaug1"],
                  writes=["v_carry"])
        return hT, mg, mark

    def mix2(hf, hT, mg, mark):
        P.barrier()
        P.phase = 'ssmA_' + str(hf)
        A.off = mark
        uT = A.take([128, 6, 8, NB], BF16)
        Hb = A.take([128, 2, 16, NB], BF16)
        m5 = A.off
        Z = A.take([128, 2, 16, NB], F32)
        wf2 = A.take([128, 3, 8, 128], BF16)
        svt = A.take([128, 2, 16, 8], F32)
        sso = A.take([128, 2, 512], F32)
        ssp = A.take([128, 2, 128], F32)
        tw = A.take([128, 4, 16, 16], F32)
        Cb = A.take([128, 2, 16, 16], F32)
        Ea = A.take([128, 2, 16, 16], F32)
        Eb = A.take([128, 2, 16, 16], F32)
        for o in range(6):
            nv = CH_ROWS[o]
            s = o % 3
            P.dma("pool", lambda e, s=s, o=o, nv=nv: e.dma_start(out=wf2[:, s, :, 0:nv],
                                                               in_=wslice(1536 + o * 96, nv)),
                  writes=[("wf2", s)])
            for (c0, n) in GROUPS:
                pb = pbank(0, 4)
                for dc in range(8):
                    P.pe(lambda e, s=s, dc=dc, c0=c0, n=n, pb=pb, nv=nv: e.matmul(
                        ps[pb][:nv, :n], lhsT=wf2[:, s, dc, 0:nv], rhs=hT[:, dc, c0:c0 + n], start=(dc == 0),
                        stop=(dc == 7)),
                        reads=[("wf2", s)] + [("hT", t) for t in tiles_in(c0, n)], writes=[("ps", pb)])
                k0, nk = c0 // 8, n // 8
                if cnt["pb"] % 2:
                    P.act(lambda e, o=o, n=n, pb=pb, nv=nv, k0=k0, nk=nk: e.copy(
                        out=uT[:nv, o, :, k0:k0 + nk].rearrange("p j k -> p k j"),
                        in_=ps[pb][:nv, :n].rearrange("p (k j) -> p k j", j=8)),
                        reads=[("ps", pb)], writes=[("uT", o)])
                else:
                    P.dve(lambda e, o=o, n=n, pb=pb, nv=nv, k0=k0, nk=nk: e.tensor_copy(
                        out=uT[:nv, o, :, k0:k0 + nk].rearrange("p j k -> p k j"),
                        in_=ps[pb][:nv, :n].rearrange("p (k j) -> p k j", j=8)),
                        reads=[("ps", pb)], writes=[("uT", o)])
        for p in range(16):
            o, i3 = p // 3, p % 3
            rsl = slice(i3 * 32, (i3 + 1) * 32)
            for ri in range(2):
                pb = pbank(4, 8)
                for i in range(8):
                    P.pe(lambda e, o=o, rsl=rsl, ri=ri, i=i, pb=pb: e.matmul(
                        ps[pb][:, 0:NB], lhsT=RB[rsl, o, i, ri, :], rhs=uT[rsl, o, i, :], start=(i == 0),
                        stop=(i == 7)),
                        reads=["RB", ("uT", o)], writes=[("ps", pb)])
                P.act(lambda e, p=p, ri=ri, pb=pb: e.copy(out=Z[:, ri, p, :], in_=ps[pb][:, 0:NB]),
                      reads=[("ps", pb)], writes=["Z"])
        if hf == 0:
            P.dve(lambda e: e.memset(Hc[:], 0.0), writes=["Hc"])
        Zv = [Z[:, ri, :, 0:128].rearrange("p a (s j) -> p a s j", j=8) for ri in range(2)]
        bc = lambda ap: ap.unsqueeze(2).to_broadcast([128, 16, 16])

        pA8p = ps[0][:, 0:288].rearrange("p (a b c) -> p a b c", a=9, b=2)
        pA8 = ps[0][:, 288:320].rearrange("p (b c) -> p b c", b=2)
        pA64 = ps[0][:, 320:448].rearrange("p (a b c) -> p a b c", a=4, b=2)
        P.dve(lambda e: e.tensor_copy(out=pA8p[:, 1:9], in_=A8p[:, 1:9, :, :]), reads=["A8p"], writes=[("ps", 0)])
        P.dve(lambda e: e.tensor_copy(out=pA8, in_=A8[:, :, :]), reads=["A8"], writes=[("ps", 0)])
        P.dve(lambda e: e.tensor_copy(out=pA64, in_=A64[:, :, :, :]), reads=["A64"], writes=[("ps", 0)])
        pT = ps[1][:, :].rearrange("p (t a b) -> p t a b", t=2, a=16)
        first = {"x": [("ps", 0), ("ps", 1)]}

        def cmul_acc(dr, di, ar_, ai_, xr, xi, keys_r, keys_w, T_):
            t0, t1, t2, t3 = T_(0), T_(1), T_(2), T_(3)
            fx = first["x"]
            first["x"] = []
            P.dve(lambda e: tt_(e, t0, ar_, xr, TT), keys_r, ["tw0"] + fx)
            P.dve(lambda e: tt_(e, t1, ai_, xi, TT), keys_r, ["tw1"])
            P.dve(lambda e: tt_(e, t2, ar_, xi, TT), keys_r, ["tw2"])
            P.dve(lambda e: tt_(e, t3, ai_, xr, TT), keys_r, ["tw3"])
            P.dve(lambda e: tt_(e, t0, t0, t1, ALU.subtract), ["tw0", "tw1"], ["tw0"])
            P.dve(lambda e: tt_(e, t2, t2, t3, ALU.add), ["tw2", "tw3"], ["tw2"])
            P.dve(lambda e: tt_(e, dr, dr, t0, ALU.add), ["tw0"] + keys_w, keys_w)
            P.dve(lambda e: tt_(e, di, di, t2, ALU.add), ["tw2"] + keys_w, keys_w)

        def Tw(n_):
            return lambda i: (pT[:, i // 2, :, 0:n_] if i % 2 == 0 else tw[:, i, :, 0:n_])

        T_ = Tw(16)
        for j in range(1, 8):
            cmul_acc(Zv[0][:, :, :, j], Zv[1][:, :, :, j], bc(pA8[:, 0, :]), bc(pA8[:, 1, :]),
                     Zv[0][:, :, :, j - 1], Zv[1][:, :, :, j - 1], ["Z"], ["Z"], T_)
        P.dve(lambda e: e.tensor_copy(out=Ea[:, :, :, :], in_=Z[:, :, :, 7:128:8]), reads=["Z"], writes=["Ea"])
        if hf == 1:
            T1 = lambda i: (pT[:, i // 2, :, 0] if i % 2 == 0 else tw[:, i, :, 0])
            cmul_acc(Ea[:, 0, :, 0], Ea[:, 1, :, 0], pA64[:, 0, 0, :], pA64[:, 0, 1, :], Hc[:, 0, :], Hc[:, 1, :],
                     ["Hc", "Ea"], ["Ea"], T1)
        bufs = [(Ea, "Ea"), (Eb, "Eb")]
        for k, d in enumerate((1, 2, 4, 8)):
            (s_, sk), (d_, dk) = bufs[k % 2], bufs[(k + 1) % 2]
            n_ = 16 - d
            P.dve(lambda e, s_=s_, d_=d_: e.tensor_copy(out=d_[:, :, :, :], in_=s_[:, :, :, :]), reads=[sk], writes=[dk])
            bcn = lambda ap, n_=n_: ap.unsqueeze(2).to_broadcast([128, 16, n_])
            cmul_acc(d_[:, 0, :, d:16], d_[:, 1, :, d:16], bcn(pA64[:, k, 0, :]), bcn(pA64[:, k, 1, :]),
                     s_[:, 0, :, 0:n_], s_[:, 1, :, 0:n_], [sk, dk], [dk], Tw(n_))
        P.dve(lambda e: e.tensor_copy(out=Cb[:, :, :, 0], in_=Hc[:, :, :]), reads=["Hc"], writes=["Cb"])
        P.dve(lambda e: e.tensor_copy(out=Cb[:, :, :, 1:16], in_=Ea[:, :, :, 0:15]), reads=["Ea", "Cb"], writes=["Cb"])
        T_ = Tw(16)
        for j in range(8):
            cmul_acc(Zv[0][:, :, :, j], Zv[1][:, :, :, j], bc(pA8p[:, j + 1, 0, :]), bc(pA8p[:, j + 1, 1, :]),
                     Cb[:, 0, :, 0:16], Cb[:, 1, :, 0:16], ["Cb", "Z"], ["Z"], T_)
        P.dve(lambda e: e.tensor_copy(out=Hb[:, :, :, 0], in_=Hc[:, :, :]), reads=["Hc"], writes=["Hb"])
        P.dve(lambda e: e.tensor_copy(out=Hb[:, :, :, 1:128], in_=Z[:, :, :, 0:127]), reads=["Z"], writes=["Hb"])
        P.dve(lambda e: e.tensor_copy(out=Hb[:, :, :, 128:136], in_=h0T[:, :, :, hf * 8:(hf + 1) * 8]),
              reads=["h0T"], writes=["Hb"])
        P.dve(lambda e: e.tensor_copy(out=Hc[:, :, :], in_=Z[:, :, :, 127]), reads=["Z", "Hb"], writes=["Hc"])
        arb = A8[:, 0, :].unsqueeze(2).to_broadcast([128, 16, 8])
        aib = A8[:, 1, :].unsqueeze(2).to_broadcast([128, 16, 8])
        h0r_, h0i_ = h0T[:, 0, :, hf * 8:(hf + 1) * 8], h0T[:, 1, :, hf * 8:(hf + 1) * 8]
        zs_r, zs_i = Z[:, 0, :, 128:136], Z[:, 1, :, 128:136]
        for (dst, x1, x2, op) in ((zs_r, h0r_, h0i_, ALU.subtract), (zs_i, h0i_, h0r_, ALU.add)):
            P.dve(lambda e, x1=x1: tt_(e, svt[:, 0], x1, arb, TT), ["h0T", "A8"], ["svt"])
            P.dve(lambda e, x2=x2: tt_(e, svt[:, 1], x2, aib, TT), ["h0T", "A8"], ["svt"])
            P.dve(lambda e, op=op: tt_(e, svt[:, 0], svt[:, 0], svt[:, 1], op), ["svt"], ["svt"])
            P.dve(lambda e, dst=dst: tt_(e, dst, dst, svt[:, 0], ALU.add), ["svt", "Z", "Hb"], ["Z"])
        for ri, dd in enumerate((srs, sis)):
            for q4 in range(4):
                pb = pbank(4, 7)
                so = q4 % 2
                for pp in range(4):
                    p = q4 * 4 + pp
                    P.pe(lambda e, ri=ri, p=p, pp=pp, pb=pb: e.transpose(
                        out=ps[pb][:8, pp * 128:(pp + 1) * 128], in_=Z[:, ri, p, 128:136], identity=ident_f[:, :]),
                        reads=["Z", "ident_f"], writes=[("ps", pb)])
                P.act(lambda e, pb=pb, so=so: e.copy(out=sso[:8, so, :], in_=ps[pb][:8, :]),
                      reads=[("ps", pb)], writes=[("sso", so)])
                P.dma("sp", lambda e, dd=dd, q4=q4, so=so: e.dma_start(
                    out=dd[hf * 8:(hf + 1) * 8, q4 * 512:(q4 + 1) * 512], in_=sso[:8, so, :]),
                    reads=[("sso", so)], writes=[("srs", ri, q4, hf)])
        if hf == 1:
            for ri, dd in enumerate((srp, sip)):
                pb = pbank(4, 7)
                P.pe(lambda e, ri=ri, pb=pb: e.transpose(out=ps[pb][:16, 0:128], in_=Hc[:, ri, :],
                                                         identity=ident_f[:, :]),
                     reads=["Hc", "ident_f"], writes=[("ps", pb)])
                P.act(lambda e, ri=ri, pb=pb: e.copy(out=ssp[:16, ri, :], in_=ps[pb][:16, 0:128]),
                      reads=[("ps", pb)], writes=[("ssp", ri)])
                P.dma("sp", lambda e, ri=ri, dd=dd: e.dma_start(out=dd[:, :], in_=ssp[:16, ri, :]),
                      reads=[("ssp", ri)], writes=[("srp", ri)])
        return uT, Hb, m5

    def mix3(hf, hT, mg, uT, Hb, m5):
        P.barrier()
        P.phase = 'ssmB_' + str(hf)
        A.off = m5
        yT = A.take([128, 6, NT], BF16)
        glw = A.take([128, 2, 6, 512], BF16)
        wgs = A.take([128, 8, 512], BF16)
        ytmp = A.take([128, 2, 3, NB], F32)
        sbt = A.take([128, 2, 2, 512], BF16)
        t1 = A.take([128, 2, 512], F32)
        ycnt = 0
        for o in range(6):
            nv = CH_ROWS[o]
            npairs = nv // 32
            yv = yT[:nv, o, :].rearrange("p (k j) -> p j k", j=8)
            for j in range(8):
                bank, off = 4 + j // 3, (j % 3) * NB
                reg = ps[bank][:, off:off + NB]
                for i3 in range(npairs):
                    p = 3 * o + i3
                    rsl = slice(i3 * 32, (i3 + 1) * 32)
                    for ri, tab in enumerate((CAr_b, nCAi_b)):
                        P.pe(lambda e, reg=reg, rsl=rsl, tab=tab, j=j, p=p, ri=ri: e.matmul(
                            reg[rsl, :], lhsT=tab[:, j + 1, p, :], rhs=Hb[:, ri, p, :], start=(ri == 0),
                            stop=False),
                            reads=["CAb", "Hb"], writes=[("ps", bank)])
                for i in range(j + 1):
                    P.pe(lambda e, reg=reg, nv=nv, o=o, i=i, j=j: e.matmul(
                        reg[:nv, :], lhsT=KBD[:nv, o, j - i, :nv], rhs=uT[:nv, o, i, :], start=False,
                        stop=(i == j)),
                        reads=["KBD", ("uT", o)], writes=[("ps", bank)])
            for bq, (j0, j1) in enumerate(((0, 3), (3, 6), (6, 8))):
                ys_ = ycnt % 2
                ycnt += 1
                L = j1 - j0
                pv = ps[4 + bq][:nv, 0:L * NB].rearrange("p (j k) -> p j k", k=NB)
                P.dve(lambda e, j0=j0, j1=j1, nv=nv, o=o, pv=pv, ys_=ys_, L=L: e.scalar_tensor_tensor(
                    out=ytmp[:nv, ys_, 0:L, :], in0=uT[:nv, o, j0:j1, :], scalar=Dsk[:nv, o:o + 1], in1=pv,
                    op0=ALU.mult, op1=ALU.add),
                    reads=[("uT", o), "Dsk", ("ps", 4 + bq)], writes=[("ytmp", ys_)])
                P.act(lambda e, yv=yv, j0=j0, j1=j1, nv=nv, ys_=ys_, L=L: e.activation(
                    out=yv[:, j0:j1, :], in_=ytmp[:nv, ys_, 0:L, :], func=AF.Gelu_apprx_tanh),
                    reads=[("ytmp", ys_)], writes=[("yT", o)])
        P.phase = 'glu_' + str(hf)
        for ch in range(2):
            csl = slice(ch * 512, (ch + 1) * 512)
            for ab, src in enumerate((glu_a_d, glu_b_d)):
                for o in range(6):
                    nv = CH_ROWS[o]
                    P.dma("pool", lambda e, ab=ab, src=src, csl=csl, o=o, nv=nv: e.dma_start(
                        out=glw[:nv, ab, o, :], in_=src[o * 96:o * 96 + nv, csl]), writes=[("glw", ab, o)])
            for dh in range(2):
                P.dma("pool", lambda e, ch=ch, dh=dh: e.dma_start(
                    out=wgs[:, dh * 4:(dh + 1) * 4, :],
                    in_=w_in[dh * 512:(dh + 1) * 512, 3072 + ch * 512:3072 + (ch + 1) * 512].rearrange(
                        "(c p) f -> p c f", p=128)), writes=[("wgs", dh)])
            for tt, (c0, rows) in enumerate(TILES):
                set_ = tt % 2
                pa, pb_, pg = 3 * set_, 3 * set_ + 1, 3 * set_ + 2
                for ab, bank in ((0, pa), (1, pb_)):
                    for o in range(6):
                        nv = CH_ROWS[o]
                        P.pe(lambda e, ab=ab, bank=bank, o=o, nv=nv, c0=c0, rows=rows: e.matmul(
                            ps[bank][:rows, :], lhsT=yT[:nv, o, c0:c0 + rows], rhs=glw[:nv, ab, o, :],
                            start=(o == 0), stop=(o == 5)),
                            reads=[("yT", o), ("glw", ab, o)], writes=[("ps", bank)])
                for dc in range(8):
                    P.pe(lambda e, dc=dc, c0=c0, rows=rows, pg=pg: e.matmul(
                        ps[pg][:rows, :], lhsT=hT[:, dc, c0:c0 + rows], rhs=wgs[:, dc, :], start=(dc == 0),
                        stop=(dc == 7)),
                        reads=[("hT", tt), ("wgs", dc // 4)], writes=[("ps", pg)])
                P.act(lambda e, rows=rows, set_=set_, pb_=pb_: e.activation(out=sbt[:rows, set_, 0, :],
                                                                         in_=ps[pb_][:rows, :], func=AF.Sigmoid),
                      reads=[("ps", pb_)], writes=[("sbt", set_, 0)])
                P.act(lambda e, rows=rows, set_=set_, pg=pg: e.activation(out=sbt[:rows, set_, 1, :],
                                                                       in_=ps[pg][:rows, :], func=AF.Sigmoid),
                      reads=[("ps", pg)], writes=[("sbt", set_, 1)])
                P.dve(lambda e, rows=rows, set_=set_, pa=pa: e.tensor_tensor(
                    out=t1[:rows, set_, :], in0=ps[pa][:rows, :], in1=sbt[:rows, set_, 0, :], op=ALU.mult),
                    reads=[("ps", pa), ("sbt", set_, 0)], writes=[("t1", set_)])
                P.dve(lambda e, rows=rows, set_=set_: e.tensor_tensor(
                    out=t1[:rows, set_, :], in0=t1[:rows, set_, :], in1=sbt[:rows, set_, 1, :], op=ALU.mult),
                    reads=[("t1", set_), ("sbt", set_, 1)], writes=[("t1", set_)])
                P.dve(lambda e, rows=rows, set_=set_, tt=tt, csl=csl: e.tensor_tensor(
                    out=mg[:rows, tt, csl], in0=mg[:rows, tt, csl], in1=t1[:rows, set_, :], op=ALU.add),
                    reads=[("t1", set_)] + [("mg", tt, j) for j in range(4)], writes=[("mg", tt, j) for j in range(4)])
        P.barrier()
        P.phase = 'wout_' + str(hf)
        A.off = m5
        wo = A.take([128, 2, 8, 512], BF16)
        ptv = ps[7][:, 0:512].bitcast(BF16).rearrange("p (c t) -> p c t", c=8)
        for hh in range(2):
            P.dma("pool", lambda e, hh=hh: e.dma_start(
                out=wo[:, hh, :, :], in_=w_out_d[:, hh * 512:(hh + 1) * 512].rearrange("(c p) f -> p c f", p=128)),
                writes=[("wo", hh)])
        def woA(tt):
            c0, rows = TILES[tt]
            pv_ = ps[6 + tt % 2][:, 0:512].bitcast(BF16).rearrange("p (c t) -> p c t", c=8)
            for dc in range(8):
                P.pe(lambda e, dc=dc, rows=rows, tt=tt, pv_=pv_: e.transpose(
                    out=pv_[:, dc, :rows], in_=mg[:rows, tt, dc * 128:(dc + 1) * 128], identity=ident_b[:rows, :rows]),
                    reads=[("mg", tt, j) for j in range(4)] + ["ident_b"], writes=[("ps", 6 + tt % 2)])
            P.act(lambda e, c0=c0, rows=rows, pv_=pv_: e.copy(out=hT[:, :, c0:c0 + rows], in_=pv_[:, :, :rows]),
                  reads=[("ps", 6 + tt % 2)], writes=[("hT", tt)])

        def woB(tt):
            c0, rows = TILES[tt]
            for hh in range(2):
                pb = (2 * tt + hh) % 4
                for dc in range(8):
                    P.pe(lambda e, dc=dc, c0=c0, rows=rows, hh=hh, pb=pb: e.matmul(
                        ps[pb][:rows, :], lhsT=hT[:, dc, c0:c0 + rows], rhs=wo[:, hh, dc, :], start=(dc == 0),
                        stop=(dc == 7)),
                        reads=[("hT", tt), ("wo", hh)], writes=[("ps", pb)])
                P.dve(lambda e, tt=tt, rows=rows, hh=hh, pb=pb: e.scalar_tensor_tensor(
                    out=x_tm[:rows, tt, hh * 512:(hh + 1) * 512], in0=ps[pb][:rows, :], scalar=1.0,
                    in1=x_tm[:rows, tt, hh * 512:(hh + 1) * 512], op0=ALU.mult, op1=ALU.add),
                    reads=[("ps", pb), ("x", tt)], writes=[("x", tt)])

        fT = ffn_takes()
        ne = NormEpilogue(2, fT[0], fT[4], fT[5])
        for tt in range(len(TILES) + 1):
            if tt < len(TILES):
                woA(tt)
            if tt >= 1:
                woB(tt - 1)
                ne(tt - 1)
        ne.flush()

    P.barrier()
    for hf in range(2):
        P.hf = hf
        if hf == 1:
            load_x(1)
        if STOP <= 0:
            break
        mt = mix_takes()
        ffn("a", 0, barrier=(hf == 0),
            after_tile=NormEpilogue(1, mt[0], mt[3], mt[5]))
        if STOP <= 1:
            break
        hT, mg, mark = mix(hf)
        if STOP <= 2:
            break
        uT, Hb, m5 = mix2(hf, hT, mg, mark)
        if STOP <= 3:
            break
        mix3(hf, hT, mg, uT, Hb, m5)
        if STOP <= 4:
            break
        def fin_tile(tt, yout, junk, hf=hf):
            c0, rows = TILES[tt]
            sl = rmsnorm_stats(tt, rows, junk)
            ys_ = tt % 2
            P.dve(lambda e: e.scalar_tensor_tensor(
                out=yout[:rows, ys_, :], in0=x_tm[:rows, tt, :], scalar=stat[:rows, sl, 3:4],
                in1=gfin[:rows, :], op0=ALU.mult, op1=ALU.mult),
                reads=[("x", tt), ("stat", sl), "gfin"], writes=[("yout", ys_)])
            dst = yp[hf * 1024 + c0: hf * 1024 + c0 + rows, :] if tt < 8 else ys[hf * 64:(hf + 1) * 64, :]
            P.dma("sp", lambda e: e.dma_start(out=dst, in_=yout[:rows, ys_, :]),
                  reads=[("yout", ys_)], writes=[("yd", hf, tt)])

        ffn("b", 2, after_tile=fin_tile, prenormed=True)

    with nc.Block() as block:
        P.emit(block, sems, dsems)
    nc._pe_phase = P.pe_phase
    return nc


def _masks():
    NEG = -30000.0
    k = np.arange(128)[:, None]
    q = np.arange(128)[None, :]
    cur = np.where(k <= q, 0.0, NEG).astype(np.float32)
    prev = np.where(k >= q, 0.0, NEG).astype(np.float32)
    mcur = np.concatenate([cur, cur], 1)
    mprev = np.concatenate([prev, prev], 1)
    k6 = np.arange(64)[:, None]
    q6 = np.arange(64)[None, :]
    new = np.where((k6 // 8 == q6 // 8) & (k6 % 8 <= q6 % 8), 0.0, NEG).astype(np.float32)
    mnew = np.concatenate([new, new], 1)
    mc = np.zeros((128, 8, 128), np.float32)
    for s in range(8):
        one = np.where((q6 // 8 == s) & (k >= q6 % 8), 0.0, NEG).astype(np.float32)
        mc[:, s, :] = np.concatenate([one, one], 1)
    return mcur, mprev, mnew, mc


_NC = None


def kernel(**inputs):
    global _NC
    if _NC is None:
        _NC = build_nc()
    nc = _NC
    f = lambda a: np.ascontiguousarray(np.asarray(a, dtype=np.float32))
    x_prompt = f(inputs["x_prompt"])
    x_sample = f(inputs["x_sample"])
    cache_k = f(inputs["cache_k_win"])[0]
    cache_v = f(inputs["cache_v_win"])[0]
    st_re = f(inputs["state_ssm_re"])[0]
    st_im = f(inputs["state_ssm_im"])[0]
    mcur, mprev, mnew, mc = _masks()
    m01 = np.stack([(mprev == 0).astype(np.float32), (mcur == 0).astype(np.float32)], 1)
    shared = {"ident": np.eye(128, dtype=np.float32), "mcur": mcur, "mprev": mprev, "mnew": mnew, "mc": mc,
              "m01": np.ascontiguousarray(m01),
              "mc2": np.where(np.arange(128)[:, None] >= (np.arange(64)[None, :] % 8), 0.0, -30000.0).astype(np.float32)}
    for k in ("ffn_a_norm", "mix_norm", "ffn_b_norm", "final_norm"):
        shared[k] = f(inputs[k]).reshape(D)
    for k in ("ffn_a_gate", "ffn_a_up", "ffn_b_gate", "ffn_b_up"):
        shared[k] = f(inputs[k]).reshape(D, DFF)
    for k in ("ffn_a_down", "ffn_b_down"):
        shared[k] = f(inputs[k]).reshape(DFF, D)
    shared["w_in"] = f(inputs["w_in"]).reshape(D, 4096)
    shared["attn_sinks"] = f(inputs["attn_sinks"]).reshape(16)
    shared["ssm_lambda_re"] = f(inputs["ssm_lambda_re"]).reshape(32, 64)
    shared["ssm_lambda_im"] = f(inputs["ssm_lambda_im"]).reshape(32, 64)
    shared["ssm_log_dt"] = f(inputs["ssm_log_dt"]).reshape(32)
    shared["ssm_b_re"] = f(inputs["ssm_b_re"]).reshape(32, 64, 16)
    shared["ssm_b_im"] = f(inputs["ssm_b_im"]).reshape(32, 64, 16)
    shared["ssm_c_re"] = f(inputs["ssm_c_re"]).reshape(32, 16, 64)
    shared["ssm_c_im"] = f(inputs["ssm_c_im"]).reshape(32, 16, 64)
    shared["ssm_d"] = f(inputs["ssm_d"]).reshape(512)
    shared["glu_a"] = f(inputs["glu_a"]).reshape(512, D)
    shared["glu_b"] = f(inputs["glu_b"]).reshape(512, D)
    shared["w_out"] = f(inputs["w_out"]).reshape(D, D)
    in_maps = []
    for c in range(8):
        m = dict(shared)
        sl = slice(16 * c, 16 * (c + 1))
        m["xp"] = x_prompt[c]
        m["xs"] = x_sample[sl].reshape(128, D)
        m["ck"] = cache_k[sl].reshape(16, 128, 256)
        m["cv"] = cache_v[sl].reshape(16, 128, 256)
        m["h0r"] = st_re[sl].reshape(16, 2048)
        m["h0i"] = st_im[sl].reshape(16, 2048)
        in_maps.append(m)
    res = run_bass_kernel_spmd(nc, in_maps, core_ids=list(range(8)))
    r = res.results
    global LAST
    LAST = r
    cat = lambda k, shp: np.concatenate([r[c][k].reshape(shp) for c in range(8)], 0)[None]
    y_prompt = np.stack([r[c]["yp"] for c in range(8)], 0)
    y_sample = np.concatenate([r[c]["ys"].reshape(16, 8, D) for c in range(8)], 0)
    return (y_prompt, y_sample,
            cat("kwp", (1, 128, 4, 64)), cat("vwp", (1, 128, 4, 64)),
            cat("kws", (16, 128, 4, 64)), cat("vws", (16, 128, 4, 64)),
            cat("srp", (1, 32, 64)), cat("sip", (1, 32, 64)),
            cat("srs", (16, 32, 64)), cat("sis", (16, 32, 64)))
```

```python
import numpy as np
import concourse.bass as bass
import concourse.mybir as mybir
from concourse.bass_utils import run_bass_kernel_spmd

F32 = mybir.dt.float32
BF16 = mybir.dt.bfloat16
AF = mybir.ActivationFunctionType
ALU = mybir.AluOpType

ENGS = ("pe", "act", "dve", "pool", "sp")
N_DMA_SEMS = 96

D = 1024
DFF = 2816
NFC = DFF // 128
NT = 1088
TILES = [(i * 128, 128) for i in range(8)] + [(1024, 64)]
GROUPS = [(0, 512), (512, 512), (1024, 64)]
EPS = 1e-6


class Op:
    __slots__ = ("id", "eng", "fn", "deps", "dma", "sig", "sem", "val")

    def __init__(self, id, eng, fn, dma):
        self.id = id
        self.eng = eng
        self.fn = fn
        self.deps = set()
        self.dma = dma
        self.sig = False
        self.sem = None
        self.val = None


class Prog:
    def __init__(self, nc):
        self.nc = nc
        self.ops = []
        self.q = {e: [] for e in ENGS}
        self.lastw = {}
        self.readers = {}
        self.dma_last = [None] * N_DMA_SEMS
        self.dma_cnt = [0] * N_DMA_SEMS
        self.dma_rr = 0
        self.dma_rr2 = {True: 0, False: 0}
        self.bar = []
        self.phase = 'init'
        self.pe_phase = []

    def _add(self, eng, fn, reads, writes, dma=False):
        op = Op(len(self.ops), eng, fn, dma)
        op.deps.update(self.bar)
        isps = lambda k: isinstance(k, tuple) and k[0] == "ps"
        writes = list(writes) + [k for k in reads if isps(k)]
        reads = [k for k in reads if not isps(k)]
        for k in reads:
            w = self.lastw.get(k)
            if w is not None:
                op.deps.add(w)
        for k in writes:
            w = self.lastw.get(k)
            if w is not None:
                op.deps.add(w)
            for r in self.readers.get(k, ()):
                op.deps.add(r)
        for k in reads:
            self.readers.setdefault(k, []).append(op.id)
        for k in writes:
            self.lastw[k] = op.id
            self.readers[k] = []
        if dma:
            half = N_DMA_SEMS // 2
            base = 0 if eng == "pool" else half
            r = self.dma_rr2[eng == "pool"]
            self.dma_rr2[eng == "pool"] = (r + 1) % half
            s = base + r
            if self.dma_last[s] is not None:
                op.deps.add(self.dma_last[s])
            self.dma_last[s] = op.id
            self.dma_cnt[s] += 1
            op.sem = ("dma", s)
            op.val = 16 * self.dma_cnt[s]
            op.sig = True
        self.ops.append(op)
        self.q[eng].append(op)
        return op

    def barrier(self):
        b = []
        for e in ENGS:
            if self.q[e]:
                b.append(self.q[e][-1].id)
        for d in self.dma_last:
            if d is not None:
                b.append(d)
        self.bar = b

    def pe(self, fn, reads=(), writes=()):
        self.pe_phase.append(self.phase)
        return self._add("pe", fn, reads, writes)

    def act(self, fn, reads=(), writes=()):
        return self._add("act", fn, reads, writes)

    def dve(self, fn, reads=(), writes=()):
        return self._add("dve", fn, reads, writes)

    def pool(self, fn, reads=(), writes=()):
        return self._add("pool", fn, reads, writes)

    def dma(self, eng, fn, reads=(), writes=()):
        return self._add(eng, fn, reads, writes, dma=True)

    def emit(self, block, sems, dma_sems):
        ops = self.ops

        import os
        nosame = int(os.environ.get("KNOSAME", "0"))

        def skip(p, op):
            if p.dma or op.dma or p.eng != op.eng:
                return False
            return p.eng == "pe" or nosame

        for op in ops:
            for d in op.deps:
                p = ops[d]
                if p.dma or skip(p, op):
                    continue
                p.sig = True
        cnt = {e: 0 for e in ENGS}
        for e in ENGS:
            for op in self.q[e]:
                if not op.dma and op.sig:
                    cnt[e] += 1
                    op.sem = ("eng", e)
                    op.val = cnt[e]

        def semh(s):
            return sems[s[1]] if s[0] == "eng" else dma_sems[s[1]]

        def run_queue(e, eng):
            waited = {}
            for op in self.q[e]:
                need = {}
                for d in op.deps:
                    p = ops[d]
                    if skip(p, op) or p.sem is None:
                        continue
                    if need.get(p.sem, 0) < p.val:
                        need[p.sem] = p.val
                for s, v in need.items():
                    if waited.get(s, 0) >= v:
                        continue
                    eng.wait_ge(semh(s), v)
                    waited[s] = v
                inst = op.fn(eng)
                if op.sig:
                    inst.then_inc(semh(op.sem), 16 if op.dma else 1)
            for op in self.q[e]:
                if op.dma and waited.get(op.sem, 0) < op.val:
                    eng.wait_ge(semh(op.sem), op.val)
                    waited[op.sem] = op.val

        if self.q["pe"]:
            @block.tensor
            def _(eng):
                run_queue("pe", eng)
        if self.q["act"]:
            @block.scalar
            def _(eng):
                run_queue("act", eng)
        if self.q["dve"]:
            @block.vector
            def _(eng):
                run_queue("dve", eng)
        if self.q["pool"]:
            @block.gpsimd
            def _(eng):
                run_queue("pool", eng)
        if self.q["sp"]:
            @block.sync
            def _(eng):
                run_queue("sp", eng)


NB = 136
CH_ROWS = [96, 96, 96, 96, 96, 32]
PI = float(np.pi)


class Arena:
    def __init__(self, t, n):
        self.t = t
        self.n = n
        self.off = 0

    def take(self, shape, dt):
        elems = 1
        for s in shape[1:]:
            elems *= s
        size = elems * (2 if dt == F32 else 1)
        off = (self.off + 31) // 32 * 32
        assert off + size <= self.n, ("arena overflow", off + size, self.n)
        self.off = off + size
        v = self.t[:, off:off + size]
        if dt == F32:
            v = v.bitcast(F32)
        nd = len(shape) - 1
        if nd > 1:
            names = ["a", "b", "c", "d"][:nd]
            kw = {names[i]: shape[1 + i] for i in range(nd)}
            v = v.rearrange("p (" + " ".join(names) + ") -> p " + " ".join(names), **kw)
        return v


def build_nc(debug=False):
    import os
    STOP = int(os.environ.get('KSTOP', '99'))
    NOTAB = int(os.environ.get('KNOTAB', '0'))
    KSUB = int(os.environ.get('KSUB', '99'))
    nc = bass.Bass("TRN2", target_bir_lowering=False)

    def din(name, shape):
        return nc.dram_tensor(name, list(shape), F32, kind="ExternalInput").ap()

    def dout(name, shape):
        return nc.dram_tensor(name, list(shape), F32, kind="ExternalOutput").ap()

    xp = din("xp", [2048, D])
    xs = din("xs", [128, D])
    ck = din("ck", [16, 128, 256])
    cv = din("cv", [16, 128, 256])
    h0r_d = din("h0r", [16, 2048])
    h0i_d = din("h0i", [16, 2048])
    ident_d = din("ident", [128, 128])
    mcur_d = din("mcur", [128, 256])
    mprev_d = din("mprev", [128, 256])
    mnew_d = din("mnew", [64, 128])
    mc_d = din("mc", [128, 8, 128])
    m01_d = din("m01", [128, 2, 256])
    mc2_d = din("mc2", [128, 64])
    gains = {k: din(k, [D]) for k in ("ffn_a_norm", "mix_norm", "ffn_b_norm", "final_norm")}
    Wg = {"a": din("ffn_a_gate", [D, DFF]), "b": din("ffn_b_gate", [D, DFF])}
    Wu = {"a": din("ffn_a_up", [D, DFF]), "b": din("ffn_b_up", [D, DFF])}
    Wd = {"a": din("ffn_a_down", [DFF, D]), "b": din("ffn_b_down", [DFF, D])}
    w_in = din("w_in", [D, 4096])
    sinks_d = din("attn_sinks", [16])
    lam_re_d = din("ssm_lambda_re", [32, 64])
    lam_im_d = din("ssm_lambda_im", [32, 64])
    log_dt_d = din("ssm_log_dt", [32])
    b_re_d = din("ssm_b_re", [32, 64, 16])
    b_im_d = din("ssm_b_im", [32, 64, 16])
    c_re_d = din("ssm_c_re", [32, 16, 64])
    c_im_d = din("ssm_c_im", [32, 16, 64])
    dsk_d = din("ssm_d", [512])
    glu_a_d = din("glu_a", [512, D])
    glu_b_d = din("glu_b", [512, D])
    w_out_d = din("w_out", [D, D])

    yp = dout("yp", [2048, D])
    ys = dout("ys", [128, D])
    kwp = dout("kwp", [128, 256])
    vwp = dout("vwp", [128, 256])
    kws = dout("kws", [16, 128, 256])
    vws = dout("vws", [16, 128, 256])
    srp = dout("srp", [16, 128])
    sip = dout("sip", [16, 128])
    srs = dout("srs", [16, 2048])
    sis = dout("sis", [16, 2048])

    def sb(name, shape, dt):
        return nc.alloc_sbuf_tensor(name, list(shape), dt)

    x_tm = sb("x_tm", [128, 9, D], F32)
    gfin = sb("gfin", [128, D], F32)
    gcol = sb("gcol", [128, 3, 8], F32)
    ident_f = sb("ident_f", [128, 128], F32)
    ident_b = sb("ident_b", [128, 128], BF16)
    mcur_b = sb("mcur_b", [128, 256], BF16)
    mprev_b = sb("mprev_b", [128, 256], BF16)
    mnew_b = sb("mnew_b", [128, 128], BF16)
    mc_b = sb("mc_b", [128, 8, 128], BF16)
    m01_b = sb("m01_b", [128, 2, 256], BF16)
    mc2_b = sb("mc2_b", [128, 64], BF16)
    esink = sb("esink", [128, 16], F32)
    stat = sb("stat", [128, 4, 4], F32)
    kk_carry = sb("kk_carry", [128, 4, 128], BF16)
    v_carry = sb("v_carry", [128, 4, 68], BF16)
    KBD = sb("KBD", [128, 6, 8, 128], BF16)
    RB = sb("RB", [128, 6, 8, 2, 128], BF16)
    CAr_b = sb("CAr_b", [128, 9, 16, 32], BF16)
    nCAi_b = sb("nCAi_b", [128, 9, 16, 32], BF16)
    A8 = sb("A8", [128, 2, 16], F32)
    A8p = sb("A8p", [128, 9, 2, 16], F32)
    A64 = sb("A64", [128, 4, 2, 16], F32)
    Dsk = sb("Dsk", [128, 6], F32)
    h0T = sb("h0T", [128, 2, 16, 16], F32)
    Hc = sb("Hc", [128, 2, 16], F32)
    rt = sb("rt", [128, 4, 16], F32)
    ps = [nc.alloc_psum_tensor("ps%d" % i, [128, 512], F32) for i in range(8)]
    rem = nc.sbuf_bytes_remaining
    an = (rem - 1024) // 2 // 32 * 32
    arena_t = sb("arena", [128, an], BF16)
    A = Arena(arena_t, an)

    sems = {e: nc.alloc_semaphore("s_" + e) for e in ENGS}
    dsems = [nc.alloc_semaphore("d%d" % i) for i in range(N_DMA_SEMS)]
    P = Prog(nc)
    TT = ALU.mult
    rot = {"n": 0}

    def tt_(e, out, in0, in1, op):
        return e.tensor_tensor(out=out, in0=in0, in1=in1, op=op)

    P.dma("sp", lambda e: e.dma_start(out=ident_f[:], in_=ident_d[:]), writes=["ident_f"])
    P.dma("pool", lambda e: e.dma_start(out=ident_b[:], in_=ident_d[:]), writes=["ident_b"])
    P.dma("pool", lambda e: e.dma_start(out=mcur_b[:], in_=mcur_d[:]), writes=["mcur"])
    P.dma("pool", lambda e: e.dma_start(out=mprev_b[:], in_=mprev_d[:]), writes=["mprev"])
    P.dve(lambda e: e.memset(mnew_b[:], 0.0), writes=["mnew"])
    P.dma("pool", lambda e: e.dma_start(out=mnew_b[:64, :], in_=mnew_d[:]), reads=["mnew"], writes=["mnew"])
    P.dma("pool", lambda e: e.dma_start(out=mc_b[:], in_=mc_d[:]), writes=["mc"])
    P.dma("pool", lambda e: e.dma_start(out=m01_b[:], in_=m01_d[:]), writes=["m01"])
    P.dma("pool", lambda e: e.dma_start(out=mc2_b[:], in_=mc2_d[:]), writes=["mc2"])
    P.dma("sp", lambda e: e.dma_start(out=gfin[:], in_=gains["final_norm"].partition_broadcast(128)), writes=["gfin"])
    for gi, k in enumerate(("ffn_a_norm", "mix_norm", "ffn_b_norm")):
        P.dma("sp", lambda e, gi=gi, k=k: e.dma_start(
            out=gcol[:, gi, :], in_=gains[k].rearrange("(c p) -> p c", p=128), allow_slow_non_contiguous=True),
            writes=["gcol"])
    P.dma("sp", lambda e: e.dma_start(out=esink[:], in_=sinks_d.partition_broadcast(128)), writes=["esink"])
    P.act(lambda e: e.activation(out=esink[:], in_=esink[:], func=AF.Exp), reads=["esink"], writes=["esink"])

    def build_tables():
        P.phase = 'tables'
        A.off = 0
        lam = A.take([128, 2, 16], F32)
        ldt = A.take([128, 16], F32)
        lt = A.take([128, 3, 128], F32)
        ld2 = A.take([128, 2], F32)
        bb = A.take([128, 2, 16, 16], F32)
        cc = A.take([128, 2, 16, 16], F32)
        w = A.take([128, 12, 16], F32)
        w3 = A.take([128, 4, 16, 16], F32)
        BBp = A.take([128, 2, 16, 32], F32)
        PW = A.take([128, 9, 2, 16], F32)
        h0s = A.take([128, 2, 2048], F32)
        ct = A.take([128, 2, 2, 128], F32)
        BBh = A.take([128, 2, 16, 32], BF16)
        BBl = A.take([128, 2, 16, 32], BF16)
        dtmp = A.take([128, 2, 16, 32], F32)
        X = A.take([128, 4, 9, 16, 16], F32)
        CAl = A.take([128, 2, 8, 16, 32], BF16)
        pwt = A.take([128, 4, 4, 16], F32)
        Y = h0s[:, :, :].rearrange("p a b -> p (a b)").rearrange("p (q m g c) -> p q m g c", q=4, m=2, g=16)
        P.pool(lambda e: e.memset(KBD[:], 0.0), [], ["KBD"])
        P.pool(lambda e: e.memset(CAr_b[:], 0.0), [], ["CAb"])
        P.pool(lambda e: e.memset(nCAi_b[:], 0.0), [], ["CAb"])
        P.pool(lambda e: e.memset(CAl[:], 0.0), [], ["CAl"])
        P.dma("sp", lambda e: e.dma_start(out=lt[:16, 0, :], in_=lam_re_d.rearrange("(p t) n -> p (t n)", t=2)),
              writes=["lt0"])
        P.dma("sp", lambda e: e.dma_start(out=lt[:16, 1, :], in_=lam_im_d.rearrange("(p t) n -> p (t n)", t=2)),
              writes=["lt1"])
        P.dma("sp", lambda e: e.dma_start(out=ld2[:16, :], in_=log_dt_d.rearrange("(p t) -> p t", t=2)),
              writes=["ld2"])
        P.dve(lambda e: e.tensor_copy(out=lt[:16, 2, :].rearrange("p (t n) -> p t n", t=2),
                                      in_=ld2[:16, :].unsqueeze(2).to_broadcast([16, 2, 64])),
              reads=["ld2"], writes=["lt2"])
        for k in range(3):
            P.pe(lambda e, k=k: e.transpose(out=ps[4][:, k * 16:(k + 1) * 16], in_=lt[:16, k, :],
                                            identity=ident_f[:16, :16]),
                 reads=["lt%d" % k, "ident_f"], writes=[("ps", 4)])
        P.dve(lambda e: e.tensor_copy(out=lam[:, :, :], in_=ps[4][:, 0:32].rearrange("p (a b) -> p a b", a=2)),
              reads=[("ps", 4)], writes=["lam"])
        P.dve(lambda e: e.tensor_copy(out=ldt[:, :], in_=ps[4][:, 32:48]), reads=[("ps", 4)], writes=["ldt"])
        for g2 in range(2):
            psl = slice(g2 * 64, (g2 + 1) * 64)
            for ri, src in enumerate((b_re_d, b_im_d)):
                P.dma("sp", lambda e, psl=psl, ri=ri, src=src, g2=g2: e.dma_start(
                    out=bb[psl, ri, :, :], in_=src.rearrange("(p t) n c -> t n p c", t=2)[g2]), writes=["bb"])
        for ri, src in enumerate((c_re_d, c_im_d)):
            for t in range(2):
                for pl in range(8):
                    p = t * 8 + pl
                    P.dma("sp" if ri == 0 else "pool", lambda e, ri=ri, t=t, pl=pl, p=p, src=src: e.dma_start(
                        out=ct[pl * 16:(pl + 1) * 16, ri, t, :].rearrange("c (t2 n) -> c t2 n", t2=2),
                        in_=src[2 * p:2 * p + 2].rearrange("t2 c n -> c t2 n")), writes=[("ct", ri, t)])
                pb = 2 + (2 * ri + t) % 2
                P.pe(lambda e, ri=ri, t=t, pb=pb: e.transpose(out=ps[pb][:, 0:128], in_=ct[:, ri, t, :],
                                                            identity=ident_f[:, :]),
                     reads=[("ct", ri, t), "ident_f"], writes=[("ps", pb)])
                P.act(lambda e, ri=ri, t=t, pb=pb: e.copy(
                    out=cc[:, ri, t * 8:(t + 1) * 8, :], in_=ps[pb][:, 0:128].rearrange("p (a c) -> p a c", c=16)),
                    reads=[("ps", pb)], writes=[("cc", ri, t)])
        P.dve(lambda e: e.memset(Dsk[:], 0.0), writes=["Dsk"])
        P.dma("sp", lambda e: e.dma_start(out=Dsk[:96, 0:5], in_=dsk_d[0:480].rearrange("(o r) -> r o", r=96),
                                          allow_slow_non_contiguous=True), reads=["Dsk"], writes=["Dsk"])
        P.dma("sp", lambda e: e.dma_start(out=Dsk[:32, 5:6], in_=dsk_d[480:512].rearrange("(o r) -> r o", r=32),
                                          allow_slow_non_contiguous=True), reads=["Dsk"], writes=["Dsk"])
        P.dma("sp", lambda e: e.dma_start(out=h0s[:16, 0, :], in_=h0r_d[:]), writes=["h0s"])
        P.dma("sp", lambda e: e.dma_start(out=h0s[:16, 1, :], in_=h0i_d[:]), writes=["h0s"])
        for ri in range(2):
            pv = ps[ri][:, :].rearrange("p (a s) -> p a s", s=32)
            for p in range(16):
                P.pe(lambda e, ri=ri, p=p, pv=pv: e.transpose(out=pv[:, p, 0:16], in_=h0s[:16, ri, p * 128:(p + 1) * 128],
                                                           identity=ident_f[:16, :16]),
                     reads=["h0s", "ident_f"], writes=[("ps", ri)])
            P.dve(lambda e, ri=ri, pv=pv: e.tensor_copy(out=h0T[:, ri, :, :], in_=pv[:, :, 0:16]),
                  reads=[("ps", ri)], writes=["h0T"])

        def V(fn, reads, writes):
            P.dve(fn, reads=reads, writes=writes)

        W = lambda i: w[:, i, :]
        lr, li = lam[:, 0, :], lam[:, 1, :]
        P.act(lambda e: e.activation(out=W(0), in_=ldt[:], func=AF.Exp), reads=["ldt"], writes=["w"])
        V(lambda e: tt_(e, W(1), lr, W(0), TT), ["lam", "w"], ["w"])
        P.act(lambda e: e.activation(out=W(2), in_=W(1), func=AF.Exp), reads=["w"], writes=["w"])
        V(lambda e: tt_(e, W(3), li, W(0), TT), ["lam", "w"], ["w"])
        for _ in range(4):
            V(lambda e: e.tensor_single_scalar(out=W(4), in_=W(3), scalar=PI, op=ALU.is_gt), ["w"], ["w"])
            V(lambda e: e.scalar_tensor_tensor(out=W(3), in0=W(4), scalar=-2.0 * PI, in1=W(3), op0=ALU.mult,
                                               op1=ALU.add), ["w"], ["w"])
        V(lambda e: e.tensor_single_scalar(out=W(4), in_=W(3), scalar=-PI, op=ALU.is_lt), ["w"], ["w"])
        V(lambda e: e.scalar_tensor_tensor(out=W(3), in0=W(4), scalar=2.0 * PI, in1=W(3), op0=ALU.mult,
                                           op1=ALU.add), ["w"], ["w"])
        V(lambda e: e.tensor_scalar(out=W(5), in0=W(3), scalar1=PI / 2, scalar2=None, op0=ALU.add), ["w"], ["w"])
        V(lambda e: e.tensor_single_scalar(out=W(4), in_=W(5), scalar=PI, op=ALU.is_gt), ["w"], ["w"])
        V(lambda e: e.scalar_tensor_tensor(out=W(5), in0=W(4), scalar=-2.0 * PI, in1=W(5), op0=ALU.mult,
                                           op1=ALU.add), ["w"], ["w"])
        P.act(lambda e: e.activation(out=W(6), in_=W(3), func=AF.Sin), reads=["w"], writes=["w"])
        P.act(lambda e: e.activation(out=W(7), in_=W(5), func=AF.Sin), reads=["w"], writes=["w"])
        abr, abi = PW[:, 1, 0, :], PW[:, 1, 1, :]
        V(lambda e: tt_(e, abr, W(2), W(7), TT), ["w"], ["PW"])
        V(lambda e: tt_(e, abi, W(2), W(6), TT), ["w"], ["PW"])
        V(lambda e: e.memset(PW[:, 0, 0, :], 1.0), [], ["PW"])
        V(lambda e: e.memset(PW[:, 0, 1, :], 0.0), [], ["PW"])
        V(lambda e: tt_(e, W(0), lr, lr, TT), ["lam"], ["w"])
        V(lambda e: tt_(e, W(1), li, li, TT), ["lam"], ["w"])
        V(lambda e: tt_(e, W(0), W(0), W(1), ALU.add), ["w"], ["w"])
        V(lambda e: e.reciprocal(out=W(0), in_=W(0)), ["w"], ["w"])
        V(lambda e: e.tensor_scalar(out=W(1), in0=abr, scalar1=-1.0, scalar2=None, op0=ALU.add), ["PW"], ["w"])
        V(lambda e: tt_(e, W(2), W(1), lr, TT), ["w", "lam"], ["w"])
        V(lambda e: tt_(e, W(3), abi, li, TT), ["PW", "lam"], ["w"])
        V(lambda e: tt_(e, W(2), W(2), W(3), ALU.add), ["w"], ["w"])
        V(lambda e: tt_(e, W(8), W(2), W(0), TT), ["w"], ["w"])
        V(lambda e: tt_(e, W(2), abi, lr, TT), ["PW", "lam"], ["w"])
        V(lambda e: tt_(e, W(3), W(1), li, TT), ["w", "lam"], ["w"])
        V(lambda e: tt_(e, W(2), W(2), W(3), ALU.subtract), ["w"], ["w"])
        V(lambda e: tt_(e, W(9), W(2), W(0), TT), ["w"], ["w"])
        frb = W(8).unsqueeze(2).to_broadcast([128, 16, 16])
        fib = W(9).unsqueeze(2).to_broadcast([128, 16, 16])
        V(lambda e: tt_(e, w3[:, 0], bb[:, 0], frb, TT), ["bb", "w"], ["w3"])
        V(lambda e: tt_(e, w3[:, 1], bb[:, 1], fib, TT), ["bb", "w"], ["w3"])
        V(lambda e: tt_(e, w3[:, 2], bb[:, 1], frb, TT), ["bb", "w"], ["w3"])
        V(lambda e: tt_(e, w3[:, 3], bb[:, 0], fib, TT), ["bb", "w"], ["w3"])
        V(lambda e: tt_(e, bb[:, 0], w3[:, 0], w3[:, 1], ALU.subtract), ["w3"], ["bb"])
        V(lambda e: tt_(e, bb[:, 1], w3[:, 2], w3[:, 3], ALU.add), ["w3"], ["bb"])
        P.pool(lambda e: e.memset(BBp[:], 0.0), [], ["BBp"])
        for ri in range(2):
            for g2 in range(2):
                psl = slice(g2 * 64, (g2 + 1) * 64)
                V(lambda e, ri=ri, g2=g2, psl=psl: e.tensor_copy(out=BBp[psl, ri, :, g2 * 16:(g2 + 1) * 16],
                                                               in_=bb[psl, ri, :, :]), ["bb", "BBp"], ["BBp"])
        V(lambda e: e.tensor_copy(out=BBh[:], in_=BBp[:]), ["BBp"], ["BBh"])
        V(lambda e: tt_(e, dtmp[:], BBp[:], BBh[:], ALU.subtract), ["BBp", "BBh"], ["dtmp"])
        V(lambda e: e.tensor_copy(out=BBl[:], in_=dtmp[:]), ["dtmp"], ["BBl"])
        def cmul(o_r, o_i, a_r, a_i, b_r, b_i, shp, rk, wk):
            n_ = shp[1] if len(shp) == 3 else 1
            t = [pwt[:, i, 0:n_, :] if len(shp) == 3 else pwt[:, i, 0, :] for i in range(4)]
            V(lambda e: tt_(e, t[0], a_r, b_r, TT), rk, ["pwt"])
            V(lambda e: tt_(e, t[1], a_i, b_i, TT), rk, ["pwt"])
            V(lambda e: tt_(e, t[2], a_r, b_i, TT), rk, ["pwt"])
            V(lambda e: tt_(e, t[3], a_i, b_r, TT), rk, ["pwt"])
            V(lambda e: tt_(e, o_r, t[0], t[1], ALU.subtract), ["pwt"], wk)
            V(lambda e: tt_(e, o_i, t[2], t[3], ALU.add), ["pwt"], wk)

        def powers(T, key):
            cmul(T[:, 2, 0, :], T[:, 2, 1, :], T[:, 1, 0, :], T[:, 1, 1, :], T[:, 1, 0, :], T[:, 1, 1, :],
                 [128, 16], [key], [key])
            for (lo, n_, b) in ((1, 2, 2), (1, 4, 4)):
                bc_ = lambda ap: ap.unsqueeze(1).to_broadcast([128, n_, 16])
                cmul(T[:, b + 1:b + 1 + n_, 0, :], T[:, b + 1:b + 1 + n_, 1, :],
                     T[:, lo:lo + n_, 0, :], T[:, lo:lo + n_, 1, :], bc_(T[:, b, 0, :]), bc_(T[:, b, 1, :]),
                     [128, n_, 16], [key], [key])

        powers(PW, "PW")
        V(lambda e: e.tensor_copy(out=A8[:], in_=PW[:, 8, :, :]), ["PW"], ["A8"])
        V(lambda e: e.tensor_copy(out=A8p[:, 1, :, :], in_=PW[:, 8, :, :]), ["PW"], ["A8p"])
        powers(A8p, "A8p")
        V(lambda e: e.tensor_copy(out=A64[:, 0, :, :], in_=A8p[:, 8, :, :]), ["A8p"], ["A64"])
        for k in range(3):
            cmul(A64[:, k + 1, 0, :], A64[:, k + 1, 1, :], A64[:, k, 0, :], A64[:, k, 1, :], A64[:, k, 0, :],
                 A64[:, k, 1, :], [128, 16], ["A64"], ["A64"])
        def rb_quarter(qd, G, Y, yk, ykr):
            m0 = 2 * qd
            for mm in range(2):
                pr_ = PW[:, m0 + mm, 0, :].unsqueeze(2).to_broadcast([128, 16, 32])
                pi_ = PW[:, m0 + mm, 1, :].unsqueeze(2).to_broadcast([128, 16, 32])
                G(lambda e, mm=mm, pr_=pr_: tt_(e, Y[:, 0, mm], BBp[:, 0], pr_, TT), ["BBp", "PW"], yk)
                G(lambda e, mm=mm, pi_=pi_: tt_(e, Y[:, 1, mm], BBp[:, 1], pi_, TT), ["BBp", "PW"], yk)
                G(lambda e, mm=mm: tt_(e, Y[:, 0, mm], Y[:, 0, mm], Y[:, 1, mm], ALU.subtract), [], yk)
                G(lambda e, mm=mm, pr_=pr_: tt_(e, Y[:, 2, mm], BBp[:, 1], pr_, TT), ["BBp", "PW"], yk)
                G(lambda e, mm=mm, pi_=pi_: tt_(e, Y[:, 3, mm], BBp[:, 0], pi_, TT), ["BBp", "PW"], yk)
                G(lambda e, mm=mm: tt_(e, Y[:, 2, mm], Y[:, 2, mm], Y[:, 3, mm], ALU.add), [], yk)
            for mm in range(2):
                i = 7 - (m0 + mm)
                for ri in range(2):
                    sel = (2 * (2 * qd + mm) + ri) % 2
                    for o in range(6):
                        nv = CH_ROWS[o]
                        bank = 4 + 2 * sel + (o // 4)
                        out = ps[bank][:nv, (o % 4) * 128:(o % 4 + 1) * 128]
                        src_ = Y[:, 2 * ri, mm, 3 * o:3 * o + nv // 32, :].rearrange("p a c -> p (a c)")
                        P.pe(lambda e, out=out, src_=src_: e.transpose(out=out, in_=src_, identity=ident_f[:]),
                             reads=[ykr, "ident_f"], writes=[("ps", bank)])
                    b0 = 4 + 2 * sel
                    P.act(lambda e, i=i, ri=ri, b0=b0: e.copy(
                        out=RB[:96, 0:4, i, ri, :], in_=ps[b0][:96, :].rearrange("p (o c) -> p o c", c=128)),
                        [("ps", b0)], [("RB", i, ri)])
                    P.act(lambda e, i=i, ri=ri, b0=b0: e.copy(
                        out=RB[:96, 4, i, ri, :], in_=ps[b0 + 1][:96, 0:128]), [("ps", b0 + 1)], [("RB4", i, ri)])
                    P.act(lambda e, i=i, ri=ri, b0=b0: e.copy(
                        out=RB[:32, 5, i, ri, :], in_=ps[b0 + 1][:32, 128:256]), [("ps", b0 + 1)], [("RB5", i, ri)])
        Gp = lambda fn, r, w_: P.pool(fn, reads=r, writes=w_)
        Gd = lambda fn, r, w_: P.dve(fn, reads=r, writes=w_)
        for qd in (0, 1):
            rb_quarter(qd, Gp, Y, ["Y", "h0s"], "Y")
        ccb = lambda ri: cc[:, ri].unsqueeze(1).to_broadcast([128, 9, 16, 16])
        pwb = lambda ri: PW[:, :, ri, :].unsqueeze(3).to_broadcast([128, 9, 16, 16])
        cck = [("cc", r_, t_) for r_ in range(2) for t_ in range(2)] + ["PW"]
        V(lambda e: tt_(e, X[:, 0], ccb(0), pwb(0), TT), cck, ["X0"])
        V(lambda e: tt_(e, X[:, 1], ccb(1), pwb(1), TT), cck, ["X1"])
        V(lambda e: tt_(e, X[:, 2], ccb(0), pwb(1), TT), cck, ["X2"])
        V(lambda e: tt_(e, X[:, 3], ccb(1), pwb(0), TT), cck, ["X3"])
        V(lambda e: tt_(e, X[:, 0], X[:, 0], X[:, 1], ALU.subtract), ["X0", "X1"], ["X0"])
        V(lambda e: tt_(e, X[:, 2], X[:, 2], X[:, 3], ALU.add), ["X2", "X3"], ["X2"])
        V(lambda e: e.tensor_scalar(out=X[:, 2], in0=X[:, 2], scalar1=-1.0, scalar2=None, op0=ALU.mult),
          ["X2"], ["X2"])
        for q, (sx, tab) in enumerate(((0, CAr_b), (2, nCAi_b))):
            for g2 in range(2):
                psl = slice(g2 * 64, (g2 + 1) * 64)
                gc = slice(g2 * 16, (g2 + 1) * 16)
                V(lambda e, sx=sx, tab=tab, psl=psl, gc=gc: e.tensor_copy(out=tab[psl, :, :, gc], in_=X[psl, sx]),
                  ["X%d" % sx, "CAb"], ["CAb"])
                V(lambda e, sx=sx, tab=tab, psl=psl, gc=gc: tt_(e, X[psl, sx + 1, 0:8], X[psl, sx, 0:8],
                                                                 tab[psl, 0:8, :, gc], ALU.subtract),
                  ["X%d" % sx, "CAb"], ["X%d" % (sx + 1)])
                V(lambda e, sx=sx, q=q, psl=psl, gc=gc: e.tensor_copy(out=CAl[psl, q, :, :, gc],
                                                                    in_=X[psl, sx + 1, 0:8]),
                  ["X%d" % (sx + 1), "CAl"], ["CAl"])
        for m in range(8 if not int(os.environ.get('KT_NOKBD', '0')) else 0):
            bk = m // 2
            for p in range(16):
                o, i3 = p // 3, p % 3
                c0_ = (m % 2) * 192 + o * 32
                out = ps[bk][i3 * 32:(i3 + 1) * 32, c0_:c0_ + 32]
                combos = []
                for q, tab in ((0, CAr_b), (1, nCAi_b)):
                    combos += [(BBh[:, q, p, :], tab[:, m, p, :]), (BBh[:, q, p, :], CAl[:, q, m, p, :]),
                               (BBl[:, q, p, :], tab[:, m, p, :])]
                for ci, (l_, r_) in enumerate(combos):
                    P.pe(lambda e, out=out, l_=l_, r_=r_, ci=ci: e.matmul(out, lhsT=l_, rhs=r_, start=(ci == 0),
                                                                      stop=(ci == 5)),
                         reads=["BBh", "BBl", "CAb", "CAl"], writes=[("ps", bk)])
        for bk in range(4 if not int(os.environ.get('KT_NOEV', '0')) else 0):
            for i3 in range(3):
                no = 6 if i3 == 0 else 5
                rsl = slice(i3 * 32, (i3 + 1) * 32)
                P.act(lambda e, bk=bk, i3=i3, no=no, rsl=rsl: e.copy(
                    out=KBD[rsl, 0:no, 2 * bk:2 * bk + 2, i3 * 32:(i3 + 1) * 32].rearrange("p o m c -> p m o c"),
                    in_=ps[bk][rsl, 0:384].rearrange("p (m o c) -> p m o c", m=2, o=6)[:, :, 0:no, :]),
                    [("ps", bk), "KBD"], ["KBD"])

        Y2 = X[:, 0:2].rearrange("p a b c d -> p (a b c d)")[:, 0:4096].rearrange("p (q m g c) -> p q m g c", q=4, m=2, g=16)
        for qd in (2, 3):
            rb_quarter(qd, Gd, Y2, ["X0", "X1", "Y2"], "Y2")

    def load_x(hf):
        for tt, (c0, rows) in enumerate(TILES):
            src = xp[hf * 1024 + c0: hf * 1024 + c0 + rows, :] if tt < 8 else xs[hf * 64:(hf + 1) * 64, :]
            P.dma("sp", lambda e, tt=tt, rows=rows, src=src: e.dma_start(out=x_tm[:rows, tt, :], in_=src),
                  writes=[("x", tt)])

    load_x(0)
    if not NOTAB:
        build_tables()
    P.dma("sp", lambda e: e.dma_start(out=kws[:, 0:120, :], in_=ck[:, 8:128, :]), writes=["kws_c"])
    P.dma("sp", lambda e: e.dma_start(out=vws[:, 0:120, :], in_=cv[:, 8:128, :]), writes=["vws_c"])
    DBG = {}
    if int(os.environ.get('KDEBUG', '0')):
        P.barrier()
        for nm, t, shp in (("KBD", KBD, [128, 6 * 8 * 128]), ("RB", RB, [128, 6 * 8 * 2 * 128]),
                           ("CAr", CAr_b, [128, 9 * 16 * 32]), ("nCAi", nCAi_b, [128, 9 * 16 * 32]),
                           ("A8", A8, [128, 32]), ("Dsk", Dsk, [128, 6]), ("h0T", h0T, [128, 512])):
            dd = dout("dbg_" + nm, shp)
            DBG[nm] = dd
            flat = t[:]
            nd = len(t.shape)
            if nd == 3:
                flat = t[:].rearrange("p a b -> p (a b)")
            elif nd == 4:
                flat = t[:].rearrange("p a b c -> p (a b c)")
            elif nd == 5:
                flat = t[:].rearrange("p a b c d -> p (a b c d)")
            P.dma("pool", lambda e, dd=dd, flat=flat: e.dma_start(out=dd[:, :], in_=flat),
                  reads=["KBD", "RB", "CAb", "A8", "Dsk", "h0T"], writes=["dbg" + nm])

    cnt = {"n": 0, "pb": 0}

    def rmsnorm_stats(tt, rows, junk, stages=None):
        sl = cnt["n"] % 4
        cnt["n"] += 1
        s1 = lambda: P.act(lambda e: e.activation(out=junk[:rows, :], in_=x_tm[:rows, tt, :], func=AF.Square,
                                                  accum_out=stat[:rows, sl, 0:1]),
                           reads=[("x", tt)], writes=["junk", ("stat", sl)])
        s2 = lambda: P.dve(lambda e: e.tensor_scalar(out=stat[:rows, sl, 1:2], in0=stat[:rows, sl, 0:1],
                                                     scalar1=1.0 / D, scalar2=EPS, op0=ALU.mult, op1=ALU.add),
                           reads=[("stat", sl)], writes=[("stat", sl)])
        s3 = lambda: P.act(lambda e: e.activation(out=stat[:rows, sl, 2:3], in_=stat[:rows, sl, 1:2], func=AF.Sqrt),
                           reads=[("stat", sl)], writes=[("stat", sl)])
        s4 = lambda: P.dve(lambda e: e.reciprocal(out=stat[:rows, sl, 3:4], in_=stat[:rows, sl, 2:3]),
                           reads=[("stat", sl)], writes=[("stat", sl)])
        if stages is None:
            s1(); s2(); s3(); s4()
        else:
            stages.extend([s1, s2, s3, s4])
        return sl

    def norm_to_hT(gi, hT, hn, junk):
        per_tile = []
        for tt, (c0, rows) in enumerate(TILES):
            st_ = []
            sl = rmsnorm_stats(tt, rows, junk, stages=st_)
            hs = tt % 2
            pbk = 6 + tt % 2
            ptv = ps[pbk][:, 0:512].bitcast(BF16).rearrange("p (c t) -> p c t", c=8)

            def s5(tt=tt, rows=rows, sl=sl, hs=hs):
                if tt % 2:
                    P.act(lambda e: e.activation(out=hn[:rows, hs, :], in_=x_tm[:rows, tt, :], func=AF.Copy,
                                                 scale=stat[:rows, sl, 3:4]),
                          reads=[("x", tt), ("stat", sl)], writes=[("hn", hs)])
                else:
                    P.dve(lambda e: e.tensor_scalar(out=hn[:rows, hs, :], in0=x_tm[:rows, tt, :],
                                                    scalar1=stat[:rows, sl, 3:4], scalar2=None, op0=ALU.mult),
                          reads=[("x", tt), ("stat", sl)], writes=[("hn", hs)])

            def s6(tt=tt, c0=c0, rows=rows, hs=hs, pbk=pbk, ptv=ptv):
                for dc in range(8):
                    P.pe(lambda e, dc=dc: e.transpose(out=ptv[:, dc, :rows], in_=hn[:rows, hs, dc * 128:(dc + 1) * 128],
                                                      identity=ident_b[:rows, :rows]),
                         reads=[("hn", hs), "ident_b"], writes=[("ps", pbk)])
                P.dve(lambda e: e.tensor_tensor(out=hT[:, :, c0:c0 + rows], in0=ptv[:, :, :rows],
                                                in1=gcol[:, gi, :].unsqueeze(2).to_broadcast([128, 8, rows]),
                                                op=ALU.mult),
                      reads=[("ps", pbk), "gcol"], writes=[("hT", tt)])

            per_tile.append([st_[0], st_[1], st_[2], st_[3], s5, s6])
        nst = 6
        for step in range(len(TILES) + nst - 1):
            for k in range(nst):
                t = step - k
                if 0 <= t < len(TILES):
                    per_tile[t][k]()

    def grp(c):
        return 0 if c < 512 else (512 if c < 1024 else 1024)

    def tiles_in(c0, n):
        return [t for t, (a, r) in enumerate(TILES) if c0 <= a < c0 + n]

    def ffn(which, gi, barrier=True, after_tile=None):
        if barrier:
            P.barrier()
        P.phase = 'ffn_' + which + str(P.hf)
        A.off = 0
        hT = A.take([128, 8, NT], BF16)
        hid = A.take([128, 11, NT], BF16)
        wd_sb = A.take([128, 11, D], BF16)
        wgu = A.take([128, 3, 2, 8, 128], BF16)
        hn = A.take([128, 2, D], BF16)
        junk = A.take([128, D], BF16)
        sg = A.take([128, 2, 512], BF16)
        if after_tile is not None:
            A.off = 45056
            ep_bufs = (A.take([128, 2, D], F32), A.take([128, D], BF16))
        norm_to_hT(gi, hT, hn, junk)
        wg, wu, wdn = Wg[which], Wu[which], Wd[which]
        k = 0
        for fh in range(2):
            for j in range(11):
                fc = fh * 11 + j
                s = fc % 3
                for m, w_ in enumerate((wg, wu)):
                    P.dma("pool", lambda e, fc=fc, s=s, m=m, w_=w_: e.dma_start(
                        out=wgu[:, s, m, :, :],
                        in_=w_[:, fc * 128:(fc + 1) * 128].rearrange("(c p) f -> p c f", p=128)),
                        writes=[("wgu", s, m)])
                P.dma("pool", lambda e, fc=fc, j=j: e.dma_start(
                    out=wd_sb[:, j, :], in_=wdn[fc * 128:(fc + 1) * 128, :]), writes=[("wd", j)])
                for (c0, n) in GROUPS:
                    tts = tiles_in(c0, n)
                    pb = (k % 3) * 2
                    ss = k % 2
                    k += 1
                    for m in range(2):
                        for dc in range(8):
                            P.pe(lambda e, m=m, dc=dc, s=s, c0=c0, n=n, pb=pb: e.matmul(
                                ps[pb + m][:, :n], lhsT=wgu[:, s, m, dc, :], rhs=hT[:, dc, c0:c0 + n],
                                start=(dc == 0), stop=(dc == 7)),
                                reads=[("wgu", s, m)] + [("hT", t) for t in tts], writes=[("ps", pb + m)])
                    P.act(lambda e, pb=pb, n=n, ss=ss: e.activation(out=sg[:, ss, :n], in_=ps[pb][:, :n],
                                                                  func=AF.Silu),
                          reads=[("ps", pb)], writes=[("sg", ss)])
                    P.dve(lambda e, pb=pb, n=n, ss=ss, j=j, c0=c0: e.tensor_tensor(
                        out=hid[:, j, c0:c0 + n], in0=sg[:, ss, :n], in1=ps[pb + 1][:, :n], op=ALU.mult),
                        reads=[("sg", ss), ("ps", pb + 1)], writes=[("hid", j, c0)])
            for tt, (c0, rows) in enumerate(TILES):
                g0 = [g for (g, n) in GROUPS if g <= c0 < g + n][0]
                for hh in range(2):
                    pb = 6 + hh
                    for j in range(11):
                        P.pe(lambda e, j=j, c0=c0, rows=rows, hh=hh, pb=pb: e.matmul(
                            ps[pb][:rows, :], lhsT=hid[:, j, c0:c0 + rows],
                            rhs=wd_sb[:, j, hh * 512:(hh + 1) * 512], start=(j == 0), stop=(j == 10)),
                            reads=[("hid", j, g0), ("wd", j)], writes=[("ps", pb)])
                    P.dve(lambda e, tt=tt, rows=rows, hh=hh, pb=pb: e.scalar_tensor_tensor(
                        out=x_tm[:rows, tt, hh * 512:(hh + 1) * 512], in0=ps[pb][:rows, :], scalar=0.5,
                        in1=x_tm[:rows, tt, hh * 512:(hh + 1) * 512], op0=ALU.mult, op1=ALU.add),
                        reads=[("ps", pb), ("x", tt)], writes=[("x", tt)])
                if fh == 1 and after_tile is not None:
                    after_tile(tt, *ep_bufs)
        return junk

    def pbank(lo, hi):
        cnt["pb"] += 1
        return lo + cnt["pb"] % (hi - lo)

    def wslice(col0, ncols):
        return w_in[:, col0:col0 + ncols].rearrange("(c p) f -> p c f", p=128)

    def mix(hf):
        P.barrier()
        P.phase = 'mix1_' + str(hf)
        A.off = 0
        hT = A.take([128, 8, NT], BF16)
        mg = A.take([128, 9, D], BF16)
        mark = A.off
        hn = A.take([128, 2, D], BF16)
        etr = A.take([128, 2, 2, 256], BF16)
        junk = etr[:, 0:2, :, :].rearrange("p a b c -> p (a b c)")
        qT = A.take([128, 8, NT], BF16)
        kkT = A.take([128, 4, NT], BF16)
        vaug = A.take([128, 9, 4, 68], BF16)
        kvout = A.take([128, 2, 512], F32)
        wf = A.take([128, 3, 8, 128], BF16)
        wt = A.take([128, 2, 8, 256], BF16)
        oT_sb = wt[:, :, :, :].rearrange("p a b c -> p (a b c)")[:, 0:2048].bitcast(F32)
        et = A.take([128, 4, 2, 256], BF16)
        etc_ = A.take([128, 2, 128], BF16)
        etn = A.take([128, 1, 128], BF16)
        den = A.take([128, 2, 2, 4], F32)
        atmp = A.take([128, 2, 4, 64], F32)
        ckd = A.take([128, 2, 4, 2, 64], BF16)
        kcT = A.take([128, 2, 4, 128], BF16)
        vcaug = A.take([128, 3, 4, 68], BF16)

        norm_to_hT(1, hT, hn, junk)
        P.dve(lambda e: e.memset(vaug[:, :, :, 64:68], 1.0), writes=["vaug1"])
        P.dve(lambda e: e.memset(vcaug[:, :, :, 64:68], 1.0), writes=["vcaug1"])

        ftiles = [("q", i) for i in range(8)] + [("kk", g) for g in range(4)]
        for ti, (kind, idx) in enumerate(ftiles):
            s = ti % 3
            if kind == "q":
                P.dma("pool", lambda e, s=s, idx=idx: e.dma_start(out=wf[:, s, :, :], in_=wslice(idx * 128, 128)),
                      writes=[("wf", s)])
                dest = qT
            else:
                for h2 in range(2):
                    P.dma("pool", lambda e, s=s, idx=idx, h2=h2: e.dma_start(
                        out=wf[:, s, :, h2 * 64:(h2 + 1) * 64], in_=wslice(1024 + idx * 64, 64)),
                        writes=[("wf", s)])
                dest = kkT
            for (c0, n) in GROUPS:
                pb = pbank(0, 4)
                for dc in range(8):
                    P.pe(lambda e, s=s, dc=dc, c0=c0, n=n, pb=pb: e.matmul(
                        ps[pb][:, :n], lhsT=wf[:, s, dc, :], rhs=hT[:, dc, c0:c0 + n], start=(dc == 0),
                        stop=(dc == 7)),
                        reads=[("wf", s)] + [("hT", t) for t in tiles_in(c0, n)], writes=[("ps", pb)])
                if cnt["pb"] % 2:
                    P.act(lambda e, dest=dest, idx=idx, c0=c0, n=n, pb=pb: e.copy(out=dest[:, idx, c0:c0 + n],
                                                                                in_=ps[pb][:, :n]),
                          reads=[("ps", pb)], writes=[(kind, idx, c0)])
                else:
                    P.dve(lambda e, dest=dest, idx=idx, c0=c0, n=n, pb=pb: e.tensor_copy(out=dest[:, idx, c0:c0 + n],
                                                                                       in_=ps[pb][:, :n]),
                          reads=[("ps", pb)], writes=[(kind, idx, c0)])

        if KSUB <= 1:
            return hT, mg, mark
        blocks = [("k", 1024), ("v", 1280)] + [("ga", 2048 + 256 * j) for j in range(4)]
        KBLK = os.environ.get('KBLK', 'k,v,ga').split(',')
        blocks = [b for b in blocks if b[0] in KBLK]
        for bi, (kind, col0) in enumerate(blocks):
            s = bi % 2
            P.dma("pool", lambda e, s=s, col0=col0: e.dma_start(out=wt[:, s, :, :], in_=wslice(col0, 256)),
                  writes=[("wt", s)])
            for tt, (c0, rows) in enumerate(TILES):
                is_out = (tt == 8) or (tt == 7 and hf == 1)
                osl = 0 if tt == 8 else 1
                if kind == "k" and not is_out:
                    continue
                pb = pbank(4, 8)
                for dc in range(8):
                    P.pe(lambda e, s=s, dc=dc, c0=c0, rows=rows, pb=pb: e.matmul(
                        ps[pb][:rows, :256], lhsT=hT[:, dc, c0:c0 + rows], rhs=wt[:, s, dc, :], start=(dc == 0),
                        stop=(dc == 7)),
                        reads=[("wt", s), ("hT", tt)], writes=[("ps", pb)])
                if kind == "k":
                    P.act(lambda e, rows=rows, osl=osl, pb=pb: e.copy(out=kvout[:rows, osl, 0:256],
                                                                     in_=ps[pb][:rows, :256]),
                          reads=[("ps", pb)], writes=[("kvout", osl, 0)])
                elif kind == "v":
                    P.act(lambda e, rows=rows, tt=tt, pb=pb: e.copy(
                        out=vaug[:rows, tt, :, 0:64], in_=ps[pb][:rows, :256].rearrange("p (g d) -> p g d", g=4)),
                        reads=[("ps", pb)], writes=[("vaug", tt)])
                    if is_out:
                        P.dve(lambda e, rows=rows, osl=osl, pb=pb: e.tensor_copy(out=kvout[:rows, osl, 256:512],
                                                                                in_=ps[pb][:rows, :256]),
                              reads=[("ps", pb)], writes=[("kvout", osl, 1)])
                else:
                    j = (col0 - 2048) // 256
                    P.act(lambda e, rows=rows, tt=tt, j=j, pb=pb: e.activation(
                        out=mg[:rows, tt, j * 256:(j + 1) * 256], in_=ps[pb][:rows, :256], func=AF.Sigmoid),
                        reads=[("ps", pb)], writes=[("mg", tt, j)])
        if KSUB <= 2:
            return hT, mg, mark
        if hf == 1:
            P.dma("sp", lambda e: e.dma_start(out=kwp[:, :], in_=kvout[:, 1, 0:256]), reads=[("kvout", 1, 0)],
                  writes=["kwp"])
            P.dma("sp", lambda e: e.dma_start(out=vwp[:, :], in_=kvout[:, 1, 256:512]), reads=[("kvout", 1, 1)],
                  writes=["vwp"])
        for s_ in range(8):
            sl_ = hf * 8 + s_
            P.dma("sp", lambda e, s_=s_, sl_=sl_: e.dma_start(out=kws[sl_, 120:128, :],
                                                            in_=kvout[s_ * 8:(s_ + 1) * 8, 0, 0:256]),
                  reads=[("kvout", 0, 0)], writes=[("kws", sl_)])
            P.dma("sp", lambda e, s_=s_, sl_=sl_: e.dma_start(out=vws[sl_, 120:128, :],
                                                            in_=kvout[s_ * 8:(s_ + 1) * 8, 0, 256:512]),
                  reads=[("kvout", 0, 1)], writes=[("vws", sl_)])

        if KSUB <= 3:
            return hT, mg, mark
        P.phase = 'attn_' + str(hf)
        def normalize(g, tt, rows, aob, dsl):
            ao = ps[aob][:rows, :].rearrange("p (r c) -> p r c", c=128)
            P.dve(lambda e: e.tensor_tensor(out=den[:rows, dsl, 0, :], in0=ao[:, :, 64],
                                            in1=esink[:rows, 4 * g:4 * g + 4], op=ALU.add),
                  reads=[("ps", aob), "esink"], writes=[("den", dsl)])
            P.dve(lambda e: e.reciprocal(out=den[:rows, dsl, 1, :], in_=den[:rows, dsl, 0, :]),
                  reads=[("den", dsl)], writes=[("den", dsl)])
            P.dve(lambda e: e.tensor_tensor(
                out=atmp[:rows, dsl, :, :], in0=ao[:, :, 0:64],
                in1=den[:rows, dsl, 1, :].unsqueeze(2).to_broadcast([rows, 4, 64]), op=ALU.mult),
                reads=[("ps", aob), ("den", dsl)], writes=[("atmp", dsl)])
            mgv = mg[:rows, tt, g * 256:(g + 1) * 256]
            P.dve(lambda e: e.tensor_tensor(out=mgv, in0=atmp[:rows, dsl, :, :].rearrange("p r c -> p (r c)"),
                                            in1=mgv, op=ALU.mult),
                  reads=[("atmp", dsl), ("mg", tt, g)], writes=[("mg", tt, g)])

        DEPTH = 2
        its = [(tt, g, hh) for tt in range(8) for g in range(4) for hh in range(2)]
        info = {}

        def stageA2(n0):
            todo = []
            for n in (n0, n0 + 1):
                tt, g, hh = its[n]
                c0 = tt * 128
                hsl = slice(hh * 64, (hh + 1) * 64)
                stb = n % 4
                st = ps[stb][:, :].rearrange("p (k c) -> p k c", k=2)
                es = n % 4
                kbs = []
                if tt > 0:
                    kbs.append((0, kkT[hsl, g, c0 - 128:c0], vaug[:, tt - 1, g, 0:65], [("kk", g, grp(c0 - 128)), ("vaug", tt - 1)]))
                elif hf == 1:
                    kbs.append((0, kk_carry[hsl, g, :], v_carry[:, g, 0:65], ["kk_carry", "v_carry"]))
                kbs.append((1, kkT[hsl, g, c0:c0 + 128], vaug[:, tt, g, 0:65], [("kk", g, grp(c0)), ("vaug", tt)]))
                info[n] = (kbs, es)
                todo.append((n, tt, g, hh, c0, hsl, stb, st, es, kbs))
            for ki in range(len(todo[0][9])):
                for (n, tt, g, hh, c0, hsl, stb, st, es, kbs) in todo:
                    (kb, kap, vap, keys) = kbs[ki]
                    qg = grp(c0)
                    P.pe(lambda e, kb=kb, kap=kap, hsl=hsl, g=g, c0=c0, st=st: e.matmul(
                        st[:, kb, :], lhsT=kap, rhs=qT[hsl, 2 * g:2 * g + 2, c0:c0 + 128], start=True, stop=True),
                        reads=[keys[0], ("q", 2 * g, qg), ("q", 2 * g + 1, qg)], writes=[("ps", stb)])
            for (n, tt, g, hh, c0, hsl, stb, st, es, kbs) in todo:
                k0 = kbs[0][0]
                er = n % 2
                P.act(lambda e, k0=k0, st=st, er=er: e.activation(out=etr[:, er, k0:2, :], in_=st[:, k0:2, :],
                                                                  func=AF.Exp, scale=0.125),
                      reads=[("ps", stb)], writes=[("etr", er)])
                (P.pool if hh == 0 else P.dve)(
                    lambda e, k0=k0, es=es, er=er: e.tensor_tensor(out=et[:, es, k0:2, :], in0=etr[:, er, k0:2, :],
                                                                   in1=m01_b[:, k0:2, :], op=ALU.mult),
                    reads=[("etr", er), "m01"], writes=[("et", es)])

        def stageB(n):
            tt, g, hh = its[n]
            kbs, es = info[n]
            aob = 4 + (n // 2) % 2
            dsl = (n // 2) % 2
            ao = ps[aob][:, :].rearrange("p (r c) -> p r c", c=128)
            for j in range(2):
                r = 2 * j + hh
                for n_, (kb, kap, vap, keys) in enumerate(kbs):
                    P.pe(lambda e, r=r, kb=kb, j=j, vap=vap, es=es, ao=ao, n_=n_, L=len(kbs): e.matmul(
                        ao[:, r, 0:65], lhsT=et[:, es, kb, j * 128:(j + 1) * 128], rhs=vap,
                        start=(n_ == 0), stop=(n_ == L - 1)),
                        reads=[("et", es), keys[1], "vaug1"], writes=[("ps", aob)])
            if hh == 1:
                normalize(g, tt, 128, aob, dsl)

        for n in range(0, len(its) + DEPTH, 2):
            if n < len(its):
                stageA2(n)
            for m_ in (n - DEPTH, n - DEPTH + 1):
                if 0 <= m_ < len(its):
                    stageB(m_)
        if KSUB <= 4:
            return hT, mg, mark
        P.phase = 'attS_' + str(hf)
        qc = slice(1024, 1088)
        oTb = [ps[0], ps[1]]
        first_in_bank = [True, True]
        ecnt = 0
        for g in range(4):
            for hh in range(2):
                hsl = slice(hh * 64, (hh + 1) * 64)
                es = 0
                ecnt += 1
                P.pe(lambda e, hsl=hsl, g=g: e.matmul(
                    ps[6][:64, 0:128], lhsT=kkT[hsl, g, qc], rhs=qT[hsl, 2 * g:2 * g + 2, qc], start=True,
                    stop=False),
                    reads=[("kk", g, 1024), ("q", 2 * g, 1024), ("q", 2 * g + 1, 1024)], writes=[("ps", 6)])
                P.pe(lambda e: e.matmul(ps[6][:64, 0:128], lhsT=ident_b[:, :64], rhs=mnew_b[:, :],
                                        start=False, stop=True),
                     reads=["ident_b", "mnew"], writes=[("ps", 6)])
                P.act(lambda e, es=es: e.activation(out=etn[:64, es, :], in_=ps[6][:64, 0:128], func=AF.Exp,
                                                    scale=0.125),
                      reads=[("ps", 6)], writes=[("etn", es)])
                for j in range(2):
                    h = 4 * g + 2 * j + hh
                    bk = h // 8
                    P.pe(lambda e, h=h, bk=bk, j=j, es=es, g=g, st_=first_in_bank[bk]: e.matmul(
                        oTb[bk][:65, (h % 8) * 64:(h % 8 + 1) * 64], lhsT=vaug[:64, 8, g, 0:65],
                        rhs=etn[:64, es, j * 64:(j + 1) * 64], start=st_, stop=False, skip_group_check=True),
                        reads=[("etn", es), ("vaug", 8), "vaug1"], writes=[("ps", bk)])
                    first_in_bank[bk] = False
        ptk = ps[7][:, 0:256].bitcast(BF16).rearrange("p (g t) -> p g t", g=4)
        def sA(s_):
            sl_ = hf * 8 + s_
            cs = s_ % 2
            for d2 in range(2):
                P.dma("pool", lambda e, cs=cs, d2=d2, sl_=sl_: e.dma_start(
                    out=ckd[:, cs, :, d2, :], in_=ck[sl_].rearrange("p (g d) -> p g d", g=4)),
                    writes=[("ckd", cs)])
            vs = s_ % 3
            P.dma("pool", lambda e, vs=vs, sl_=sl_: e.dma_start(
                out=vcaug[:, vs, :, 0:64], in_=cv[sl_].rearrange("p (g d) -> p g d", g=4)),
                reads=["vcaug1"], writes=[("vcaug", vs)])
            for g in range(4):
                P.pe(lambda e, cs=cs, g=g: e.transpose(
                    out=ptk[:, g, :], in_=ckd[:, cs, g, :, :].rearrange("p a d -> p (a d)"), identity=ident_b[:, :]),
                    reads=[("ckd", cs), "ident_b"], writes=[("ps", 7)])
            P.act(lambda e, cs=cs: e.copy(out=kcT[:, cs, :, :], in_=ptk[:, :, :]), reads=[("ps", 7)],
                  writes=[("kcT", cs)])
        def sA2(s_):
            cs = s_ % 2
            qs = slice(1024 + 8 * s_, 1032 + 8 * s_)
            for hh in range(2):
                hsl = slice(hh * 64, (hh + 1) * 64)
                stb = 2 + hh
                P.pe(lambda e, stb=stb: e.matmul(ps[stb][:, 0:64], lhsT=ident_b[:, :], rhs=mc2_b[:, :], start=True,
                                                 stop=False),
                     reads=["ident_b", "mc2"], writes=[("ps", stb)])
                for g in range(4):
                    P.pe(lambda e, stb=stb, hsl=hsl, g=g, cs=cs, qs=qs: e.matmul(
                        ps[stb][:, g * 16:(g + 1) * 16], lhsT=kcT[hsl, cs, g, :], rhs=qT[hsl, 2 * g:2 * g + 2, qs],
                        start=False, stop=(g == 3)),
                        reads=[("kcT", cs), ("q", 2 * g, 1024), ("q", 2 * g + 1, 1024)], writes=[("ps", stb)])
                P.act(lambda e, stb=stb, cs=cs, hh=hh: e.activation(out=etc_[:, cs, hh * 64:(hh + 1) * 64],
                                                                   in_=ps[stb][:, 0:64], func=AF.Exp, scale=0.125),
                      reads=[("ps", stb)], writes=[("etc", cs, hh)])
        def sB(s_):
            cs = s_ % 2
            for g in range(4):
                for j in range(2):
                    for hh in range(2):
                        h = 4 * g + 2 * j + hh
                        bk = h // 8
                        c_ = hh * 64 + g * 16 + j * 8
                        P.pe(lambda e, h=h, bk=bk, g=g, cs=cs, c_=c_, s_=s_: e.matmul(
                            oTb[bk][:65, (h % 8) * 64 + s_ * 8:(h % 8) * 64 + s_ * 8 + 8],
                            lhsT=vcaug[:, s_ % 3, g, 0:65], rhs=etc_[:, cs, c_:c_ + 8], start=False, stop=(s_ == 7),
                            skip_group_check=True),
                            reads=[("etc", cs, hh), ("vcaug", s_ % 3), "vcaug1"], writes=[("ps", bk)])
        for s_ in range(10):
            if s_ < 8:
                sA(s_)
            if 1 <= s_ < 9:
                sA2(s_ - 1)
            if s_ >= 2:
                sB(s_ - 2)
        for bk in range(2):
            P.act(lambda e, bk=bk: e.copy(out=oT_sb[:65, bk * 512:(bk + 1) * 512], in_=oTb[bk][:65, :]),
                  reads=[("ps", bk)], writes=[("oT_sb", bk)])
        for g in range(4):
            aob = 4 + g % 2
            ao = ps[aob][:64, :].rearrange("p (r c) -> p r c", c=128)
            for r in range(4):
                h = 4 * g + r
                P.pe(lambda e, ao=ao, r=r, h=h: e.transpose(out=ao[:, r, 0:65], in_=oT_sb[:65, h * 64:(h + 1) * 64],
                                                          identity=ident_f[:65, :65]),
                     reads=[("oT_sb", h // 8), "ident_f"], writes=[("ps", aob)])
            normalize(g, 8, 64, aob, g % 2)
        if hf == 0:
            P.dve(lambda e: e.tensor_copy(out=kk_carry[:, :, :], in_=kkT[:, :, 896:1024]),
                  reads=[("kk", g, 512) for g in range(4)], writes=["kk_carry"])
            P.dve(lambda e: e.tensor_copy(out=v_carry[:, :, :], in_=vaug[:, 7, :, :]), reads=[("vaug", 7), "vaug1"],
                  writes=["v_carry"])
        return hT, mg, mark

    def mix2(hf, hT, mg, mark):
        P.barrier()
        P.phase = 'ssmA_' + str(hf)
        A.off = mark
        uT = A.take([128, 6, 8, NB], BF16)
        Hb = A.take([128, 2, 16, NB], BF16)
        m5 = A.off
        Z = A.take([128, 2, 16, NB], F32)
        wf2 = A.take([128, 3, 8, 128], BF16)
        svt = A.take([128, 2, 16, 8], F32)
        sso = A.take([128, 2, 512], F32)
        ssp = A.take([128, 2, 128], F32)
        tw = A.take([128, 4, 16, 16], F32)
        Cb = A.take([128, 2, 16, 16], F32)
        Ea = A.take([128, 2, 16, 16], F32)
        Eb = A.take([128, 2, 16, 16], F32)
        for o in range(6):
            nv = CH_ROWS[o]
            s = o % 3
            P.dma("pool", lambda e, s=s, o=o, nv=nv: e.dma_start(out=wf2[:, s, :, 0:nv],
                                                               in_=wslice(1536 + o * 96, nv)),
                  writes=[("wf2", s)])
            for (c0, n) in GROUPS:
                pb = pbank(0, 4)
                for dc in range(8):
                    P.pe(lambda e, s=s, dc=dc, c0=c0, n=n, pb=pb, nv=nv: e.matmul(
                        ps[pb][:nv, :n], lhsT=wf2[:, s, dc, 0:nv], rhs=hT[:, dc, c0:c0 + n], start=(dc == 0),
                        stop=(dc == 7)),
                        reads=[("wf2", s)] + [("hT", t) for t in tiles_in(c0, n)], writes=[("ps", pb)])
                k0, nk = c0 // 8, n // 8
                if cnt["pb"] % 2:
                    P.act(lambda e, o=o, n=n, pb=pb, nv=nv, k0=k0, nk=nk: e.copy(
                        out=uT[:nv, o, :, k0:k0 + nk].rearrange("p j k -> p k j"),
                        in_=ps[pb][:nv, :n].rearrange("p (k j) -> p k j", j=8)),
                        reads=[("ps", pb)], writes=[("uT", o)])
                else:
                    P.dve(lambda e, o=o, n=n, pb=pb, nv=nv, k0=k0, nk=nk: e.tensor_copy(
                        out=uT[:nv, o, :, k0:k0 + nk].rearrange("p j k -> p k j"),
                        in_=ps[pb][:nv, :n].rearrange("p (k j) -> p k j", j=8)),
                        reads=[("ps", pb)], writes=[("uT", o)])
        for p in range(16):
            o, i3 = p // 3, p % 3
            rsl = slice(i3 * 32, (i3 + 1) * 32)
            for ri in range(2):
                pb = pbank(4, 8)
                for i in range(8):
                    P.pe(lambda e, o=o, rsl=rsl, ri=ri, i=i, pb=pb: e.matmul(
                        ps[pb][:, 0:NB], lhsT=RB[rsl, o, i, ri, :], rhs=uT[rsl, o, i, :], start=(i == 0),
                        stop=(i == 7)),
                        reads=["RB", ("uT", o)], writes=[("ps", pb)])
                P.act(lambda e, p=p, ri=ri, pb=pb: e.copy(out=Z[:, ri, p, :], in_=ps[pb][:, 0:NB]),
                      reads=[("ps", pb)], writes=["Z"])
        if hf == 0:
            P.dve(lambda e: e.memset(Hc[:], 0.0), writes=["Hc"])
        Zv = [Z[:, ri, :, 0:128].rearrange("p a (s j) -> p a s j", j=8) for ri in range(2)]
        bc = lambda ap: ap.unsqueeze(2).to_broadcast([128, 16, 16])

        pA8p = ps[0][:, 0:288].rearrange("p (a b c) -> p a b c", a=9, b=2)
        pA8 = ps[0][:, 288:320].rearrange("p (b c) -> p b c", b=2)
        pA64 = ps[0][:, 320:448].rearrange("p (a b c) -> p a b c", a=4, b=2)
        P.dve(lambda e: e.tensor_copy(out=pA8p[:, 1:9], in_=A8p[:, 1:9, :, :]), reads=["A8p"], writes=[("ps", 0)])
        P.dve(lambda e: e.tensor_copy(out=pA8, in_=A8[:, :, :]), reads=["A8"], writes=[("ps", 0)])
        P.dve(lambda e: e.tensor_copy(out=pA64, in_=A64[:, :, :, :]), reads=["A64"], writes=[("ps", 0)])
        pT = ps[1][:, :].rearrange("p (t a b) -> p t a b", t=2, a=16)
        first = {"x": [("ps", 0), ("ps", 1)]}

        def cmul_acc(dr, di, ar_, ai_, xr, xi, keys_r, keys_w, T_):
            t0, t1, t2, t3 = T_(0), T_(1), T_(2), T_(3)
            fx = first["x"]
            first["x"] = []
            P.dve(lambda e: tt_(e, t0, ar_, xr, TT), keys_r, ["tw0"] + fx)
            P.dve(lambda e: tt_(e, t1, ai_, xi, TT), keys_r, ["tw1"])
            P.dve(lambda e: tt_(e, t2, ar_, xi, TT), keys_r, ["tw2"])
            P.dve(lambda e: tt_(e, t3, ai_, xr, TT), keys_r, ["tw3"])
            P.dve(lambda e: tt_(e, t0, t0, t1, ALU.subtract), ["tw0", "tw1"], ["tw0"])
            P.dve(lambda e: tt_(e, t2, t2, t3, ALU.add), ["tw2", "tw3"], ["tw2"])
            P.dve(lambda e: tt_(e, dr, dr, t0, ALU.add), ["tw0"] + keys_w, keys_w)
            P.dve(lambda e: tt_(e, di, di, t2, ALU.add), ["tw2"] + keys_w, keys_w)

        def Tw(n_):
            return lambda i: (pT[:, i // 2, :, 0:n_] if i % 2 == 0 else tw[:, i, :, 0:n_])

        T_ = Tw(16)
        for j in range(1, 8):
            cmul_acc(Zv[0][:, :, :, j], Zv[1][:, :, :, j], bc(pA8[:, 0, :]), bc(pA8[:, 1, :]),
                     Zv[0][:, :, :, j - 1], Zv[1][:, :, :, j - 1], ["Z"], ["Z"], T_)
        P.dve(lambda e: e.tensor_copy(out=Ea[:, :, :, :], in_=Z[:, :, :, 7:128:8]), reads=["Z"], writes=["Ea"])
        if hf == 1:
            T1 = lambda i: (pT[:, i // 2, :, 0] if i % 2 == 0 else tw[:, i, :, 0])
            cmul_acc(Ea[:, 0, :, 0], Ea[:, 1, :, 0], pA64[:, 0, 0, :], pA64[:, 0, 1, :], Hc[:, 0, :], Hc[:, 1, :],
                     ["Hc", "Ea"], ["Ea"], T1)
        bufs = [(Ea, "Ea"), (Eb, "Eb")]
        for k, d in enumerate((1, 2, 4, 8)):
            (s_, sk), (d_, dk) = bufs[k % 2], bufs[(k + 1) % 2]
            n_ = 16 - d
            P.dve(lambda e, s_=s_, d_=d_: e.tensor_copy(out=d_[:, :, :, :], in_=s_[:, :, :, :]), reads=[sk], writes=[dk])
            bcn = lambda ap, n_=n_: ap.unsqueeze(2).to_broadcast([128, 16, n_])
            cmul_acc(d_[:, 0, :, d:16], d_[:, 1, :, d:16], bcn(pA64[:, k, 0, :]), bcn(pA64[:, k, 1, :]),
                     s_[:, 0, :, 0:n_], s_[:, 1, :, 0:n_], [sk, dk], [dk], Tw(n_))
        P.dve(lambda e: e.tensor_copy(out=Cb[:, :, :, 0], in_=Hc[:, :, :]), reads=["Hc"], writes=["Cb"])
        P.dve(lambda e: e.tensor_copy(out=Cb[:, :, :, 1:16], in_=Ea[:, :, :, 0:15]), reads=["Ea", "Cb"], writes=["Cb"])
        T_ = Tw(16)
        for j in range(8):
            cmul_acc(Zv[0][:, :, :, j], Zv[1][:, :, :, j], bc(pA8p[:, j + 1, 0, :]), bc(pA8p[:, j + 1, 1, :]),
                     Cb[:, 0, :, 0:16], Cb[:, 1, :, 0:16], ["Cb", "Z"], ["Z"], T_)
        P.dve(lambda e: e.tensor_copy(out=Hb[:, :, :, 0], in_=Hc[:, :, :]), reads=["Hc"], writes=["Hb"])
        P.dve(lambda e: e.tensor_copy(out=Hb[:, :, :, 1:128], in_=Z[:, :, :, 0:127]), reads=["Z"], writes=["Hb"])
        P.dve(lambda e: e.tensor_copy(out=Hb[:, :, :, 128:136], in_=h0T[:, :, :, hf * 8:(hf + 1) * 8]),
              reads=["h0T"], writes=["Hb"])
        P.dve(lambda e: e.tensor_copy(out=Hc[:, :, :], in_=Z[:, :, :, 127]), reads=["Z", "Hb"], writes=["Hc"])
        arb = A8[:, 0, :].unsqueeze(2).to_broadcast([128, 16, 8])
        aib = A8[:, 1, :].unsqueeze(2).to_broadcast([128, 16, 8])
        h0r_, h0i_ = h0T[:, 0, :, hf * 8:(hf + 1) * 8], h0T[:, 1, :, hf * 8:(hf + 1) * 8]
        zs_r, zs_i = Z[:, 0, :, 128:136], Z[:, 1, :, 128:136]
        for (dst, x1, x2, op) in ((zs_r, h0r_, h0i_, ALU.subtract), (zs_i, h0i_, h0r_, ALU.add)):
            P.dve(lambda e, x1=x1: tt_(e, svt[:, 0], x1, arb, TT), ["h0T", "A8"], ["svt"])
            P.dve(lambda e, x2=x2: tt_(e, svt[:, 1], x2, aib, TT), ["h0T", "A8"], ["svt"])
            P.dve(lambda e, op=op: tt_(e, svt[:, 0], svt[:, 0], svt[:, 1], op), ["svt"], ["svt"])
            P.dve(lambda e, dst=dst: tt_(e, dst, dst, svt[:, 0], ALU.add), ["svt", "Z", "Hb"], ["Z"])
        for ri, dd in enumerate((srs, sis)):
            for q4 in range(4):
                pb = pbank(4, 7)
                so = q4 % 2
                for pp in range(4):
                    p = q4 * 4 + pp
                    P.pe(lambda e, ri=ri, p=p, pp=pp, pb=pb: e.transpose(
                        out=ps[pb][:8, pp * 128:(pp + 1) * 128], in_=Z[:, ri, p, 128:136], identity=ident_f[:, :]),
                        reads=["Z", "ident_f"], writes=[("ps", pb)])
                P.act(lambda e, pb=pb, so=so: e.copy(out=sso[:8, so, :], in_=ps[pb][:8, :]),
                      reads=[("ps", pb)], writes=[("sso", so)])
                P.dma("sp", lambda e, dd=dd, q4=q4, so=so: e.dma_start(
                    out=dd[hf * 8:(hf + 1) * 8, q4 * 512:(q4 + 1) * 512], in_=sso[:8, so, :]),
                    reads=[("sso", so)], writes=[("srs", ri, q4, hf)])
        if hf == 1:
            for ri, dd in enumerate((srp, sip)):
                pb = pbank(4, 7)
                P.pe(lambda e, ri=ri, pb=pb: e.transpose(out=ps[pb][:16, 0:128], in_=Hc[:, ri, :],
                                                         identity=ident_f[:, :]),
                     reads=["Hc", "ident_f"], writes=[("ps", pb)])
                P.act(lambda e, ri=ri, pb=pb: e.copy(out=ssp[:16, ri, :], in_=ps[pb][:16, 0:128]),
                      reads=[("ps", pb)], writes=[("ssp", ri)])
                P.dma("sp", lambda e, ri=ri, dd=dd: e.dma_start(out=dd[:, :], in_=ssp[:16, ri, :]),
                      reads=[("ssp", ri)], writes=[("srp", ri)])
        return uT, Hb, m5

    def mix3(hf, hT, mg, uT, Hb, m5):
        P.barrier()
        P.phase = 'ssmB_' + str(hf)
        A.off = m5
        yT = A.take([128, 6, NT], BF16)
        glw = A.take([128, 2, 6, 512], BF16)
        wgs = A.take([128, 8, 512], BF16)
        ytmp = A.take([128, 2, 3, NB], F32)
        sbt = A.take([128, 2, 2, 512], BF16)
        t1 = A.take([128, 2, 512], F32)
        ycnt = 0
        for o in range(6):
            nv = CH_ROWS[o]
            npairs = nv // 32
            yv = yT[:nv, o, :].rearrange("p (k j) -> p j k", j=8)
            for j in range(8):
                bank, off = 4 + j // 3, (j % 3) * NB
                reg = ps[bank][:, off:off + NB]
                for i3 in range(npairs):
                    p = 3 * o + i3
                    rsl = slice(i3 * 32, (i3 + 1) * 32)
                    for ri, tab in enumerate((CAr_b, nCAi_b)):
                        P.pe(lambda e, reg=reg, rsl=rsl, tab=tab, j=j, p=p, ri=ri: e.matmul(
                            reg[rsl, :], lhsT=tab[:, j + 1, p, :], rhs=Hb[:, ri, p, :], start=(ri == 0),
                            stop=False),
                            reads=["CAb", "Hb"], writes=[("ps", bank)])
                for i in range(j + 1):
                    P.pe(lambda e, reg=reg, nv=nv, o=o, i=i, j=j: e.matmul(
                        reg[:nv, :], lhsT=KBD[:nv, o, j - i, :nv], rhs=uT[:nv, o, i, :], start=False,
                        stop=(i == j)),
                        reads=["KBD", ("uT", o)], writes=[("ps", bank)])
            for bq, (j0, j1) in enumerate(((0, 3), (3, 6), (6, 8))):
                ys_ = ycnt % 2
                ycnt += 1
                L = j1 - j0
                pv = ps[4 + bq][:nv, 0:L * NB].rearrange("p (j k) -> p j k", k=NB)
                P.dve(lambda e, j0=j0, j1=j1, nv=nv, o=o, pv=pv, ys_=ys_, L=L: e.scalar_tensor_tensor(
                    out=ytmp[:nv, ys_, 0:L, :], in0=uT[:nv, o, j0:j1, :], scalar=Dsk[:nv, o:o + 1], in1=pv,
                    op0=ALU.mult, op1=ALU.add),
                    reads=[("uT", o), "Dsk", ("ps", 4 + bq)], writes=[("ytmp", ys_)])
                P.act(lambda e, yv=yv, j0=j0, j1=j1, nv=nv, ys_=ys_, L=L: e.activation(
                    out=yv[:, j0:j1, :], in_=ytmp[:nv, ys_, 0:L, :], func=AF.Gelu_apprx_tanh),
                    reads=[("ytmp", ys_)], writes=[("yT", o)])
        P.phase = 'glu_' + str(hf)
        for ch in range(2):
            csl = slice(ch * 512, (ch + 1) * 512)
            for ab, src in enumerate((glu_a_d, glu_b_d)):
                for o in range(6):
                    nv = CH_ROWS[o]
                    P.dma("pool", lambda e, ab=ab, src=src, csl=csl, o=o, nv=nv: e.dma_start(
                        out=glw[:nv, ab, o, :], in_=src[o * 96:o * 96 + nv, csl]), writes=[("glw", ab, o)])
            for dh in range(2):
                P.dma("pool", lambda e, ch=ch, dh=dh: e.dma_start(
                    out=wgs[:, dh * 4:(dh + 1) * 4, :],
                    in_=w_in[dh * 512:(dh + 1) * 512, 3072 + ch * 512:3072 + (ch + 1) * 512].rearrange(
                        "(c p) f -> p c f", p=128)), writes=[("wgs", dh)])
            for tt, (c0, rows) in enumerate(TILES):
                set_ = tt % 2
                pa, pb_, pg = 3 * set_, 3 * set_ + 1, 3 * set_ + 2
                for ab, bank in ((0, pa), (1, pb_)):
                    for o in range(6):
                        nv = CH_ROWS[o]
                        P.pe(lambda e, ab=ab, bank=bank, o=o, nv=nv, c0=c0, rows=rows: e.matmul(
                            ps[bank][:rows, :], lhsT=yT[:nv, o, c0:c0 + rows], rhs=glw[:nv, ab, o, :],
                            start=(o == 0), stop=(o == 5)),
                            reads=[("yT", o), ("glw", ab, o)], writes=[("ps", bank)])
                for dc in range(8):
                    P.pe(lambda e, dc=dc, c0=c0, rows=rows, pg=pg: e.matmul(
                        ps[pg][:rows, :], lhsT=hT[:, dc, c0:c0 + rows], rhs=wgs[:, dc, :], start=(dc == 0),
                        stop=(dc == 7)),
                        reads=[("hT", tt), ("wgs", dc // 4)], writes=[("ps", pg)])
                P.act(lambda e, rows=rows, set_=set_, pb_=pb_: e.activation(out=sbt[:rows, set_, 0, :],
                                                                         in_=ps[pb_][:rows, :], func=AF.Sigmoid),
                      reads=[("ps", pb_)], writes=[("sbt", set_, 0)])
                P.act(lambda e, rows=rows, set_=set_, pg=pg: e.activation(out=sbt[:rows, set_, 1, :],
                                                                       in_=ps[pg][:rows, :], func=AF.Sigmoid),
                      reads=[("ps", pg)], writes=[("sbt", set_, 1)])
                P.dve(lambda e, rows=rows, set_=set_, pa=pa: e.tensor_tensor(
                    out=t1[:rows, set_, :], in0=ps[pa][:rows, :], in1=sbt[:rows, set_, 0, :], op=ALU.mult),
                    reads=[("ps", pa), ("sbt", set_, 0)], writes=[("t1", set_)])
                P.dve(lambda e, rows=rows, set_=set_: e.tensor_tensor(
                    out=t1[:rows, set_, :], in0=t1[:rows, set_, :], in1=sbt[:rows, set_, 1, :], op=ALU.mult),
                    reads=[("t1", set_), ("sbt", set_, 1)], writes=[("t1", set_)])
                P.dve(lambda e, rows=rows, set_=set_, tt=tt, csl=csl: e.tensor_tensor(
                    out=mg[:rows, tt, csl], in0=mg[:rows, tt, csl], in1=t1[:rows, set_, :], op=ALU.add),
                    reads=[("t1", set_)] + [("mg", tt, j) for j in range(4)], writes=[("mg", tt, j) for j in range(4)])
        P.barrier()
        P.phase = 'wout_' + str(hf)
        A.off = m5
        wo = A.take([128, 2, 8, 512], BF16)
        ptv = ps[7][:, 0:512].bitcast(BF16).rearrange("p (c t) -> p c t", c=8)
        for hh in range(2):
            P.dma("pool", lambda e, hh=hh: e.dma_start(
                out=wo[:, hh, :, :], in_=w_out_d[:, hh * 512:(hh + 1) * 512].rearrange("(c p) f -> p c f", p=128)),
                writes=[("wo", hh)])
        def woA(tt):
            c0, rows = TILES[tt]
            pv_ = ps[6 + tt % 2][:, 0:512].bitcast(BF16).rearrange("p (c t) -> p c t", c=8)
            for dc in range(8):
                P.pe(lambda e, dc=dc, rows=rows, tt=tt, pv_=pv_: e.transpose(
                    out=pv_[:, dc, :rows], in_=mg[:rows, tt, dc * 128:(dc + 1) * 128], identity=ident_b[:rows, :rows]),
                    reads=[("mg", tt, j) for j in range(4)] + ["ident_b"], writes=[("ps", 6 + tt % 2)])
            P.act(lambda e, c0=c0, rows=rows, pv_=pv_: e.copy(out=hT[:, :, c0:c0 + rows], in_=pv_[:, :, :rows]),
                  reads=[("ps", 6 + tt % 2)], writes=[("hT", tt)])

        def woB(tt):
            c0, rows = TILES[tt]
            for hh in range(2):
                pb = (2 * tt + hh) % 4
                for dc in range(8):
                    P.pe(lambda e, dc=dc, c0=c0, rows=rows, hh=hh, pb=pb: e.matmul(
                        ps[pb][:rows, :], lhsT=hT[:, dc, c0:c0 + rows], rhs=wo[:, hh, dc, :], start=(dc == 0),
                        stop=(dc == 7)),
                        reads=[("hT", tt), ("wo", hh)], writes=[("ps", pb)])
                P.dve(lambda e, tt=tt, rows=rows, hh=hh, pb=pb: e.scalar_tensor_tensor(
                    out=x_tm[:rows, tt, hh * 512:(hh + 1) * 512], in0=ps[pb][:rows, :], scalar=1.0,
                    in1=x_tm[:rows, tt, hh * 512:(hh + 1) * 512], op0=ALU.mult, op1=ALU.add),
                    reads=[("ps", pb), ("x", tt)], writes=[("x", tt)])

        for tt in range(len(TILES) + 1):
            if tt < len(TILES):
                woA(tt)
            if tt >= 1:
                woB(tt - 1)

    P.barrier()
    for hf in range(2):
        P.hf = hf
        if hf == 1:
            load_x(1)
        if STOP <= 0:
            break
        ffn("a", 0, barrier=(hf == 0))
        if STOP <= 1:
            break
        hT, mg, mark = mix(hf)
        if STOP <= 2:
            break
        uT, Hb, m5 = mix2(hf, hT, mg, mark)
        if STOP <= 3:
            break
        mix3(hf, hT, mg, uT, Hb, m5)
        if STOP <= 4:
            break
        def fin_tile(tt, yout, junk, hf=hf):
            c0, rows = TILES[tt]
            sl = rmsnorm_stats(tt, rows, junk)
            ys_ = tt % 2
            P.dve(lambda e: e.scalar_tensor_tensor(
                out=yout[:rows, ys_, :], in0=x_tm[:rows, tt, :], scalar=stat[:rows, sl, 3:4],
                in1=gfin[:rows, :], op0=ALU.mult, op1=ALU.mult),
                reads=[("x", tt), ("stat", sl), "gfin"], writes=[("yout", ys_)])
            dst = yp[hf * 1024 + c0: hf * 1024 + c0 + rows, :] if tt < 8 else ys[hf * 64:(hf + 1) * 64, :]
            P.dma("sp", lambda e: e.dma_start(out=dst, in_=yout[:rows, ys_, :]),
                  reads=[("yout", ys_)], writes=[("yd", hf, tt)])

        ffn("b", 2, after_tile=fin_tile)

    with nc.Block() as block:
        P.emit(block, sems, dsems)
    nc._pe_phase = P.pe_phase
    return nc


def _masks():
    NEG = -30000.0
    k = np.arange(128)[:, None]
    q = np.arange(128)[None, :]
    cur = np.where(k <= q, 0.0, NEG).astype(np.float32)
    prev = np.where(k >= q, 0.0, NEG).astype(np.float32)
    mcur = np.concatenate([cur, cur], 1)
    mprev = np.concatenate([prev, prev], 1)
    k6 = np.arange(64)[:, None]
    q6 = np.arange(64)[None, :]
    new = np.where((k6 // 8 == q6 // 8) & (k6 % 8 <= q6 % 8), 0.0, NEG).astype(np.float32)
    mnew = np.concatenate([new, new], 1)
    mc = np.zeros((128, 8, 128), np.float32)
    for s in range(8):
        one = np.where((q6 // 8 == s) & (k >= q6 % 8), 0.0, NEG).astype(np.float32)
        mc[:, s, :] = np.concatenate([one, one], 1)
    return mcur, mprev, mnew, mc


_NC = None


def kernel(**inputs):
    global _NC
    if _NC is None:
        _NC = build_nc()
    nc = _NC
    f = lambda a: np.ascontiguousarray(np.asarray(a, dtype=np.float32))
    x_prompt = f(inputs["x_prompt"])
    x_sample = f(inputs["x_sample"])
    cache_k = f(inputs["cache_k_win"])[0]
    cache_v = f(inputs["cache_v_win"])[0]
    st_re = f(inputs["state_ssm_re"])[0]
    st_im = f(inputs["state_ssm_im"])[0]
    mcur, mprev, mnew, mc = _masks()
    m01 = np.stack([(mprev == 0).astype(np.float32), (mcur == 0).astype(np.float32)], 1)
    shared = {"ident": np.eye(128, dtype=np.float32), "mcur": mcur, "mprev": mprev, "mnew": mnew, "mc": mc,
              "m01": np.ascontiguousarray(m01),
              "mc2": np.where(np.arange(128)[:, None] >= (np.arange(64)[None, :] % 8), 0.0, -30000.0).astype(np.float32)}
    for k in ("ffn_a_norm", "mix_norm", "ffn_b_norm", "final_norm"):
        shared[k] = f(inputs[k]).reshape(D)
    for k in ("ffn_a_gate", "ffn_a_up", "ffn_b_gate", "ffn_b_up"):
        shared[k] = f(inputs[k]).reshape(D, DFF)
    for k in ("ffn_a_down", "ffn_b_down"):
        shared[k] = f(inputs[k]).reshape(DFF, D)
    shared["w_in"] = f(inputs["w_in"]).reshape(D, 4096)
    shared["attn_sinks"] = f(inputs["attn_sinks"]).reshape(16)
    shared["ssm_lambda_re"] = f(inputs["ssm_lambda_re"]).reshape(32, 64)
    shared["ssm_lambda_im"] = f(inputs["ssm_lambda_im"]).reshape(32, 64)
    shared["ssm_log_dt"] = f(inputs["ssm_log_dt"]).reshape(32)
    shared["ssm_b_re"] = f(inputs["ssm_b_re"]).reshape(32, 64, 16)
    shared["ssm_b_im"] = f(inputs["ssm_b_im"]).reshape(32, 64, 16)
    shared["ssm_c_re"] = f(inputs["ssm_c_re"]).reshape(32, 16, 64)
    shared["ssm_c_im"] = f(inputs["ssm_c_im"]).reshape(32, 16, 64)
    shared["ssm_d"] = f(inputs["ssm_d"]).reshape(512)
    shared["glu_a"] = f(inputs["glu_a"]).reshape(512, D)
    shared["glu_b"] = f(inputs["glu_b"]).reshape(512, D)
    shared["w_out"] = f(inputs["w_out"]).reshape(D, D)
    in_maps = []
    for c in range(8):
        m = dict(shared)
        sl = slice(16 * c, 16 * (c + 1))
        m["xp"] = x_prompt[c]
        m["xs"] = x_sample[sl].reshape(128, D)
        m["ck"] = cache_k[sl].reshape(16, 128, 256)
        m["cv"] = cache_v[sl].reshape(16, 128, 256)
        m["h0r"] = st_re[sl].reshape(16, 2048)
        m["h0i"] = st_im[sl].reshape(16, 2048)
        in_maps.append(m)
    res = run_bass_kernel_spmd(nc, in_maps, core_ids=list(range(8)))
    r = res.results
    global LAST
    LAST = r
    cat = lambda k, shp: np.concatenate([r[c][k].reshape(shp) for c in range(8)], 0)[None]
    y_prompt = np.stack([r[c]["yp"] for c in range(8)], 0)
    y_sample = np.concatenate([r[c]["ys"].reshape(16, 8, D) for c in range(8)], 0)
    return (y_prompt, y_sample,
            cat("kwp", (1, 128, 4, 64)), cat("vwp", (1, 128, 4, 64)),
            cat("kws", (16, 128, 4, 64)), cat("vws", (16, 128, 4, 64)),
            cat("srp", (1, 32, 64)), cat("sip", (1, 32, 64)),
            cat("srs", (16, 32, 64)), cat("sis", (16, 32, 64)))
```
